# Optimizing a Trainium2 kernel written in Bass

```python
import math
import jax, jax.numpy as jnp
from jax import lax
import numpy as np

D_MODEL = 2048
BATCH = 4
SEQ = 2048
DEPTH = 4
DEC_BATCH = 32
DEC_SEQ = 8
PAST_LEN = 16384
PAGE_SIZE = 128

N_A = DEPTH // 2
N_B = DEPTH - N_A
RET_HEADS = 8
RET_DK = D_MODEL // RET_HEADS
RET_DV = 2 * D_MODEL // RET_HEADS
RET_CHUNK = 128
RET_ROPE_BASE = 10000.0
N_Q_HEADS = 32
N_KV_HEADS = 4
HEAD_DIM = D_MODEL // N_Q_HEADS
GQA_GROUPS = N_Q_HEADS // N_KV_HEADS
ROPE_DIM = HEAD_DIM // 4
ROPE_THETA = 500000.0
WINDOW = 128
ATT_BLOCK = 128
D_FF = 2 * D_MODEL
CONV_W = 3
N_MOD = 6
EPS = 1e-6
NEG_INF = -1e30

kernel_name = 'yoco_retention_swa_sink_convffn_step'


def rmsnorm(x, g):
    xf = x.astype(jnp.float32)
    y = xf * lax.rsqrt(jnp.mean(xf * xf, axis=-1, keepdims=True) + EPS)
    return (y * g.astype(jnp.float32)).astype(x.dtype)


def modulate(h, shift, scale):
    return h * (1 + scale[:, None, :]) + shift[:, None, :]


def rope(x, pos, dim, base):
    half = dim // 2
    inv = base ** (-jnp.arange(half, dtype=jnp.float32) * 2.0 / dim)
    ang = pos.astype(jnp.float32)[:, None] * inv[None, :]
    cos = jnp.cos(ang)[:, None, :]
    sin = jnp.sin(ang)[:, None, :]
    xr = x[..., :dim].astype(jnp.float32)
    x1, x2 = xr[..., :half], xr[..., half:]
    rot = jnp.concatenate([x1 * cos - x2 * sin, x2 * cos + x1 * sin], axis=-1).astype(x.dtype)
    return jnp.concatenate([rot, x[..., dim:]], axis=-1)


def retention_log_decay():
    return jnp.log(1.0 - jnp.exp2(-5.0 - jnp.arange(RET_HEADS, dtype=jnp.float32)))


def retention_chunk(q, k, v, s_prev, log_gamma):
    L = q.shape[1]
    idx = jnp.arange(L, dtype=jnp.float32)
    rel = idx[:, None] - idx[None, :]
    decay = jnp.where(rel >= 0, jnp.exp(jnp.maximum(rel, 0.0)[None] * log_gamma[:, None, None]), 0.0)
    scores = jnp.einsum('blhd,bmhd->bhlm', q, k) * decay[None]
    o = jnp.einsum('bhlm,bmhe->blhe', scores, v)
    q_decay = jnp.exp((idx + 1.0)[:, None] * log_gamma[None, :])
    o = o + jnp.einsum('blhd,bhde->blhe', q, s_prev) * q_decay[None, :, :, None]
    k_decay = jnp.exp((L - 1.0 - idx)[:, None] * log_gamma[None, :])
    s_new = (jnp.exp(L * log_gamma)[None, :, None, None] * s_prev
             + jnp.einsum('blhd,blhe->bhde', k * k_decay[None, :, :, None], v))
    return o, s_new


def retention_prompt(q, k, v, log_gamma):
    B, T = q.shape[:2]
    n = T // RET_CHUNK

    def to_chunks(a):
        return a.reshape(B, n, RET_CHUNK, *a.shape[2:]).swapaxes(0, 1)

    def step(s, qkv):
        o, s_new = retention_chunk(qkv[0], qkv[1], qkv[2], s, log_gamma)
        return s_new, o

    s0 = jnp.zeros((B, RET_HEADS, RET_DK, RET_DV), jnp.float32)
    s_fin, o = lax.scan(step, s0, (to_chunks(q), to_chunks(k), to_chunks(v)))
    return o.swapaxes(0, 1).reshape(B, T, RET_HEADS, RET_DV), s_fin


def retention_mixer(h, pos, w_in, gn_g, w_out, log_gamma, s_prev):
    B, T, _ = h.shape
    qk = RET_HEADS * RET_DK
    vd = RET_HEADS * RET_DV
    q, k, v, g = jnp.split(h @ w_in, [qk, 2 * qk, 2 * qk + vd], axis=-1)
    q = rope(q.reshape(B, T, RET_HEADS, RET_DK), pos, RET_DK, RET_ROPE_BASE).astype(jnp.float32)
    k = rope(k.reshape(B, T, RET_HEADS, RET_DK), pos, RET_DK, RET_ROPE_BASE).astype(jnp.float32) * RET_DK ** -0.5
    v = v.reshape(B, T, RET_HEADS, RET_DV).astype(jnp.float32)
    if s_prev is None:
        o, s_new = retention_prompt(q, k, v, log_gamma)
    else:
        o, s_new = retention_chunk(q, k, v, s_prev.astype(jnp.float32), log_gamma)
    mu = jnp.mean(o, axis=-1, keepdims=True)
    var = jnp.mean(jnp.square(o - mu), axis=-1, keepdims=True)
    o = ((o - mu) * lax.rsqrt(var + EPS)).reshape(B, T, vd) * gn_g.astype(jnp.float32)
    y = (jax.nn.silu(g.astype(jnp.float32)) * o).astype(h.dtype) @ w_out
    return y, s_new


def sink_attention(q, k, v, q_pos, k_pos, sinks):
    lead = q.shape[:-3]
    tq = q.shape[-3]
    qg = q.reshape(*lead, tq, N_KV_HEADS, GQA_GROUPS, HEAD_DIM)
    s = jnp.einsum('...qkgd,...skd->...kgqs', qg, k, preferred_element_type=jnp.float32) * HEAD_DIM ** -0.5
    rel = q_pos[..., :, None] - k_pos[..., None, :]
    ok = (rel >= 0) & (rel <= WINDOW) & (k_pos[..., None, :] >= 0)
    s = jnp.where(ok[..., None, None, :, :], s, NEG_INF)
    sink = sinks.astype(jnp.float32).reshape(N_KV_HEADS, GQA_GROUPS, 1, 1)
    m = jnp.maximum(jnp.max(s, axis=-1, keepdims=True), sink)
    p = jnp.exp(s - m)
    p = (p / (jnp.sum(p, axis=-1, keepdims=True) + jnp.exp(sink - m))).astype(v.dtype)
    o = jnp.einsum('...kgqs,...skd->...qkgd', p, v)
    return o.reshape(*lead, tq, N_Q_HEADS * HEAD_DIM)


def swa_prompt(q, k, v, sinks):
    B, T = q.shape[:2]
    nb = T // ATT_BLOCK
    qb = q.reshape(B, nb, ATT_BLOCK, N_Q_HEADS, HEAD_DIM)

    def with_prev(a):
        a = a.reshape(B, nb, ATT_BLOCK, N_KV_HEADS, HEAD_DIM)
        prev = jnp.pad(a, ((0, 0), (1, 0), (0, 0), (0, 0), (0, 0)))[:, :-1]
        return jnp.concatenate([prev, a], axis=2)

    pos = jnp.arange(T, dtype=jnp.int32).reshape(nb, ATT_BLOCK)
    k_pos = jnp.concatenate([pos - ATT_BLOCK, pos], axis=1)
    o = sink_attention(qb, with_prev(k), with_prev(v), pos, k_pos, sinks)
    return o.reshape(B, T, N_Q_HEADS * HEAD_DIM)


def conv_ffn(h, w_up, conv_w, conv_b, w_down, conv_state):
    u = h @ w_up
    B, T, C = u.shape
    past = jnp.zeros((B, CONV_W - 1, C), u.dtype) if conv_state is None else conv_state.astype(u.dtype)
    ext = jnp.concatenate([past, u], axis=1)
    acc = conv_b
    for i in range(CONV_W):
        acc = acc + conv_w[i] * ext[:, i:i + T]
    a, b = jnp.split(acc, 2, axis=-1)
    y = (jax.nn.silu(a) * b) @ w_down
    return y, ext[:, -(CONV_W - 1):]


def setup_inputs(seed: int = 0) -> dict:
    key = jax.random.key(seed)
    ks = iter(jax.random.split(key, 40))

    def nrm(shape, scale):
        return jax.random.normal(next(ks), shape, jnp.float32) * scale

    f2 = 2 * D_FF
    win_buf = min(WINDOW, PAST_LEN)
    d = D_MODEL
    return {
        'x_prompt': nrm((BATCH, SEQ, d), 1.0),
        'x_sample': nrm((DEC_BATCH, DEC_SEQ, d), 1.0),
        'state_ret': nrm((N_A, DEC_BATCH, RET_HEADS, RET_DK, RET_DV), 0.05),
        'cache_win_k': nrm((DEC_BATCH, win_buf, N_KV_HEADS, HEAD_DIM), 1.0),
        'cache_win_v': nrm((DEC_BATCH, win_buf, N_KV_HEADS, HEAD_DIM), 1.0),
        'state_conv': nrm((DEPTH, DEC_BATCH, CONV_W - 1, f2), 1.0),
        'c_prompt': nrm((BATCH, d), 1.0),
        'c_sample': nrm((DEC_BATCH, d), 1.0),
        'w_ada': nrm((DEPTH, d, N_MOD * d), 0.5 * d ** -0.5),
        'b_ada': nrm((DEPTH, N_MOD * d), 0.02),
        'norm_mix': 1.0 + nrm((DEPTH, d), 0.05),
        'norm_ffn': 1.0 + nrm((DEPTH, d), 0.05),
        'ret_w_in': nrm((N_A, d, 2 * RET_HEADS * RET_DK + 2 * RET_HEADS * RET_DV), d ** -0.5),
        'ret_gn': 1.0 + nrm((N_A, RET_HEADS * RET_DV), 0.05),
        'ret_w_out': nrm((N_A, RET_HEADS * RET_DV, d), (RET_HEADS * RET_DV) ** -0.5),
        'kv_norm': 1.0 + nrm((d,), 0.05),
        'kv_w_ada': nrm((d, 2 * d), 0.5 * d ** -0.5),
        'kv_b_ada': nrm((2 * d,), 0.02),
        'w_kv': nrm((d, 2 * N_KV_HEADS * HEAD_DIM), d ** -0.5),
        'att_w_q': nrm((N_B, d, N_Q_HEADS * HEAD_DIM), d ** -0.5),
        'att_sinks': nrm((N_B, N_Q_HEADS), 1.0),
        'att_w_o': nrm((N_B, N_Q_HEADS * HEAD_DIM, d), (N_Q_HEADS * HEAD_DIM) ** -0.5),
        'ffn_w_up': nrm((DEPTH, d, f2), d ** -0.5),
        'ffn_conv_w': nrm((DEPTH, CONV_W, f2), CONV_W ** -0.5),
        'ffn_conv_b': nrm((DEPTH, f2), 0.02),
        'ffn_w_down': nrm((DEPTH, D_FF, d), D_FF ** -0.5),
        'norm_f': 1.0 + nrm((d,), 0.05),
    }


def reference(x_prompt, x_sample, state_ret, cache_win_k, cache_win_v, state_conv, c_prompt, c_sample,
              w_ada, b_ada, norm_mix, norm_ffn, ret_w_in, ret_gn, ret_w_out,
              kv_norm, kv_w_ada, kv_b_ada, w_kv, att_w_q, att_sinks, att_w_o,
              ffn_w_up, ffn_conv_w, ffn_conv_b, ffn_w_down, norm_f):
    log_gamma = retention_log_decay()

    def trunk(x, c, pos, ret_in, win_k_in, win_v_in, conv_in):
        sample = ret_in is not None
        B, T, _ = x.shape
        cs = jax.nn.silu(c)
        ret_out, conv_out = [], []
        k_sh = v_sh = k_pos = win_k_out = win_v_out = None
        for l in range(DEPTH):
            if l == N_A:
                kv_shift, kv_scale = jnp.split(cs @ kv_w_ada + kv_b_ada, 2, axis=-1)
                hk = modulate(rmsnorm(x, kv_norm), kv_shift, kv_scale)
                k_new, v_new = jnp.split(hk @ w_kv, 2, axis=-1)
                k_new = rope(k_new.reshape(B, T, N_KV_HEADS, HEAD_DIM), pos, ROPE_DIM, ROPE_THETA)
                v_new = v_new.reshape(B, T, N_KV_HEADS, HEAD_DIM)
                if sample:
                    wb = win_k_in.shape[1]
                    k_sh = jnp.concatenate([win_k_in.astype(k_new.dtype), k_new], axis=1)
                    v_sh = jnp.concatenate([win_v_in.astype(v_new.dtype), v_new], axis=1)
                    k_pos = PAST_LEN - wb + jnp.arange(wb + T, dtype=jnp.int32)
                    win_k_out, win_v_out = k_sh[:, -wb:], v_sh[:, -wb:]
                else:
                    k_sh, v_sh = k_new, v_new
                    wp = min(WINDOW, T)
                    win_k_out, win_v_out = k_new[:, -wp:], v_new[:, -wp:]
            sh1, sc1, g1, sh2, sc2, g2 = jnp.split(cs @ w_ada[l] + b_ada[l], N_MOD, axis=-1)
            h = modulate(rmsnorm(x, norm_mix[l]), sh1, sc1)
            if l < N_A:
                y, s_new = retention_mixer(h, pos, ret_w_in[l], ret_gn[l], ret_w_out[l], log_gamma,
                                           ret_in[l] if sample else None)
                ret_out.append(s_new)
            else:
                j = l - N_A
                q = rope((h @ att_w_q[j]).reshape(B, T, N_Q_HEADS, HEAD_DIM), pos, ROPE_DIM, ROPE_THETA)
                if sample:
                    o = sink_attention(q, k_sh, v_sh, pos, k_pos, att_sinks[j])
                else:
                    o = swa_prompt(q, k_sh, v_sh, att_sinks[j])
                y = o @ att_w_o[j]
            x = x + g1[:, None, :] * y
            h = modulate(rmsnorm(x, norm_ffn[l]), sh2, sc2)
            y, c_state = conv_ffn(h, ffn_w_up[l], ffn_conv_w[l], ffn_conv_b[l], ffn_w_down[l],
                                  conv_in[l] if sample else None)
            conv_out.append(c_state)
            x = x + g2[:, None, :] * y
        return rmsnorm(x, norm_f), jnp.stack(ret_out), win_k_out, win_v_out, jnp.stack(conv_out)

    pos_p = jnp.arange(x_prompt.shape[1], dtype=jnp.int32)
    pos_s = PAST_LEN + jnp.arange(x_sample.shape[1], dtype=jnp.int32)
    y_prompt, ret_prompt, win_k_prompt, win_v_prompt, conv_prompt = trunk(
        x_prompt, c_prompt, pos_p, None, None, None, None)
    y_sample, ret_sample, win_k_sample, win_v_sample, conv_sample = trunk(
        x_sample, c_sample, pos_s, state_ret, cache_win_k, cache_win_v, state_conv)
    return (y_prompt, y_sample, ret_prompt, ret_sample, win_k_prompt, win_v_prompt,
            win_k_sample, win_v_sample, conv_prompt, conv_sample)
```

```python
import contextlib
import numpy as np
import concourse.bass as bass
import concourse.mybir as mybir
from concourse.bass_utils import run_bass_kernel_spmd

F32 = mybir.dt.float32
BF16 = mybir.dt.bfloat16
ALU = mybir.AluOpType
AF = mybir.ActivationFunctionType

D = 2048
KC = 16
TP = 1024
TS = 32
NT = TP + TS
NTX = NT + 2
DEPTH = 4
N_A = 2
RH = 8
PAST = 16384
EPS = 1e-6
NEG = -1e30
UE = 1074
NTILES = [(0, 512), (512, 512), (1024, 32)]
NTILES_X = [(0, 512), (512, 512), (1024, 34)]
TTILES = [(i * 128, 128) for i in range(8)] + [(1024, 32)]
GAM = [1.0 - 2.0 ** (-5.0 - h) for h in range(RH)]
NCORES = 8
WCOLS = 4096


class Buf:
    __slots__ = ("name", "w", "r")

    def __init__(self, name):
        self.name = name
        self.w = None
        self.r = {}


class Q:
    def __init__(self, fw, name, eng, is_pe=False):
        self.fw = fw
        self.name = name
        self.eng = eng
        self.is_pe = is_pe
        self.sid = fw.new_sem(name)
        self.cnt = 0
        self.seen = {}
        self.dsems = []
        self.dcnt = []
        self.drr = 0
        self.log = []
        self.tags = []

    def wait(self, ev, same_ok=False):
        if ev is None:
            return
        s, v = ev
        if s == self.sid and (same_ok or self.is_pe):
            return
        if self.seen.get(s, 0) >= v:
            return
        self.eng.wait_ge(self.fw.sems[s], v)
        self.seen[s] = v
        self.log.append(("w", s, v))


class FW:
    def __init__(self, nc):
        self.nc = nc
        self.sems = []
        self.tag = ""
        self.kind = ""
        self.pe = Q(self, "pe", nc.tensor, is_pe=True)
        self.act = Q(self, "act", nc.scalar)
        self.dve = Q(self, "dve", nc.vector)
        self.pool = Q(self, "pool", nc.gpsimd)
        self.sp = Q(self, "sp", nc.sync)
        for q, n in ((self.pool, 6), (self.sp, 10)):
            for i in range(n):
                q.dsems.append(self.new_sem(f"{q.name}_d{i}"))
                q.dcnt.append(0)
        self.cc_sid = self.new_sem("cc")
        self.cc_cnt = 0
        self.queues = [self.pe, self.act, self.dve, self.pool, self.sp]

    def new_sem(self, name):
        self.sems.append(self.nc.alloc_semaphore("s_" + name))
        return len(self.sems) - 1

    def _deps(self, q, R, W):
        for b in R:
            q.wait(b.w)
        for b in W:
            q.wait(b.w, same_ok=True)
            for s, v in b.r.items():
                q.wait((s, v), same_ok=True)

    def _commit(self, ev, R, W):
        s, v = ev
        for b in R:
            if b.r.get(s, 0) < v:
                b.r[s] = v
        for b in W:
            b.w = ev
            b.r = {}

    def op(self, q, emit, R=(), W=(), sig=True):
        self._deps(q, R, W)
        ins = emit()
        q.tags.append((self.tag, self.kind))
        ev = (q.sid, q.cnt + 1)
        if sig:
            ins.then_inc(self.sems[q.sid], 1)
            q.cnt += 1
            q.log.append(("i", q.sid, 1))
        self._commit(ev, R, W)
        return ev

    def dma(self, q, out, in_, R=(), W=()):
        self._deps(q, R, W)
        j = q.drr
        q.drr = (q.drr + 1) % len(q.dsems)
        sid = q.dsems[j]
        q.wait((sid, 16 * q.dcnt[j]))
        ins = q.eng.dma_start(out=out, in_=in_)
        q.dcnt[j] += 1
        ins.then_inc(self.sems[sid], 16)
        q.log.append(("i", sid, 16))
        ev = (sid, 16 * q.dcnt[j])
        self._commit(ev, R, W)
        return ev

    def allgather(self, in_ap, out_ap, R, W):
        q = self.pool
        self._deps(q, R, W)
        q.wait((self.cc_sid, self.cc_cnt))
        ins = self.nc.gpsimd.collective_compute(
            "AllGather", ALU.bypass, replica_groups=[[0, 1], [2, 3], [4, 5], [6, 7]],
            ins=[in_ap], outs=[out_ap])
        self.cc_cnt += 1
        ins.then_inc(self.sems[self.cc_sid], 1)
        q.log.append(("i", self.cc_sid, 1))
        ev = (self.cc_sid, self.cc_cnt)
        self._commit(ev, R, W)
        return ev

    def barrier(self):
        evs = []
        for q in self.queues:
            if q.cnt:
                evs.append((q.sid, q.cnt))
            for sid, c in zip(q.dsems, q.dcnt):
                if c:
                    evs.append((sid, 16 * c))
        if self.cc_cnt:
            evs.append((self.cc_sid, self.cc_cnt))
        for q in self.queues:
            for ev in evs:
                q.wait(ev, same_ok=True)

    def simulate(self):
        val = {}
        pos = {q.name: 0 for q in self.queues}
        progress = True
        while progress:
            progress = False
            for q in self.queues:
                p = pos[q.name]
                while p < len(q.log):
                    k, s_, v = q.log[p]
                    if k == "w":
                        if val.get(s_, 0) < v:
                            break
                    else:
                        val[s_] = val.get(s_, 0) + v
                    p += 1
                    progress = True
                pos[q.name] = p
        stuck = {q.name: (pos[q.name], len(q.log), q.log[pos[q.name]] if pos[q.name] < len(q.log) else None) for q in self.queues}
        ok = all(pos[q.name] == len(q.log) for q in self.queues)
        return ok, stuck, val

    def finish(self):
        q = self.sp
        for qq in self.queues:
            if qq.cnt:
                q.wait((qq.sid, qq.cnt), same_ok=True)
            for sid, c in zip(qq.dsems, qq.dcnt):
                if c:
                    q.wait((sid, 16 * c))
        if self.cc_cnt:
            q.wait((self.cc_sid, self.cc_cnt))


def n_weight_pieces():
    per_ret = RH * 8 + 8 * 6
    per_att = 4 * 4 + 8 * 6
    return 2 * per_ret + 2 + 2 * per_att


def build_program():
    nc = bass.Bass("TRN2", target_bir_lowering=False)
    fw = FW(nc)
    PE, ACT, DVE, POOL, SP = fw.pe, fw.act, fw.dve, fw.pool, fw.sp
    NP = n_weight_pieces()

    def din(name, shape, dt=F32):
        return nc.dram_tensor(name, list(shape), dt, kind="ExternalInput")

    def dout(name, shape, dt=F32):
        return nc.dram_tensor(name, list(shape), dt, kind="ExternalOutput")

    d_xT = din("xT", [128, KC, NT])
    d_cT = din("cT", [128, KC, 5])
    d_flag = din("flag", [128, 1])
    d_wts = din("wts", [NP, 128, WCOLS])
    d_vecl = din("vecl", [DEPTH, 128, 416])
    d_vecg = din("vecg", [128, 64])
    d_ropeR = din("ropeR", [128, 2, NT])
    d_qdec = din("qdec", [RH, 128, 160])
    d_rsc = din("rsc", [128, 64])
    d_cmask = din("cmask", [128, 160])
    d_sret = din("sret", [N_A, 4, RH, 256, 512])
    d_convs = din("convs", [DEPTH, 128, 64, 8])
    d_ropeA = din("ropeA", [128, 9, 2, 8])
    d_amask = din("amask", [128, 2, 256])
    d_smask = din("smask", [32, 544])
    d_ckT = din("ckT", [128, 4, 512])
    d_cv = din("cv", [128, 4, 256])
    d_cknat = din("cknat", [4, 128, 256])
    d_cvnat = din("cvnat", [4, 128, 256])
    d_sinks = din("sinks", [128, 2, 32])
    d_ident = din("ident", [128, 128])
    d_bada = din("bada", [128, DEPTH, 96])
    NAP = 2 * (8 * 48 + 32)
    d_wada = din("wada", [NAP, 128, 1024])

    o_yT = dout("yT", [128, KC, NT])
    o_retp = dout("retp", [N_A, RH, 256, 512])
    o_rets = dout("rets", [N_A, 4, RH, 256, 512])
    o_wkp = dout("wkp", [128, 256])
    o_wvp = dout("wvp", [128, 256])
    o_wks = dout("wks", [4, 128, 256])
    o_wvs = dout("wvs", [4, 128, 256])
    o_conv = dout("convo", [DEPTH, 128, 640])

    x_sb = [[(nc.dram_tensor(f"xs_b{l}_{h}", [128, 1024], F32), nc.dram_tensor(f"xs_g{l}_{h}", [256, 1024], F32))
             for h in range(RH)] for l in range(N_A)]
    x_fb = [(nc.dram_tensor(f"xf_b{l}", [128, 32], F32), nc.dram_tensor(f"xf_g{l}", [256, 32], F32)) for l in range(DEPTH)]
    x_kv = (nc.dram_tensor("xkv_b", [128, 512], F32), nc.dram_tensor("xkv_g", [256, 512], F32))
    dbufs = {}

    def dbuf(t):
        if t.name not in dbufs:
            dbufs[t.name] = Buf(t.name)
        return dbufs[t.name]

    es = contextlib.ExitStack()

    uniq = [0]

    def sb(name, shape, dt, stack=None):
        uniq[0] += 1
        t = (stack or es).enter_context(nc.sbuf_tensor(f"s_{name}_{uniq[0]}", list(shape), dt))
        return t, Buf(name)

    x, _ = sb("x", [128, KC, NT], F32)
    xb = [[Buf(f"x{c}_{n}") for n in range(3)] for c in range(KC)]
    hT, _ = sb("hT", [128, KC, NTX], BF16)
    hb = [Buf(f"h{c}") for c in range(KC)]
    modall, _ = sb("modall", [128, 64, 33], F32)
    modallb = [Buf(f"mod_{j}") for j in range(64)]
    M = {"t": modall, "b": modallb, "g": 32}
    bada, badab = sb("bada", [128, DEPTH, 96], F32)
    NWS = 3
    wsl = [sb(f"w{i}", [128, WCOLS], BF16) for i in range(NWS)]
    asl = [sb(f"wa{i}", [128, 1024], BF16) for i in range(2)]
    vecl, veclb = sb("vecl", [128, 416], F32)
    vecg, vecgb = sb("vecg", [128, 64], F32)
    identb, identbb = sb("identb", [128, 128], BF16)
    onesb, onesbb = sb("onesb", [128, 128], BF16)
    csT, csTb = sb("csT", [128, KC, 33], BF16)
    flag, flagb = sb("flag", [128, 1], F32)
    xnb, xnbb = sb("xnb", [128, KC, 2], F32)
    small, smallb = sb("small", [128, 64], F32)
    setup_stk = contextlib.ExitStack()
    identf, identfb = sb("identf", [128, 128], F32, setup_stk)
    cTs, cTsb = sb("cTs", [128, KC, 5], F32, setup_stk)

    psf = [(nc.alloc_psum_tensor(f"psf{i}", [128, 512], F32), Buf(f"psf{i}")) for i in range(6)]
    psb = [(nc.alloc_psum_tensor(f"psb{i}", [128, 1024], BF16), Buf(f"psb{i}")) for i in range(2)]
    rr = {"f": 0, "b": 0}

    def PF():
        i = rr["f"]
        rr["f"] = (i + 1) % 6
        return psf[i]

    def PB():
        i = rr["b"]
        rr["b"] = (i + 1) % 2
        return psb[i]

    wstate = {"n": 0}

    specs = []

    def WNEXT(spec, cols=WCOLS):
        i = wstate["n"]
        wstate["n"] += 1
        specs.append(spec)
        t, b = wsl[i % NWS]
        fw.dma(POOL, t[:, 0:cols], d_wts[i, :, 0:cols], W=[b])
        return t, b

    def mm(out, lhsT, rhs, start, stop, R, W, tp=None, sig=None):
        sg = stop if sig is None else sig
        fw.kind = 'm' if tp is None else 'p'
        if tp is None:
            fw.op(PE, lambda: nc.tensor.matmul(out, lhsT, rhs, start=start, stop=stop), R=R, W=W, sig=sg)
        else:
            fw.op(PE, lambda: nc.tensor.matmul(out, lhsT, rhs, start=start, stop=stop, tile_position=tp), R=R, W=W, sig=sg)

    def tr(out, in_, R, W, sig=True):
        kp = in_.shape[0]
        fw.kind = 't'
        fw.op(PE, lambda: nc.tensor.transpose(out, in_, identb[0:kp, 0:kp]), R=list(R) + [identbb], W=W, sig=sig)

    def act(out, in_, func, R, W, bias=0.0, scale=1.0, accum=None):
        if accum is None:
            return fw.op(ACT, lambda: nc.scalar.activation(out, in_, func, bias=bias, scale=scale), R=R, W=W)
        return fw.op(ACT, lambda: nc.scalar.activation(out, in_, func, bias=bias, scale=scale, accum_out=accum), R=R, W=W)

    def tt(q, out, a, b, op, R, W):
        return fw.op(q, lambda: q.eng.tensor_tensor(out, a, b, op), R=R, W=W)

    def ts(q, out, a, s1, s2, op0, op1, R, W):
        if s2 is None:
            return fw.op(q, lambda: q.eng.tensor_scalar(out, a, s1, None, op0), R=R, W=W)
        return fw.op(q, lambda: q.eng.tensor_scalar(out, a, s1, s2, op0, op1), R=R, W=W)

    def stt(q, out, a, s, b, op0, op1, R, W):
        return fw.op(q, lambda: q.eng.scalar_tensor_tensor(out, a, s, b, op0, op1), R=R, W=W)

    def cp(q, out, in_, R, W):
        if q is ACT:
            return act(out, in_, AF.Copy, R, W)
        return fw.op(q, lambda: q.eng.tensor_copy(out, in_), R=R, W=W)

    def ntile_of(c0):
        return 0 if c0 < 512 else (1 if c0 < 1024 else 2)

    for c in range(KC):
        fw.dma(SP, x[:, c, :], d_xT[:, c, :], W=xb[c])
    fw.dma(SP, cTs[:], d_cT[:, :, :], W=[cTsb])
    fw.dma(SP, flag[:], d_flag[:, :], W=[flagb])
    fw.dma(SP, vecg[:], d_vecg[:, :], W=[vecgb])
    fw.dma(SP, identf[:], d_ident[:, :], W=[identfb])
    fw.op(DVE, lambda: nc.vector.memset(onesb[:], 1.0), W=[onesbb])
    cp(DVE, identb[:], identf[:], [identfb], [identbb])
    act(csT[:, :, 0:1], cTs[:, :, 0:1], AF.Silu, [cTsb], [csTb])
    for s in range(4):
        for t in range(8):
            act(csT[:, :, 1 + 8 * s + t:2 + 8 * s + t], cTs[:, :, 1 + s:2 + s], AF.Silu, [cTsb], [csTb])
    fw.barrier()
    setup_stk.close()

    class AdaJob:
        def __init__(self, wname, l, c0, nj, dst_t, dst_b, bias_fn, jmap=None):
            self.wname, self.l, self.c0, self.nj = wname, l, c0, nj
            self.dst_t, self.dst_b, self.bias_fn = dst_t, dst_b, bias_fn
            self.jmap = jmap or (lambda j: j)
            self.done = 0
            self.total = nj * 2

        def step(self):
            i = self.done
            self.done += 1
            prev_tag = fw.tag
            fw.tag = "adaln"
            j, hf = i // 2, i % 2
            n = astate["use"]
            astate["use"] += 1
            assert aplan[n] == (self.wname, self.l, self.c0 + j * 128 + hf * 64), (n, aplan[n], self.wname, self.l, j, hf)
            ada_dma(n)
            wt, wb_ = asl[n % 2]
            wv = wt[:, :].rearrange("p (k c) -> p k c", c=64)
            pt, pb = PF()
            lo, hi = hf * 64, hf * 64 + 64
            for k in range(KC):
                mm(pt[lo:hi, 0:33], wv[:, k, :], csT[:, k, :], k == 0, k == KC - 1, [wb_, csTb], [pb], tp=(0, lo))
            bcol, bb = self.bias_fn(j)
            jd = self.jmap(j)
            act(self.dst_t[lo:hi, jd, :], pt[lo:hi, 0:33], AF.Identity, [pb, bb], [self.dst_b[jd]], bias=bcol[lo:hi, :])
            ada_dma(n + 2)
            fw.tag = prev_tag

    aplan = []

    def _plan(wname, l, c0, nj):
        for j in range(nj):
            for hf in range(2):
                aplan.append((wname, l, c0 + j * 128 + hf * 64))

    _plan("w_ada", 0, 0, 48)
    for l_ in range(DEPTH):
        _plan("w_ada", l_, 6144, 48)
        if l_ + 1 < DEPTH:
            if l_ + 1 == N_A:
                _plan("kv_w_ada", None, 0, 32)
            _plan("w_ada", l_ + 1, 0, 48)
    assert len(aplan) == NAP
    astate = {"use": 0, "dma": 0, "opp": 0, "stride": 1}

    def ada_dma(n):
        while astate["dma"] <= n and astate["dma"] < NAP:
            i = astate["dma"]
            astate["dma"] += 1
            t, b_ = asl[i % 2]
            fw.dma(POOL, t[:, :], d_wada[i, :, :], W=[b_])

    def ada_one():
        astate["opp"] += 1
        if astate["opp"] % astate["stride"]:
            return
        for jb in ada["jobs"]:
            if jb.done < jb.total:
                jb.step()
                return

    ada = {"jobs": [], "k": 0}

    def ada_queue(job):
        ada["jobs"].append(job)

    def ada_total():
        return sum(jb.total for jb in ada["jobs"]), sum(jb.done for jb in ada["jobs"])

    def ada_tick(frac):
        tot, done = ada_total()
        want = min(tot, int(np.ceil(tot * frac)))
        for jb in ada["jobs"]:
            while done < want and jb.done < jb.total:
                jb.step()
                done += 1

    def ada_flush():
        for jb in ada["jobs"]:
            while jb.done < jb.total:
                jb.step()
        ada["jobs"] = []

    def ada_step(n):
        for jb in ada["jobs"]:
            while n > 0 and jb.done < jb.total:
                jb.step()
                n -= 1

    def ada_job_std(l, part, slot):
        return AdaJob("w_ada", l, part * 6144, 48, modall, modallb,
                      lambda j: (bada[:, l, part * 48 + j:part * 48 + j + 1], badab),
                      jmap=lambda j: j if j < 32 else 32 + 16 * slot + (j - 32))

    def make_amod(gt, gbuf, gcol0):
        mod, modb = M["t"], M["b"]
        for kc in range(KC):
            ts(DVE, mod[:, 16 + kc, :], mod[:, 16 + kc, :], 1.0, gt[:, gcol0 + kc:gcol0 + kc + 1], ALU.add, ALU.mult,
               [modb[16 + kc], gbuf], [modb[16 + kc]])

    def rms_stats(stk, ncols, with_nb):
        fw.tag = "norm"
        rstd, rstdb = sb("rstd", [128, NTX], F32, stk)
        sqs = [sb(f"sq{i}", [128, NTX], BF16, stk) for i in range(2)]
        tiles = NTILES_X if with_nb else NTILES
        pts = [PF() for _ in tiles]
        for kc in range(KC):
            sqt, sqb = sqs[kc % 2]
            act(sqt[:, 0:NT], x[:, kc, :], AF.Square, xb[kc], [sqb])
            if with_nb:
                act(sqt[:, NT:NTX], xnb[:, kc, :], AF.Square, [xnbb], [sqb])
            for (n0, nn), (pt, pb) in zip(tiles, pts):
                mm(pt[:, 0:nn], onesb[:, :], sqt[:, n0:n0 + nn], kc == 0, kc == KC - 1, [onesbb, sqb], [pb], sig=True)
        for (n0, nn), (pt, pb) in zip(tiles, pts):
            ts(DVE, rstd[:, n0:n0 + nn], pt[:, 0:nn], 1.0 / D, EPS, ALU.mult, ALU.add, [pb], [rstdb])
        act(rstd[:, 0:ncols], rstd[:, 0:ncols], AF.Sqrt, [rstdb], [rstdb])
        fw.op(DVE, lambda: nc.vector.reciprocal(rstd[:, 0:ncols], rstd[:, 0:ncols]), R=[rstdb], W=[rstdb])
        return rstd, rstdb

    def norm_mod(with_nb):
        stk = contextlib.ExitStack()
        mod, modb = M["t"], M["b"]
        ncols = NTX if with_nb else NT
        rstd, rstdb = rms_stats(stk, ncols, with_nb)
        ntmp = [sb(f"ntmp{i}", [128, NTX], F32, stk) for i in range(2)]
        for kc in range(KC):
            nt_, ntb = ntmp[kc % 2]
            A = mod[:, 16 + kc, :]
            Ab = modb[16 + kc]
            tt(DVE, nt_[:, 0:NT], x[:, kc, :], rstd[:, 0:NT], ALU.mult, xb[kc] + [rstdb], [ntb])
            if with_nb:
                tt(DVE, nt_[:, NT:NTX], xnb[:, kc, :], rstd[:, NT:NTX], ALU.mult, [xnbb, rstdb], [ntb])
            act(hT[:, kc, 0:TP], nt_[:, 0:TP], AF.Identity, [ntb, Ab, modb[kc]], [hb[kc]], bias=mod[:, kc, 0:1], scale=A[:, 0:1])
            if with_nb:
                act(hT[:, kc, NT:NTX], nt_[:, NT:NTX], AF.Identity, [ntb, Ab, modb[kc]], [hb[kc]], bias=mod[:, kc, 0:1], scale=A[:, 0:1])
            tt(DVE, nt_[:, TP:NT], nt_[:, TP:NT], A[:, 1:33], ALU.mult, [ntb, Ab], [ntb])
            tt(DVE, hT[:, kc, TP:NT], nt_[:, TP:NT], mod[:, kc, 1:33], ALU.add, [ntb, modb[kc]], [hb[kc]])
            if kc % 4 == 1 and M["g"] is not None:
                ada_one()
                ada_one()
        fw.barrier()
        stk.close()

    def resid_add(pt, pb, m, n0, nn, gj):
        mod, modb = M["t"], M["b"]
        nt_i = ntile_of(n0)
        if n0 < TP:
            stt(DVE, x[:, m, n0:n0 + nn], pt[:, 0:nn], mod[:, gj, 0:1], x[:, m, n0:n0 + nn], ALU.mult, ALU.add,
                [pb, modb[gj], xb[m][nt_i]], [xb[m][nt_i]])
        else:
            tt(DVE, small[:, 0:32], pt[:, 0:32], mod[:, gj, 1:33], ALU.mult, [pb, modb[gj]], [smallb])
            tt(DVE, x[:, m, TP:NT], x[:, m, TP:NT], small[:, 0:32], ALU.add, [smallb, xb[m][2]], [xb[m][2]])

    def proj_rows(pieces, kchunks, src, srcb):
        for m in range(KC):
            pts = [PF() for _ in NTILES]
            for kc in range(kchunks):
                wt, wb_ = pieces[kc // 2]
                wv = wt[:, :].rearrange("p (k c) -> p k c", c=2048)
                for (n0, nn), (pt, pb) in zip(NTILES, pts):
                    mm(pt[:, 0:nn], wv[:, kc % 2, m * 128:(m + 1) * 128], src[:, kc, n0:n0 + nn], kc == 0, kc == kchunks - 1,
                       [wb_, srcb], [pb])
            for (n0, nn), (pt, pb) in zip(NTILES, pts):
                resid_add(pt, pb, m, n0, nn, M["g"] + m)
            ada_one()

    def ffn(l):
        bt, gt = x_fb[l]
        fw.dma(SP, bt.ap().rearrange("p (k t) -> p k t", t=2), x[:, :, TP - 2:TP], R=[xb[c][1] for c in range(KC)], W=[dbuf(bt)])
        fw.allgather(bt.ap().opt(), gt.ap().opt(), R=[dbuf(bt)], W=[dbuf(gt)])
        fw.dma(SP, xnb[:], gt.ap()[0:128, :].rearrange("p (k t) -> p k t", t=2), R=[dbuf(gt)], W=[xnbb])
        make_amod(vecl, veclb, 112)
        norm_mod(True)
        astate["stride"] = 2
        st = contextlib.ExitStack()
        uext = [sb(f"uext{i}", [128, UE], F32, st) for i in range(2)]
        acc = [sb(f"acc{i}", [128, 1064], F32, st) for i in range(2)]
        sa, sab = sb("sa", [128, 1064], F32, st)
        zg = [sb(f"zg{i}", [128, 4, NT], BF16, st) for i in range(2)]
        cst, cstb = sb("cst", [128, 64, 8], F32, st)
        convo, convob = sb("convo", [128, 64, 10], F32, st)
        fw.dma(SP, cst[:], d_convs[l, :, :, :], W=[cstb])
        for i in range(2):
            fw.op(DVE, lambda i=i: nc.vector.memset(uext[i][0][:], 0.0), W=[uext[i][1]])
        def down(gi):
            zt_, zb_ = zg[gi % 2]
            fw.tag = "ffn.down"
            pieces = [WNEXT(("r", "ffn_w_down", l, gi * 512 + r * 256)) for r in range(2)]
            proj_rows(pieces, 4, zt_, zb_)

        for grp in range(8):
            zt, zb = zg[grp % 2]
            for jj in range(4):
                j = grp * 4 + jj
                if jj == 1 and grp > 0:
                    down(grp - 1)
                fw.tag = "ffn.up"
                wt, wb_ = WNEXT(("c2", "ffn_w_up", l, j))
                wv = wt[:, :].rearrange("p (k c) -> p k c", c=256)
                for half in range(2):
                    ut, ub = uext[half]
                    at, ab = acc[half]
                    ch = half * 32 + j
                    w0 = vecl[:, 128 + ch:129 + ch]
                    w1 = vecl[:, 192 + ch:193 + ch]
                    w2 = vecl[:, 256 + ch:257 + ch]
                    cb = vecl[:, 320 + ch:321 + ch]
                    cp(ACT, ut[:, 1026:1066].rearrange("p (s t) -> p s t", t=10)[:, :, 0:2],
                       cst[:, ch, :].rearrange("p (s t) -> p s t", t=2), [cstb], [ub])
                    for (n0, nn) in NTILES_X:
                        pt, pb = PF()
                        for k in range(KC):
                            mm(pt[:, 0:nn], wv[:, k, half * 128:(half + 1) * 128], hT[:, k, n0:n0 + nn], k == 0, k == KC - 1,
                               [wb_, hb[k]], [pb])
                        ada_one()
                        if n0 < TP:
                            cp(ACT, ut[:, 2 + n0:2 + n0 + nn], pt[:, 0:nn], [pb], [ub])
                        else:
                            cp(ACT, ut[:, 1026:1066].rearrange("p (s t) -> p s t", t=10)[:, :, 2:10],
                               pt[:, 0:32].rearrange("p (s t) -> p s t", t=8), [pb], [ub])
                            ts(DVE, ut[:, 0:2], pt[:, 32:34], flag[:, 0:1], None, ALU.mult, None, [pb, flagb], [ub])
                    ts(DVE, at[:, :], ut[:, 2:1066], w2, cb, ALU.mult, ALU.add, [ub, veclb], [ab])
                    stt(DVE, at[:, :], ut[:, 1:1065], w1, at[:, :], ALU.mult, ALU.add, [ub, veclb, ab], [ab])
                    stt(DVE, at[:, :], ut[:, 0:1064], w0, at[:, :], ALU.mult, ALU.add, [ub, veclb, ab], [ab])
                    cp(ACT, convo[:, ch, :].rearrange("p (s t) -> p s t", t=2),
                       ut[:, 1024:1074].rearrange("p (s t) -> p s t", t=10)[:, :, 0:2], [ub], [convob])
                a_t, a_b = acc[0]
                b_t, b_b = acc[1]
                act(sa[:, :], a_t[:, :], AF.Silu, [a_b], [sab])
                tt(DVE, zt[:, jj, 0:TP], sa[:, 0:TP], b_t[:, 0:TP], ALU.mult, [sab, b_b], [zb])
                tt(DVE, zt[:, jj, TP:NT].rearrange("p (s t) -> p s t", t=8),
                   sa[:, 1024:1064].rearrange("p (s t) -> p s t", t=10)[:, :, 2:10],
                   b_t[:, 1024:1064].rearrange("p (s t) -> p s t", t=10)[:, :, 2:10], ALU.mult, [sab, b_b], [zb])
        down(7)
        ada_flush()
        fw.dma(SP, o_conv[l, :, :], convo[:, :, :].rearrange("p c t -> p (c t)"), R=[convob])
        fw.barrier()
        st.close()

    def retention(l):
        make_amod(vecl, veclb, 96)
        norm_mod(False)
        astate["stride"] = 3
        st = contextlib.ExitStack()
        rope, ropeb = sb("rope", [128, 2, NT], F32, st)
        cmask, cmaskb = sb("cmask", [128, 160], F32, st)
        rsc, rscb = sb("rsc", [128, 64], F32, st)
        qd, qdb = sb("qd", [128, 160], F32, st)
        q2T, q2b = sb("q2T", [128, 2, NT], BF16, st)
        kT, kTb = sb("kT", [128, 2, NT], BF16, st)
        ktok, ktokb = sb("ktok", [128, 9, 256], BF16, st)
        ktf = [sb(f"ktf{i}", [128, 256], BF16, st) for i in range(2)]
        v, vb = sb("v", [128, 9, 512], BF16, st)
        gnsg, gnsgb = sb("gnsg", [128, 9, 512], BF16, st)
        S, Sbuf = sb("S", [128, 2, 512], F32, st)
        Sb, Sbb = sb("Sb", [128, 2, 512], BF16, st)
        S2, S2buf = sb("S2", [128, 2, 512], F32, st)
        gT, gTb = sb("gT", [128, 4, NT], BF16, st)
        ra = [sb(f"ra{i}", [128, 512], F32, st) for i in range(2)]
        scr = [dict(PT=sb(f"PT{i}", [128, 128], BF16, st), gated=sb(f"gated{i}", [128, 512], BF16, st),
                    st6=sb(f"st6{i}", [128, 8], F32, st), tn=ra[i]) for i in range(2)]
        q2m, q2mb = sb("q2m", [128, 2, 32], BF16, st)
        ktm, ktmb = sb("ktm", [32, 256], BF16, st)
        fw.dma(SP, rope[:], d_ropeR[:, :, :], W=[ropeb])
        fw.dma(SP, cmask[:], d_cmask[:, :], W=[cmaskb])
        fw.dma(SP, rsc[:], d_rsc[:, :], W=[rscb])
        cosT = rope[:, 0, :]
        sinT = rope[:, 1, :]

        def rope_proj(wt, wb_, out_t, out_b, scale_q):
            wv = wt[:, :].rearrange("p (k c) -> p k c", c=256)
            for (n0, nn) in NTILES:
                p1, p1b = PF()
                p2, p2b = PF()
                for mc, (pt, pb) in enumerate(((p1, p1b), (p2, p2b))):
                    for k in range(KC):
                        mm(pt[:, 0:nn], wv[:, k, mc * 128:(mc + 1) * 128], hT[:, k, n0:n0 + nn], k == 0, k == KC - 1,
                           [wb_, hb[k]], [pb])
                a_, ab_ = ra[0]
                b_, bb_ = ra[1]
                cs = cosT[:, n0:n0 + nn]
                sn = sinT[:, n0:n0 + nn]
                if nn == 512:
                    qdv = qd[:, 0:128].unsqueeze(1).to_broadcast([128, 4, 128])
                    av = a_[:, 0:512].rearrange("p (c t) -> p c t", t=128)
                else:
                    qdv = qd[:, 128:160]
                    av = a_[:, 0:32]
                for half in range(2):
                    pa, pab = (p1, p1b) if half == 0 else (p2, p2b)
                    pc, pcb = (p2, p2b) if half == 0 else (p1, p1b)
                    tt(DVE, a_[:, 0:nn], pa[:, 0:nn], cs, ALU.mult, [pab, ropeb], [ab_])
                    tt(DVE, b_[:, 0:nn], pc[:, 0:nn], sn, ALU.mult, [pcb, ropeb], [bb_])
                    op = ALU.subtract if half == 0 else ALU.add
                    if scale_q:
                        tt(DVE, a_[:, 0:nn], a_[:, 0:nn], b_[:, 0:nn], op, [ab_, bb_], [ab_])
                        ov = out_t[:, half, n0:n0 + nn]
                        if nn == 512:
                            ov = ov.rearrange("p (c t) -> p c t", t=128)
                        tt(DVE, ov, av, qdv, ALU.mult, [ab_, qdb], [out_b])
                    else:
                        tt(DVE, out_t[:, half, n0:n0 + nn], a_[:, 0:nn], b_[:, 0:nn], op, [ab_, bb_], [out_b])

        def tok_proj(evac, c0w):
            for half in range(2):
                wt, wb_ = WNEXT(("c", "ret_w_in", l, c0w + half * 256))
                wv = wt[:, :].rearrange("p (k c) -> p k c", c=256)
                for t, (c0, cn) in enumerate(TTILES):
                    pt, pb = PF()
                    for k in range(KC):
                        mm(pt[0:cn, 0:256], hT[:, k, c0:c0 + cn], wv[:, k, :], k == 0, k == KC - 1, [wb_, hb[k]], [pb])
                    evac(t, cn, half, pt, pb)
                    ada_one()

        for h in range(RH):
            gam = GAM[h]
            kdec_c = rsc[:, h:h + 1]
            ksc_c = rsc[:, 8 + h:9 + h]
            kdec_s = rsc[0:32, 16 + h:17 + h]
            ksc_s = rsc[0:32, 24 + h:25 + h]
            fw.dma(SP, qd[:], d_qdec[h, :, :], W=[qdb])
            sbufs = [(S2, S2buf), (S, Sbuf), (S2, S2buf), (S, Sbuf)]
            fw.dma(SP, S2[:], d_sret[l, 0, h, :, :].rearrange("(c p) e -> p c e", p=128), W=[S2buf])
            fw.tag = "ret.K"
            wt, wb_ = WNEXT(("c", "ret_w_in", l, 2048 + h * 256))
            rope_proj(wt, wb_, kT, kTb, False)
            fw.tag = "ret.ktr"
            for t, (c0, cn) in enumerate(TTILES):
                pt, pb = PB()
                for dc in range(2):
                    tr(pt[0:cn, dc * 128:(dc + 1) * 128], kT[:, dc, c0:c0 + cn], [kTb], [pb], sig=(dc == 1))
                act(ktok[0:cn, t, :], pt[0:cn, 0:256], AF.Identity, [pb, rscb], [ktokb], scale=(kdec_c if cn == 128 else kdec_s))
            fw.tag = "ret.V"
            tok_proj(lambda t, cn, half, pt, pb: cp(ACT, v[0:cn, t, half * 256:(half + 1) * 256], pt[0:cn, 0:256], [pb], [vb]),
                     4096 + h * 512)
            fw.tag = "ret.Sloc"
            pS = [PF(), PF()]
            for t in range(8):
                kf, kfb = ktf[t % 2]
                ts(DVE, kf[:, :], ktok[:, t, :], float(gam ** (128 * (7 - t))), None, ALU.mult, None, [ktokb], [kfb])
                for dc in range(2):
                    mm(pS[dc][0][:, :], kf[:, dc * 128:(dc + 1) * 128], v[:, t, :], t == 0, t == 7, [kfb, vb], [pS[dc][1]], sig=True)
            for dc in range(2):
                cp(ACT, S[:, dc, :], pS[dc][0][:, :], [pS[dc][1]], [Sbuf])
            bt, gt = x_sb[l][h]
            fw.dma(SP, bt.ap().rearrange("p (c e) -> p c e", c=2), S[:], R=[Sbuf], W=[dbuf(bt)])
            fw.allgather(bt.ap().opt(), gt.ap().opt(), R=[dbuf(bt)], W=[dbuf(gt)])
            fw.dma(SP, S[:], gt.ap()[0:128, :].rearrange("p (c e) -> p c e", c=2), R=[dbuf(gt)], W=[Sbuf])
            ts(DVE, S[:], S[:], flag[:, 0:1], None, ALU.mult, None, [Sbuf, flagb], [Sbuf])
            cp(ACT, Sb[:], S[:], [Sbuf], [Sbb])
            fw.tag = "ret.Q"
            wt, wb_ = WNEXT(("c", "ret_w_in", l, h * 256))
            rope_proj(wt, wb_, q2T, q2b, True)
            fw.tag = "ret.G"
            tok_proj(lambda t, cn, half, pt, pb: act(gnsg[0:cn, t, half * 256:(half + 1) * 256], pt[0:cn, 0:256], AF.Silu, [pb], [gnsgb]),
                     8192 + h * 512)

            def gn_chain(t, cn, po, pob, sc_):
                (st6_, st6b_), (tn_, tnb_), (gated_, gatedb_) = sc_["st6"], sc_["tn"], sc_["gated"]
                fw.op(DVE, lambda: nc.vector.bn_stats(st6_[0:cn, 0:6], po[0:cn, :]), R=[pob], W=[st6b_])
                fw.op(DVE, lambda: nc.vector.bn_aggr(st6_[0:cn, 6:8], st6_[0:cn, 0:6]), R=[st6b_], W=[st6b_])
                ts(DVE, st6_[0:cn, 7:8], st6_[0:cn, 7:8], EPS, None, ALU.add, None, [st6b_], [st6b_])
                act(st6_[0:cn, 7:8], st6_[0:cn, 7:8], AF.Sqrt, [st6b_], [st6b_])
                fw.op(DVE, lambda: nc.vector.reciprocal(st6_[0:cn, 7:8], st6_[0:cn, 7:8]), R=[st6b_], W=[st6b_])
                ts(DVE, tn_[0:cn, :], po[0:cn, :], st6_[0:cn, 6:7], st6_[0:cn, 7:8], ALU.subtract, ALU.mult, [pob, st6b_], [tnb_])
                tt(DVE, gated_[0:cn, :], tn_[0:cn, :], gnsg[0:cn, t, :], ALU.mult, [tnb_, gnsgb], [gatedb_])

            def stage_b(t, c0, cn, sc_):
                gated_, gatedb_ = sc_["gated"]
                pt, pb = PB()
                for ec in range(4):
                    tr(pt[:, ec * 128:ec * 128 + cn], gated_[0:cn, ec * 128:(ec + 1) * 128], [gatedb_], [pb], sig=(ec == 3))
                for ec in range(4):
                    gcol = vecl[:, 384 + h * 4 + ec:385 + h * 4 + ec]
                    act(gT[:, ec, c0:c0 + cn], pt[:, ec * 128:ec * 128 + cn], AF.Identity, [pb, veclb], [gTb], scale=gcol)

            fw.tag = "ret.scan"
            pend = None
            for t in range(8):
                c0 = t * 128
                sc_ = scr[t % 2]
                PT_, PTb_ = sc_["PT"]
                psc, pscb = PF()
                pS = [PF(), PF()]
                po, pob = PF()
                for dc in range(2):
                    mm(psc[:, 0:128], kT[:, dc, c0:c0 + 128], q2T[:, dc, c0:c0 + 128], dc == 0, dc == 1, [kTb, q2b], [pscb])
                stt(DVE, PT_[:, :], psc[:, 0:128], ksc_c, cmask[:, 0:128], ALU.mult, ALU.mult, [pscb, rscb, cmaskb], [PTb_])
                for dc in range(2):
                    mm(pS[dc][0][:, :], ktok[:, t, dc * 128:(dc + 1) * 128], v[:, t, :], True, True, [ktokb, vb], [pS[dc][1]])
                mm(po[:, :], PT_[:, :], v[:, t, :], True, False, [PTb_, vb], [pob])
                for dc in range(2):
                    mm(po[:, :], q2T[:, dc, c0:c0 + 128], Sb[:, dc, :], False, dc == 1, [q2b, Sbb], [pob])
                for dc in range(2):
                    stt(DVE, S[:, dc, :], S[:, dc, :], float(gam ** 128), pS[dc][0][:, :], ALU.mult, ALU.add, [Sbuf, pS[dc][1]], [Sbuf])
                cp(ACT, Sb[:], S[:], [Sbuf], [Sbb])
                gn_chain(t, 128, po, pob, sc_)
                if pend is not None:
                    stage_b(*pend)
                pend = (t, c0, 128, sc_)
            fw.dma(SP, o_retp[l, h, :, :].rearrange("(c p) e -> p c e", p=128), S[:], R=[Sbuf])

            fw.tag = "ret.samp"
            c0 = TP
            sc_ = scr[0]
            PT_, PTb_ = sc_["PT"]
            psc, pscb = PF()
            po, pob = PF()
            pS = [PF(), PF()]
            for dc in range(2):
                mm(psc[0:32, 0:32], kT[:, dc, c0:c0 + 32], q2T[:, dc, c0:c0 + 32], dc == 0, dc == 1, [kTb, q2b], [pscb])
            stt(DVE, PT_[0:32, 0:32], psc[0:32, 0:32], ksc_s, cmask[0:32, 128:160], ALU.mult, ALU.mult, [pscb, rscb, cmaskb], [PTb_])
            mm(po[0:32, :], PT_[0:32, 0:32], v[0:32, 8, :], True, False, [PTb_, vb], [pob])
            stage_b(*pend)
            for s in range(4):
                Sx, Sxb = sbufs[s]
                if s + 1 < 4:
                    Sn, Snb = sbufs[s + 1]
                    fw.dma(SP, Sn[:], d_sret[l, s + 1, h, :, :].rearrange("(c p) e -> p c e", p=128), W=[Snb])
                cp(ACT, Sb[:], Sx[:], [Sxb], [Sbb])
                fw.op(DVE, lambda: nc.vector.memset(q2m[:], 0.0), W=[q2mb])
                cp(DVE, q2m[:, :, 8 * s:8 * s + 8], q2T[:, :, c0 + 8 * s:c0 + 8 * s + 8], [q2b], [q2mb])
                for dc in range(2):
                    mm(po[0:32, :], q2m[:, dc, :], Sb[:, dc, :], False, (s == 3 and dc == 1), [q2mb, Sbb], [pob])
                ts(DVE, ktm[:, :], ktok[0:32, 8, :], rsc[0:32, 32 + s:33 + s], None, ALU.mult, None, [ktokb, rscb], [ktmb])
                for dc in range(2):
                    mm(pS[dc][0][:, :], ktm[:, dc * 128:(dc + 1) * 128], v[0:32, 8, :], True, True, [ktmb, vb], [pS[dc][1]])
                for dc in range(2):
                    stt(DVE, Sx[:, dc, :], Sx[:, dc, :], float(gam ** 8), pS[dc][0][:, :], ALU.mult, ALU.add, [Sxb, pS[dc][1]], [Sxb])
                fw.dma(SP, o_rets[l, s, h, :, :].rearrange("(c p) e -> p c e", p=128), Sx[:], R=[Sxb])
            gn_chain(8, 32, po, pob, sc_)
            stage_b(8, c0, 32, sc_)
            fw.tag = "ret.out"
            pieces = [WNEXT(("r", "ret_w_out", l, h * 512 + r * 256)) for r in range(2)]
            proj_rows(pieces, 4, gT, gTb)
        ada_flush()
        fw.barrier()
        st.close()

    att = {}

    def rope_apply(buf_t, buf_b, nh, t, cn, ropeA, ropeAb, rt):
        sv = buf_t[0:cn, 0:nh * 64].rearrange("p (h d) -> p h d", d=64)
        x1 = sv[:, :, 0:8]
        x2 = sv[:, :, 8:16]
        cs = ropeA[0:cn, t, 0, :].unsqueeze(1).to_broadcast([cn, nh, 8])
        sn = ropeA[0:cn, t, 1, :].unsqueeze(1).to_broadcast([cn, nh, 8])
        (r0, r0b), (r1, r1b) = rt
        a = r0[0:cn, 0, 0:nh, :]
        b = r0[0:cn, 1, 0:nh, :]
        c = r1[0:cn, 0, 0:nh, :]
        d = r1[0:cn, 1, 0:nh, :]
        tt(DVE, a, x1, cs, ALU.mult, [buf_b, ropeAb], [r0b])
        tt(DVE, b, x2, sn, ALU.mult, [buf_b, ropeAb], [r0b])
        tt(DVE, c, x2, cs, ALU.mult, [buf_b, ropeAb], [r1b])
        tt(DVE, d, x1, sn, ALU.mult, [buf_b, ropeAb], [r1b])
        tt(DVE, x1, a, b, ALU.subtract, [r0b], [buf_b])
        tt(DVE, x2, c, d, ALU.add, [r1b], [buf_b])

    def kv_stage(stk):
        kTa, kTab, vA, vAb = att["kTa"], att["kTab"], att["vA"], att["vAb"]
        saveM = dict(M)
        M["t"], M["b"] = att["kvmod"]
        M["g"] = None
        make_amod(vecg, vecgb, 0)
        norm_mod(False)
        M.update(saveM)
        st = contextlib.ExitStack()
        ropeA, ropeAb = sb("ropeA", [128, 9, 2, 8], F32, st)
        kf, kfb = sb("kf", [128, 512], F32, st)
        kr, krb = sb("kr", [128, 2, 256], BF16, st)
        rt = [sb(f"rt{i}", [128, 2, 8, 8], F32, st) for i in range(2)]
        fw.dma(SP, ropeA[:], d_ropeA[:, :, :, :], W=[ropeAb])
        fw.dma(POOL, kTa[:, :, 128 + TP:128 + TP + 512], d_ckT[:, :, :], W=[kTab])
        for s in range(4):
            fw.dma(POOL, vA[:, 10 + s, :], d_cv[:, s, :], W=[vAb])
        fw.tag = "kv"
        wk_ = WNEXT(("c", "w_kv", None, 0))
        wv_ = WNEXT(("c", "w_kv", None, 256))

        def finish_tile(tile_idx, keycol0, cn):
            cp(ACT, vA[0:cn, tile_idx, :], kf[0:cn, 256:512], [kfb], [vAb])
            for g in range(4):
                for dup in range(2):
                    o0 = (g % 2) * 128 + dup * 64
                    cp(DVE, kr[0:cn, g // 2, o0:o0 + 64], kf[0:cn, g * 64:(g + 1) * 64], [kfb], [krb])
            pt, pb = PB()
            for g in range(4):
                tr(pt[:, g * 128:g * 128 + cn], kr[0:cn, g // 2, (g % 2) * 128:(g % 2) * 128 + 128], [krb], [pb], sig=(g == 3))
            cp(ACT, kTa[:, :, keycol0:keycol0 + cn], pt[:, 0:512].rearrange("p (g c) -> p g c", c=128)[:, :, 0:cn], [pb], [kTab])

        bt, gt = x_kv
        for t, (c0, cn) in enumerate(TTILES):
            for half, (wt, wb_) in enumerate((wk_, wv_)):
                wv2 = wt[:, :].rearrange("p (k c) -> p k c", c=256)
                pt, pb = PF()
                for k in range(KC):
                    mm(pt[0:cn, 0:256], hT[:, k, c0:c0 + cn], wv2[:, k, :], k == 0, k == KC - 1, [wb_, hb[k]], [pb])
                cp(ACT, kf[0:cn, half * 256:(half + 1) * 256], pt[0:cn, 0:256], [pb], [kfb])
            rope_apply(kf, kfb, 4, t, cn, ropeA, ropeAb, rt)
            if t < 8:
                finish_tile(1 + t, 128 + c0, 128)
            else:
                finish_tile(9, 128 + TP + 512, 32)
            if t == 7:
                fw.dma(SP, o_wkp[:, :], kf[:, 0:256], R=[kfb])
                fw.dma(SP, o_wvp[:, :], kf[:, 256:512], R=[kfb])
                fw.dma(SP, bt.ap(), kf[:, :], R=[kfb], W=[dbuf(bt)])
                fw.allgather(bt.ap().opt(), gt.ap().opt(), R=[dbuf(bt)], W=[dbuf(gt)])
            if t == 8:
                for s in range(4):
                    fw.dma(SP, o_wks[s, 120:128, :], kf[8 * s:8 * s + 8, 0:256], R=[kfb])
                    fw.dma(SP, o_wvs[s, 120:128, :], kf[8 * s:8 * s + 8, 256:512], R=[kfb])
                    fw.dma(SP, o_wks[s, 0:120, :], d_cknat[s, 8:128, :])
                    fw.dma(SP, o_wvs[s, 0:120, :], d_cvnat[s, 8:128, :])
        fw.dma(SP, kf[:, :], gt.ap()[0:128, :], R=[dbuf(gt)], W=[kfb])
        finish_tile(0, 0, 128)
        fw.barrier()
        st.close()

    def attention(l, j):
        make_amod(vecl, veclb, 96)
        norm_mod(False)
        astate["stride"] = 1
        st = contextlib.ExitStack()
        kTa, kTab, vA, vAb = att["kTa"], att["kTab"], att["vA"], att["vAb"]
        ropeA, ropeAb = sb("ropeA", [128, 9, 2, 8], F32, st)
        rt = [sb(f"rt{i}", [128, 2, 8, 8], F32, st) for i in range(2)]
        qf, qfb = sb("qf", [128, 512], F32, st)
        qbf, qbfb = sb("qbf", [128, 512], BF16, st)
        qT, qTb = sb("qT", [128, 4, NT], BF16, st)
        oT, oTb = sb("oT", [128, 4, NT], BF16, st)
        amask, amaskb = sb("amask", [128, 2, 256], F32, st)
        smask, smaskb = sb("smask", [32, 544], F32, st)
        amk, amkb = sb("amk", [128, 2, 256], BF16, st)
        smk, smkb = sb("smk", [32, 544], BF16, st)
        sinks, sinksb = sb("sinks", [128, 2, 32], F32, st)
        pes = [sb(f"pe{i}", [128, 4, 256], BF16, st) for i in range(2)]
        pTs = [sb(f"pT{i}", [128, 8, 128], BF16, st) for i in range(2)]
        sms = [sb(f"sm{i}", [128, 32], F32, st) for i in range(2)]
        pss = [sb(f"pss{i}", [32, 544], BF16, st) for i in range(2)]
        fw.dma(SP, ropeA[:], d_ropeA[:, :, :, :], W=[ropeAb])
        fw.dma(SP, amask[:], d_amask[:, :, :], W=[amaskb])
        fw.dma(SP, smask[:], d_smask[:, :], W=[smaskb])
        fw.dma(SP, sinks[:], d_sinks[:, :, :], W=[sinksb])
        cp(DVE, amk[:], amask[:], [amaskb], [amkb])
        cp(DVE, smk[:], smask[:], [smaskb], [smkb])
        ctr = {"i": 0}

        for g in range(4):
            fw.tag = "att.q"
            wq = [WNEXT(("c", "att_w_q", j, g * 512 + half * 256)) for half in range(2)]
            for t, (c0, cn) in enumerate(TTILES):
                for half in range(2):
                    wt, wb_ = wq[half]
                    wv2 = wt[:, :].rearrange("p (k c) -> p k c", c=256)
                    pt, pb = PF()
                    for k in range(KC):
                        mm(pt[0:cn, 0:256], hT[:, k, c0:c0 + cn], wv2[:, k, :], k == 0, k == KC - 1, [wb_, hb[k]], [pb])
                    cp(ACT, qf[0:cn, half * 256:(half + 1) * 256], pt[0:cn, 0:256], [pb], [qfb])
                rope_apply(qf, qfb, 8, t, cn, ropeA, ropeAb, rt)
                cp(ACT, qbf[0:cn, :], qf[0:cn, :], [qfb], [qbfb])
                pt, pb = PB()
                for cc in range(4):
                    tr(pt[:, cc * 128:cc * 128 + cn], qbf[0:cn, cc * 128:(cc + 1) * 128], [qbfb], [pb], sig=(cc == 3))
                cp(ACT, qT[:, :, c0:c0 + cn], pt[:, 0:512].rearrange("p (c t) -> p c t", t=128)[:, :, 0:cn], [pb], [qTb])
                ada_one()
            def stage_a(t, bh):
                c0 = t * 128
                wm = 0 if t == 0 else 1
                ix = ctr["i"] % 2
                ctr["i"] += 1
                pe_, peb = pes[ix]
                sm, smb = sms[ix]
                banks = [PF(), PF()]
                for i, (ps, psb_) in enumerate(banks):
                    hp = 2 * bh + i
                    for hh in range(2):
                        base = hh * 64
                        mm(ps[:, hh * 256:(hh + 1) * 256], qT[base:base + 64, hp, c0:c0 + 128], kTa[base:base + 64, g, c0:c0 + 256],
                           True, False, [qTb, kTab], [psb_])
                        mm(ps[:, hh * 256:(hh + 1) * 256], identb[:, :], amk[:, wm, :], False, True, [identbb, amkb], [psb_])
                h0 = g * 8 + 4 * bh
                sk4 = sinks[:, j, h0:h0 + 4]
                for i, (ps, psb_) in enumerate(banks):
                    fw.op(DVE, lambda ps=ps, i=i: nc.vector.tensor_reduce(
                        sm[:, 2 * i:2 * i + 2], ps[:, 0:512].rearrange("p (h k) -> p h k", k=256), mybir.AxisListType.X, ALU.max),
                        R=[psb_], W=[smb])
                stt(DVE, sm[:, 4:8], sm[:, 0:4], 0.125, sk4, ALU.mult, ALU.max, [smb, sinksb], [smb])
                ts(DVE, sm[:, 8:12], sm[:, 4:8], -1.0, None, ALU.mult, None, [smb], [smb])
                tt(DVE, sm[:, 16:20], sk4, sm[:, 4:8], ALU.subtract, [sinksb, smb], [smb])
                for i, (ps, psb_) in enumerate(banks):
                    for hh in range(2):
                        hd = 2 * i + hh
                        act(pe_[:, hd, :], ps[:, hh * 256:(hh + 1) * 256], AF.Exp, [psb_, smb], [peb, smb],
                            bias=sm[:, 8 + hd:9 + hd], scale=0.125, accum=sm[:, 12 + hd:13 + hd])
                act(sm[:, 20:24], sm[:, 16:20], AF.Exp, [smb], [smb])
                tt(DVE, sm[:, 24:28], sm[:, 12:16], sm[:, 20:24], ALU.add, [smb], [smb])
                fw.op(DVE, lambda: nc.vector.reciprocal(sm[:, 28:32], sm[:, 24:28]), R=[smb], W=[smb])
                tt(DVE, pe_[:, :, :], pe_[:, :, :], sm[:, 28:32].unsqueeze(2).to_broadcast([128, 4, 256]), ALU.mult, [peb, smb], [peb])
                return (t, bh, ix)

            def stage_bb(t, bh, ix):
                c0 = t * 128
                pe_, peb = pes[ix]
                pT, pTb = pTs[ix]
                ptT, ptTb = PB()
                for hd in range(4):
                    for kb in range(2):
                        o0 = (hd * 2 + kb) * 128
                        tr(ptT[:, o0:o0 + 128], pe_[:, hd, kb * 128:(kb + 1) * 128], [peb], [ptTb], sig=(hd == 3 and kb == 1))
                cp(ACT, pT[:, :, :], ptT[:, 0:1024].rearrange("p (a q) -> p a q", q=128), [ptTb], [pTb])
                for i in range(2):
                    hp = 2 * bh + i
                    po, pob = PF()
                    for hh in range(2):
                        hd = 2 * i + hh
                        for kb in range(2):
                            mm(po[hh * 64:(hh + 1) * 64, 0:128], vA[:, t + kb, g * 64:(g + 1) * 64], pT[:, hd * 2 + kb, :], kb == 0, kb == 1,
                               [vAb, pTb], [pob], tp=(0, hh * 64))
                    cp(ACT, oT[:, hp, c0:c0 + 128], po[:, 0:128], [pob], [oTb])

            fw.tag = "att.blk"
            pend = None
            for t in range(8):
                for bh in range(2):
                    cur = stage_a(t, bh)
                    ada_one()
                    if pend is not None:
                        stage_bb(*pend)
                    pend = cur
            stage_bb(*pend)
            fw.tag = "att.samp"
            c0 = TP
            K0 = 128 + TP

            def samp_a(hp, hh):
                ix = ctr["i"] % 2
                ctr["i"] += 1
                pn, pnb = pss[ix]
                sm, smb = sms[ix]
                hq = g * 8 + hp * 2 + hh
                base = hh * 64
                ps, psb_ = PF()
                ps2, ps2b = PF()
                mm(ps[0:32, 0:512], qT[base:base + 64, hp, c0:c0 + 32], kTa[base:base + 64, g, K0:K0 + 512], True, False, [qTb, kTab], [psb_])
                mm(ps[0:32, 0:512], identb[0:32, 0:32], smk[0:32, 0:512], False, True, [identbb, smkb], [psb_])
                mm(ps2[0:32, 0:32], qT[base:base + 64, hp, c0:c0 + 32], kTa[base:base + 64, g, K0 + 512:K0 + 544], True, False, [qTb, kTab], [ps2b])
                mm(ps2[0:32, 0:32], identb[0:32, 0:32], smk[0:32, 512:544], False, True, [identbb, smkb], [ps2b])
                fw.op(DVE, lambda: nc.vector.tensor_reduce(sm[0:32, 0:1], ps[0:32, 0:512], mybir.AxisListType.X, ALU.max), R=[psb_], W=[smb])
                fw.op(DVE, lambda: nc.vector.tensor_reduce(sm[0:32, 1:2], ps2[0:32, 0:32], mybir.AxisListType.X, ALU.max), R=[ps2b], W=[smb])
                tt(DVE, sm[0:32, 2:3], sm[0:32, 0:1], sm[0:32, 1:2], ALU.max, [smb], [smb])
                ts(DVE, sm[0:32, 4:5], sm[0:32, 2:3], 0.125, sinks[0:32, j, hq:hq + 1], ALU.mult, ALU.max, [smb, sinksb], [smb])
                ts(DVE, sm[0:32, 8:9], sm[0:32, 4:5], -1.0, None, ALU.mult, None, [smb], [smb])
                act(pn[0:32, 0:512], ps[0:32, 0:512], AF.Exp, [psb_, smb], [pnb, smb], bias=sm[0:32, 8:9], scale=0.125, accum=sm[0:32, 12:13])
                act(pn[0:32, 512:544], ps2[0:32, 0:32], AF.Exp, [ps2b, smb], [pnb, smb], bias=sm[0:32, 8:9], scale=0.125, accum=sm[0:32, 13:14])
                tt(DVE, sm[0:32, 16:17], sinks[0:32, j, hq:hq + 1], sm[0:32, 4:5], ALU.subtract, [sinksb, smb], [smb])
                act(sm[0:32, 20:21], sm[0:32, 16:17], AF.Exp, [smb], [smb])
                tt(DVE, sm[0:32, 24:25], sm[0:32, 12:13], sm[0:32, 13:14], ALU.add, [smb], [smb])
                tt(DVE, sm[0:32, 25:26], sm[0:32, 24:25], sm[0:32, 20:21], ALU.add, [smb], [smb])
                fw.op(DVE, lambda: nc.vector.reciprocal(sm[0:32, 28:29], sm[0:32, 25:26]), R=[smb], W=[smb])
                ts(DVE, pn[0:32, :], pn[0:32, :], sm[0:32, 28:29], None, ALU.mult, None, [pnb, smb], [pnb])
                return (hp, hh, ix)

            spo = {}

            def samp_b(hp, hh, ix):
                pn, pnb = pss[ix]
                pT, pTb = pTs[ix]
                if hh == 0:
                    spo[hp] = PF()
                po, pob = spo[hp]
                ptT, ptTb = PB()
                for kb in range(5):
                    kn = 128 if kb < 4 else 32
                    tr(ptT[0:kn, kb * 32:(kb + 1) * 32], pn[0:32, kb * 128:kb * 128 + kn], [pnb], [ptTb], sig=(kb == 4))
                cp(ACT, pT[:, 0:4, 0:32], ptT[:, 0:128].rearrange("p (k q) -> p k q", q=32), [ptTb], [pTb])
                cp(ACT, pT[0:32, 4, 0:32], ptT[0:32, 128:160], [ptTb], [pTb])
                for kb in range(5):
                    kn = 128 if kb < 4 else 32
                    vt = vA[0:kn, (10 + kb) if kb < 4 else 9, g * 64:(g + 1) * 64]
                    mm(po[hh * 64:(hh + 1) * 64, 0:32], vt, pT[0:kn, kb, 0:32], kb == 0, kb == 4, [vAb, pTb], [pob], tp=(0, hh * 64))
                if hh == 1:
                    cp(ACT, oT[:, hp, c0:c0 + 32], po[:, 0:32], [pob], [oTb])

            pend = None
            for hp in range(4):
                for hh in range(2):
                    cur = samp_a(hp, hh)
                    if pend is not None:
                        samp_b(*pend)
                    pend = cur
            samp_b(*pend)
            fw.tag = "att.o"
            pieces = [WNEXT(("r", "att_w_o", j, g * 512 + r * 256)) for r in range(2)]
            proj_rows(pieces, 4, oT, oTb)
        ada_flush()
        fw.barrier()
        st.close()

    fw.dma(SP, bada[:], d_bada[:, :, :], W=[badab])
    kvstack = contextlib.ExitStack()
    kvmodstack = contextlib.ExitStack()
    ada_dma(1)
    ada_queue(ada_job_std(0, 0, 0))
    ada_flush()
    k = 0
    for l in range(DEPTH):
        fw.dma(SP, vecl[:], d_vecl[l, :, :], W=[veclb])
        M.update(t=modall, b=modallb, g=32 + 16 * (k % 2))
        ada_queue(ada_job_std(l, 1, (k + 1) % 2))
        if l < N_A:
            retention(l)
        else:
            if l == N_A:
                kv_stage(kvstack)
                kvmodstack.close()
            attention(l, l - N_A)
        k += 1
        M.update(t=modall, b=modallb, g=32 + 16 * (k % 2))
        if l + 1 < DEPTH:
            if l + 1 == N_A:
                kTa, kTab = sb("kTa", [128, 4, 128 + TP + 512 + 32], BF16, kvstack)
                vA, vAb = sb("vA", [128, 14, 256], BF16, kvstack)
                att.update(dict(kTa=kTa, kTab=kTab, vA=vA, vAb=vAb))
                kvm, _ = sb("kvmod", [128, 32, 33], F32, kvmodstack)
                kvmb = [Buf(f"kvmod{j}") for j in range(32)]
                att["kvmod"] = (kvm, kvmb)
                ada_queue(AdaJob("kv_w_ada", None, 0, 32, kvm, kvmb, lambda j: (vecg[:, 16 + j:17 + j], vecgb)))
            ada_queue(ada_job_std(l + 1, 0, (k + 1) % 2))
        ffn(l)
        k += 1
    kvstack.close()
    stk = contextlib.ExitStack()
    rstd, rstdb = rms_stats(stk, NT, False)
    for kc in range(KC):
        stt(DVE, x[:, kc, :], x[:, kc, :], vecg[:, 48 + kc:49 + kc], rstd[:, 0:NT], ALU.mult, ALU.mult, xb[kc] + [vecgb, rstdb], xb[kc])
        fw.dma(SP, o_yT[:, kc, :], x[:, kc, :], R=xb[kc])
    fw.finish()
    stk.close()
    es.close()
    assert wstate["n"] == NP, (wstate["n"], NP)
    ok, stuck, val = fw.simulate()
    assert ok, ("semaphore deadlock in emitted program", stuck)
    build_program.stats = {q.name: (q.cnt, len(q.log)) for q in fw.queues}
    build_program.maxsem = max(val.values())
    build_program.specs = specs
    build_program.aplan = aplan
    build_program.tags = {q.name: q.tags for q in fw.queues}
    return nc


def _piece_cols(Wm, cols):
    sub = Wm[:, cols]
    return np.ascontiguousarray(sub.reshape(KC, 128, 256).transpose(1, 0, 2)).reshape(128, WCOLS)


def _piece_rows(Wm, r0):
    sub = Wm[r0:r0 + 256, :]
    return np.ascontiguousarray(sub.reshape(2, 128, 2048).transpose(1, 0, 2)).reshape(128, WCOLS)


def build_weights(inp, specs):
    out = np.empty((len(specs), 128, WCOLS), np.float32)
    for i, (kind, name, idx, a0) in enumerate(specs):
        Wm = inp[name] if idx is None else inp[name][idx]
        if kind == "c":
            out[i] = _piece_cols(Wm, np.arange(a0, a0 + 256))
        elif kind == "c2":
            out[i] = _piece_cols(Wm, np.concatenate([np.arange(a0 * 128, (a0 + 1) * 128), np.arange(4096 + a0 * 128, 4096 + (a0 + 1) * 128)]))
        else:
            out[i] = _piece_rows(Wm, a0)
    return out


def build_wada(inp, aplan):
    out = np.empty((len(aplan), 128, 1024), np.float32)
    for i, (name, idx, c0) in enumerate(aplan):
        Wm = inp[name] if idx is None else inp[name][idx]
        out[i] = np.ascontiguousarray(Wm[:, c0:c0 + 64].reshape(KC, 128, 64).transpose(1, 0, 2)).reshape(128, 1024)
    return out


def fm(vec):
    return np.ascontiguousarray(np.asarray(vec, np.float32).reshape(-1, 128).T)


def build_inputs(inp, specs, aplan):
    f32 = np.float32
    wts = build_weights(inp, specs)
    wada = build_wada(inp, aplan)
    vecl = np.zeros((DEPTH, 128, 416), f32)
    for l in range(DEPTH):
        vecl[l, :, 0:96] = fm(inp["b_ada"][l])
        vecl[l, :, 96:112] = fm(inp["norm_mix"][l])
        vecl[l, :, 112:128] = fm(inp["norm_ffn"][l])
        for tap in range(3):
            vecl[l, :, 128 + tap * 64:128 + (tap + 1) * 64] = fm(inp["ffn_conv_w"][l, tap])
        vecl[l, :, 320:384] = fm(inp["ffn_conv_b"][l])
        if l < N_A:
            vecl[l, :, 384:416] = fm(inp["ret_gn"][l])
    bada = np.ascontiguousarray(np.stack([fm(inp["b_ada"][l]) for l in range(DEPTH)], 1))
    vecg = np.zeros((128, 64), f32)
    vecg[:, 0:16] = fm(inp["kv_norm"])
    vecg[:, 16:48] = fm(inp["kv_b_ada"])
    vecg[:, 48:64] = fm(inp["norm_f"])
    gam = np.array(GAM, np.float64)
    m = np.arange(128, dtype=np.float64)
    rsc = np.zeros((128, 64), np.float64)
    for h in range(RH):
        rsc[:, h] = gam[h] ** (127 - m) / 16.0
        rsc[:, 8 + h] = gam[h] ** (-(m + 1)) / 16.0
        ms = (np.arange(32) % 8).astype(np.float64)
        rsc[:32, 16 + h] = gam[h] ** (7 - ms) / 16.0
        rsc[:32, 24 + h] = gam[h] ** (-(ms + 1)) / 16.0
    for s in range(4):
        rsc[8 * s:8 * s + 8, 32 + s] = 1.0
    rsc = rsc.astype(f32)
    cmask = np.zeros((128, 160), f32)
    cmask[:, 0:128] = (m[:, None] <= m[None, :]).astype(f32)
    i32 = np.arange(32)
    cmask[:32, 128:160] = ((i32[:, None] // 8 == i32[None, :] // 8) & (i32[:, None] <= i32[None, :])).astype(f32)
    inv_r = (10000.0 ** (-np.arange(128, dtype=f32) * 2.0 / 256.0)).astype(f32)
    inv_a = (500000.0 ** (-np.arange(8, dtype=f32) * 2.0 / 16.0)).astype(f32)
    qi = np.arange(128)[:, None]
    kj = np.arange(256)[None, :]
    band = np.where(((kj < 128) & (kj >= qi)) | ((kj >= 128) & (kj - 128 <= qi)), 0.0, NEG).astype(f32)
    first = np.where((kj >= 128) & (kj - 128 <= qi), 0.0, NEG).astype(f32)
    sm_ = np.full((32, 544), NEG, f32)
    for s in range(4):
        for i in range(8):
            r = 8 * s + i
            for jx in range(136):
                if i <= jx <= 128 + i:
                    if jx < 128:
                        sm_[r, s * 128 + jx] = 0.0
                    else:
                        sm_[r, 512 + 8 * s + (jx - 128)] = 0.0
    sinks = np.ascontiguousarray(np.broadcast_to(np.asarray(inp["att_sinks"], f32).reshape(1, 2, 32), (128, 2, 32)))

    maps = []
    for c in range(NCORES):
        s_, half = c // 2, c % 2
        t0 = half * TP
        ss = slice(4 * c, 4 * c + 4)
        xcat = np.concatenate([inp["x_prompt"][s_, t0:t0 + TP, :], inp["x_sample"][ss].reshape(TS, D)], 0)
        xT = np.ascontiguousarray(xcat.reshape(NT, KC, 128).transpose(2, 1, 0))
        ccat = np.concatenate([inp["c_prompt"][s_:s_ + 1], inp["c_sample"][ss]], 0)
        cT = np.ascontiguousarray(ccat.reshape(5, KC, 128).transpose(2, 1, 0))
        pos = np.concatenate([np.arange(t0, t0 + TP), np.tile(PAST + np.arange(8), 4)]).astype(f32)
        ang = pos[None, :] * inv_r[:, None]
        ropeR = np.stack([np.cos(ang), np.sin(ang)], 1).astype(f32)
        lpos = np.concatenate([np.arange(128), np.tile(np.arange(8), 4)]).astype(np.float64)
        qdec = np.ascontiguousarray(np.broadcast_to((gam[:, None] ** (lpos[None, :] + 1.0))[:, None, :], (RH, 128, 160))).astype(f32)
        angA = pos[:, None] * inv_a[None, :]
        ropeA = np.zeros((128, 9, 2, 8), f32)
        for t, (c0, cn) in enumerate(TTILES):
            ropeA[:cn, t, 0, :] = np.cos(angA[c0:c0 + cn]).astype(f32)
            ropeA[:cn, t, 1, :] = np.sin(angA[c0:c0 + cn]).astype(f32)
        amask = np.stack([first if half == 0 else band, band], 1).astype(f32)
        ck = np.asarray(inp["cache_win_k"][ss], f32)
        cvv = np.asarray(inp["cache_win_v"][ss], f32)
        ckT = np.zeros((128, 4, 512), f32)
        for g in range(4):
            kt = ck[:, :, g, :].transpose(2, 0, 1).reshape(64, 512)
            ckT[0:64, g, :] = kt
            ckT[64:128, g, :] = kt
        cv = np.ascontiguousarray(cvv.reshape(4, 128, 256).transpose(1, 0, 2))
        convs = np.ascontiguousarray(
            np.asarray(inp["state_conv"][:, ss], f32).reshape(DEPTH, 4, 2, 64, 128).transpose(0, 4, 3, 1, 2)).reshape(DEPTH, 128, 64, 8)
        maps.append(dict(
            xT=xT, cT=cT, flag=np.full((128, 1), float(half), f32), wts=wts, vecl=vecl, vecg=vecg, ropeR=ropeR, qdec=qdec,
            rsc=rsc, cmask=cmask, sret=np.ascontiguousarray(inp["state_ret"][:, ss]), convs=convs, ropeA=ropeA,
            amask=amask, smask=sm_, ckT=ckT, cv=cv, cknat=np.ascontiguousarray(ck.reshape(4, 128, 256)),
            cvnat=np.ascontiguousarray(cvv.reshape(4, 128, 256)), sinks=sinks, ident=np.eye(128, dtype=f32), bada=bada, wada=wada))
    return maps


def assemble(res):
    f32 = np.float32
    y_p = np.zeros((4, 2048, D), f32)
    y_s = np.zeros((32, 8, D), f32)
    ret_p = np.zeros((N_A, 4, RH, 256, 512), f32)
    ret_s = np.zeros((N_A, 32, RH, 256, 512), f32)
    wk_p = np.zeros((4, 128, 4, 64), f32)
    wv_p = np.zeros((4, 128, 4, 64), f32)
    wk_s = np.zeros((32, 128, 4, 64), f32)
    wv_s = np.zeros((32, 128, 4, 64), f32)
    cv_p = np.zeros((DEPTH, 4, 2, 8192), f32)
    cv_s = np.zeros((DEPTH, 32, 2, 8192), f32)
    for c in range(NCORES):
        r = res[c]
        s_, half = c // 2, c % 2
        t0 = half * TP
        yt = r["yT"].transpose(2, 1, 0).reshape(NT, D)
        y_p[s_, t0:t0 + TP] = yt[:TP]
        y_s[4 * c:4 * c + 4] = yt[TP:].reshape(4, 8, D)
        ret_s[:, 4 * c:4 * c + 4] = r["rets"]
        wk_s[4 * c:4 * c + 4] = r["wks"].reshape(4, 128, 4, 64)
        wv_s[4 * c:4 * c + 4] = r["wvs"].reshape(4, 128, 4, 64)
        co = r["convo"].reshape(DEPTH, 128, 64, 5, 2).transpose(0, 3, 4, 2, 1).reshape(DEPTH, 5, 2, 8192)
        cv_s[:, 4 * c:4 * c + 4] = co[:, 1:5]
        if half == 1:
            ret_p[:, s_] = r["retp"]
            wk_p[s_] = r["wkp"].reshape(128, 4, 64)
            wv_p[s_] = r["wvp"].reshape(128, 4, 64)
            cv_p[:, s_] = co[:, 0]
    return (y_p, y_s, ret_p, ret_s, wk_p, wv_p, wk_s, wv_s, cv_p, cv_s)


_CACHE = {}


def kernel(**inputs):
    inp = {k: np.asarray(v) for k, v in inputs.items()}
    if "nc" not in _CACHE:
        _CACHE["nc"] = build_program()
    nc = _CACHE["nc"]
    maps = build_inputs(inp, build_program.specs, build_program.aplan)
    res = run_bass_kernel_spmd(nc, maps, core_ids=list(range(NCORES)))
    return assemble(res.results)
```

```python
import contextlib
import numpy as np
import concourse.bass as bass
import concourse.mybir as mybir
from concourse.bass_utils import run_bass_kernel_spmd

F32 = mybir.dt.float32
BF16 = mybir.dt.bfloat16
ALU = mybir.AluOpType
AF = mybir.ActivationFunctionType

D = 2048
KC = 16
TP = 1024
TS = 32
NT = TP + TS
NTX = NT + 2
DEPTH = 4
N_A = 2
RH = 8
PAST = 16384
EPS = 1e-6
NEG = -1e30
UE = 1074
NTILES = [(0, 512), (512, 512), (1024, 32)]
NTILES_X = [(0, 512), (512, 512), (1024, 34)]
TTILES = [(i * 128, 128) for i in range(8)] + [(1024, 32)]
GAM = [1.0 - 2.0 ** (-5.0 - h) for h in range(RH)]
NCORES = 8
WCOLS = 4096


class Buf:
    __slots__ = ("name", "w", "r")

    def __init__(self, name):
        self.name = name
        self.w = None
        self.r = {}


class Q:
    def __init__(self, fw, name, eng, is_pe=False):
        self.fw = fw
        self.name = name
        self.eng = eng
        self.is_pe = is_pe
        self.sid = fw.new_sem(name)
        self.cnt = 0
        self.seen = {}
        self.dsems = []
        self.dcnt = []
        self.drr = 0
        self.log = []
        self.tags = []

    def wait(self, ev, same_ok=False):
        if ev is None:
            return
        s, v = ev
        if s == self.sid and (same_ok or self.is_pe):
            return
        if self.seen.get(s, 0) >= v:
            return
        self.eng.wait_ge(self.fw.sems[s], v)
        self.seen[s] = v
        self.log.append(("w", s, v))


class FW:
    def __init__(self, nc):
        self.nc = nc
        self.sems = []
        self.tag = ""
        self.kind = ""
        self.pe = Q(self, "pe", nc.tensor, is_pe=True)
        self.act = Q(self, "act", nc.scalar)
        self.dve = Q(self, "dve", nc.vector)
        self.pool = Q(self, "pool", nc.gpsimd)
        self.sp = Q(self, "sp", nc.sync)
        for q, n in ((self.pool, 6), (self.sp, 10)):
            for i in range(n):
                q.dsems.append(self.new_sem(f"{q.name}_d{i}"))
                q.dcnt.append(0)
        self.cc_sid = self.new_sem("cc")
        self.cc_cnt = 0
        self.queues = [self.pe, self.act, self.dve, self.pool, self.sp]

    def new_sem(self, name):
        self.sems.append(self.nc.alloc_semaphore("s_" + name))
        return len(self.sems) - 1

    def _deps(self, q, R, W):
        for b in R:
            q.wait(b.w)
        for b in W:
            q.wait(b.w, same_ok=True)
            for s, v in b.r.items():
                q.wait((s, v), same_ok=True)

    def _commit(self, ev, R, W):
        s, v = ev
        for b in R:
            if b.r.get(s, 0) < v:
                b.r[s] = v
        for b in W:
            b.w = ev
            b.r = {}

    def op(self, q, emit, R=(), W=(), sig=True):
        self._deps(q, R, W)
        ins = emit()
        q.tags.append((self.tag, self.kind))
        ev = (q.sid, q.cnt + 1)
        if sig:
            ins.then_inc(self.sems[q.sid], 1)
            q.cnt += 1
            q.log.append(("i", q.sid, 1))
        self._commit(ev, R, W)
        return ev

    def dma(self, q, out, in_, R=(), W=()):
        self._deps(q, R, W)
        j = q.drr
        q.drr = (q.drr + 1) % len(q.dsems)
        sid = q.dsems[j]
        q.wait((sid, 16 * q.dcnt[j]))
        ins = q.eng.dma_start(out=out, in_=in_)
        q.dcnt[j] += 1
        ins.then_inc(self.sems[sid], 16)
        q.log.append(("i", sid, 16))
        ev = (sid, 16 * q.dcnt[j])
        self._commit(ev, R, W)
        return ev

    def allgather(self, in_ap, out_ap, R, W):
        q = self.pool
        self._deps(q, R, W)
        q.wait((self.cc_sid, self.cc_cnt))
        ins = self.nc.gpsimd.collective_compute(
            "AllGather", ALU.bypass, replica_groups=[[0, 1], [2, 3], [4, 5], [6, 7]],
            ins=[in_ap], outs=[out_ap])
        self.cc_cnt += 1
        ins.then_inc(self.sems[self.cc_sid], 1)
        q.log.append(("i", self.cc_sid, 1))
        ev = (self.cc_sid, self.cc_cnt)
        self._commit(ev, R, W)
        return ev

    def barrier(self):
        evs = []
        for q in self.queues:
            if q.cnt:
                evs.append((q.sid, q.cnt))
            for sid, c in zip(q.dsems, q.dcnt):
                if c:
                    evs.append((sid, 16 * c))
        if self.cc_cnt:
            evs.append((self.cc_sid, self.cc_cnt))
        for q in self.queues:
            for ev in evs:
                q.wait(ev, same_ok=True)

    def simulate(self):
        val = {}
        pos = {q.name: 0 for q in self.queues}
        progress = True
        while progress:
            progress = False
            for q in self.queues:
                p = pos[q.name]
                while p < len(q.log):
                    k, s_, v = q.log[p]
                    if k == "w":
                        if val.get(s_, 0) < v:
                            break
                    else:
                        val[s_] = val.get(s_, 0) + v
                    p += 1
                    progress = True
                pos[q.name] = p
        stuck = {q.name: (pos[q.name], len(q.log), q.log[pos[q.name]] if pos[q.name] < len(q.log) else None) for q in self.queues}
        ok = all(pos[q.name] == len(q.log) for q in self.queues)
        return ok, stuck, val

    def finish(self):
        q = self.sp
        for qq in self.queues:
            if qq.cnt:
                q.wait((qq.sid, qq.cnt), same_ok=True)
            for sid, c in zip(qq.dsems, qq.dcnt):
                if c:
                    q.wait((sid, 16 * c))
        if self.cc_cnt:
            q.wait((self.cc_sid, self.cc_cnt))


def n_weight_pieces():
    per_ret = 24 + RH * 8 + 24 + 8 * 6
    per_att = 24 + 4 * 4 + 24 + 8 * 6
    return 2 * per_ret + 16 + 2 + 2 * per_att


def build_program():
    nc = bass.Bass("TRN2", target_bir_lowering=False)
    fw = FW(nc)
    PE, ACT, DVE, POOL, SP = fw.pe, fw.act, fw.dve, fw.pool, fw.sp
    NP = n_weight_pieces()

    def din(name, shape, dt=F32):
        return nc.dram_tensor(name, list(shape), dt, kind="ExternalInput")

    def dout(name, shape, dt=F32):
        return nc.dram_tensor(name, list(shape), dt, kind="ExternalOutput")

    d_xT = din("xT", [128, KC, NT])
    d_cT = din("cT", [128, KC, 5])
    d_flag = din("flag", [128, 1])
    d_wts = din("wts", [NP, 128, WCOLS])
    d_vecl = din("vecl", [DEPTH, 128, 416])
    d_vecg = din("vecg", [128, 64])
    d_ropeR = din("ropeR", [128, 2, NT])
    d_qdec = din("qdec", [RH, 128, 160])
    d_rsc = din("rsc", [128, 64])
    d_cmask = din("cmask", [128, 160])
    d_sret = din("sret", [N_A, 4, RH, 256, 512])
    d_convs = din("convs", [DEPTH, 128, 64, 8])
    d_ropeA = din("ropeA", [128, 9, 2, 8])
    d_amask = din("amask", [128, 2, 256])
    d_smask = din("smask", [32, 544])
    d_ckT = din("ckT", [128, 4, 512])
    d_cv = din("cv", [128, 4, 256])
    d_cknat = din("cknat", [4, 128, 256])
    d_cvnat = din("cvnat", [4, 128, 256])
    d_sinks = din("sinks", [128, 2, 32])
    d_ident = din("ident", [128, 128])
    d_bada = din("bada", [128, DEPTH, 96])

    o_yT = dout("yT", [128, KC, NT])
    o_retp = dout("retp", [N_A, RH, 256, 512])
    o_rets = dout("rets", [N_A, 4, RH, 256, 512])
    o_wkp = dout("wkp", [128, 256])
    o_wvp = dout("wvp", [128, 256])
    o_wks = dout("wks", [4, 128, 256])
    o_wvs = dout("wvs", [4, 128, 256])
    o_conv = dout("convo", [DEPTH, 128, 640])

    x_sb = [[(nc.dram_tensor(f"xs_b{l}_{h}", [128, 1024], F32), nc.dram_tensor(f"xs_g{l}_{h}", [256, 1024], F32))
             for h in range(RH)] for l in range(N_A)]
    x_fb = [(nc.dram_tensor(f"xf_b{l}", [128, 32], F32), nc.dram_tensor(f"xf_g{l}", [256, 32], F32)) for l in range(DEPTH)]
    x_kv = (nc.dram_tensor("xkv_b", [128, 512], F32), nc.dram_tensor("xkv_g", [256, 512], F32))
    dbufs = {}

    def dbuf(t):
        if t.name not in dbufs:
            dbufs[t.name] = Buf(t.name)
        return dbufs[t.name]

    es = contextlib.ExitStack()

    uniq = [0]

    def sb(name, shape, dt, stack=None):
        uniq[0] += 1
        t = (stack or es).enter_context(nc.sbuf_tensor(f"s_{name}_{uniq[0]}", list(shape), dt))
        return t, Buf(name)

    x, _ = sb("x", [128, KC, NT], F32)
    xb = [[Buf(f"x{c}_{n}") for n in range(3)] for c in range(KC)]
    hT, _ = sb("hT", [128, KC, NTX], BF16)
    hb = [Buf(f"h{c}") for c in range(KC)]
    modall, _ = sb("modall", [128, 64, 33], F32)
    modallb = [Buf(f"mod_{j}") for j in range(64)]
    M = {"t": modall, "b": modallb, "g": 32}
    bada, badab = sb("bada", [128, DEPTH, 96], F32)
    NWS = 3
    wsl = [sb(f"w{i}", [128, WCOLS], BF16) for i in range(NWS)]
    vecl, veclb = sb("vecl", [128, 416], F32)
    vecg, vecgb = sb("vecg", [128, 64], F32)
    identb, identbb = sb("identb", [128, 128], BF16)
    identf, identfb = sb("identf", [128, 128], F32)
    onesb, onesbb = sb("onesb", [128, 128], BF16)
    csT, csTb = sb("csT", [128, KC, 33], BF16)
    cTs, cTsb = sb("cTs", [128, KC, 5], F32)
    flag, flagb = sb("flag", [128, 1], F32)
    xnb, xnbb = sb("xnb", [128, KC, 2], F32)
    small, smallb = sb("small", [128, 64], F32)

    psf = [(nc.alloc_psum_tensor(f"psf{i}", [128, 512], F32), Buf(f"psf{i}")) for i in range(6)]
    psb = [(nc.alloc_psum_tensor(f"psb{i}", [128, 1024], BF16), Buf(f"psb{i}")) for i in range(2)]
    rr = {"f": 0, "b": 0}

    reserved = set()

    def PF():
        while True:
            i = rr["f"]
            rr["f"] = (i + 1) % 6
            if i not in reserved:
                return psf[i]

    def bank_index(pb):
        return [b for _, b in psf].index(pb)

    def PB():
        i = rr["b"]
        rr["b"] = (i + 1) % 2
        return psb[i]

    wstate = {"n": 0}

    specs = []

    wcur = {"l": list(wsl)}

    def extra_slot(stack):
        if nc.sbuf_bytes_remaining >= 2 * WCOLS + 128:
            wcur["l"] = list(wsl) + [sb("wx", [128, WCOLS], BF16, stack)]

    def WNEXT(spec, cols=WCOLS):
        i = wstate["n"]
        wstate["n"] += 1
        specs.append(spec)
        t, b = wcur["l"][i % len(wcur["l"])]
        fw.dma(POOL, t[:, 0:cols], d_wts[i, :, 0:cols], W=[b])
        return t, b

    def mm(out, lhsT, rhs, start, stop, R, W, tp=None, sig=None):
        sg = stop if sig is None else sig
        fw.kind = 'm' if tp is None else 'p'
        if tp is None:
            fw.op(PE, lambda: nc.tensor.matmul(out, lhsT, rhs, start=start, stop=stop), R=R, W=W, sig=sg)
        else:
            fw.op(PE, lambda: nc.tensor.matmul(out, lhsT, rhs, start=start, stop=stop, tile_position=tp), R=R, W=W, sig=sg)

    def tr(out, in_, R, W, sig=True):
        kp = in_.shape[0]
        fw.kind = 't'
        fw.op(PE, lambda: nc.tensor.transpose(out, in_, identb[0:kp, 0:kp]), R=list(R) + [identbb], W=W, sig=sig)

    def act(out, in_, func, R, W, bias=0.0, scale=1.0, accum=None):
        if accum is None:
            return fw.op(ACT, lambda: nc.scalar.activation(out, in_, func, bias=bias, scale=scale), R=R, W=W)
        return fw.op(ACT, lambda: nc.scalar.activation(out, in_, func, bias=bias, scale=scale, accum_out=accum), R=R, W=W)

    def tt(q, out, a, b, op, R, W):
        return fw.op(q, lambda: q.eng.tensor_tensor(out, a, b, op), R=R, W=W)

    def ts(q, out, a, s1, s2, op0, op1, R, W):
        if s2 is None:
            return fw.op(q, lambda: q.eng.tensor_scalar(out, a, s1, None, op0), R=R, W=W)
        return fw.op(q, lambda: q.eng.tensor_scalar(out, a, s1, s2, op0, op1), R=R, W=W)

    def stt(q, out, a, s, b, op0, op1, R, W):
        return fw.op(q, lambda: q.eng.scalar_tensor_tensor(out, a, s, b, op0, op1), R=R, W=W)

    def cp(q, out, in_, R, W):
        if q is ACT:
            return act(out, in_, AF.Copy, R, W)
        return fw.op(q, lambda: q.eng.tensor_copy(out, in_), R=R, W=W)

    def ntile_of(c0):
        return 0 if c0 < 512 else (1 if c0 < 1024 else 2)

    for c in range(KC):
        fw.dma(SP, x[:, c, :], d_xT[:, c, :], W=xb[c])
    fw.dma(SP, cTs[:], d_cT[:, :, :], W=[cTsb])
    fw.dma(SP, flag[:], d_flag[:, :], W=[flagb])
    fw.dma(SP, vecg[:], d_vecg[:, :], W=[vecgb])
    fw.dma(SP, identf[:], d_ident[:, :], W=[identfb])
    fw.op(DVE, lambda: nc.vector.memset(onesb[:], 1.0), W=[onesbb])
    cp(DVE, identb[:], identf[:], [identfb], [identbb])
    act(csT[:, :, 0:1], cTs[:, :, 0:1], AF.Silu, [cTsb], [csTb])
    for s in range(4):
        for t in range(8):
            act(csT[:, :, 1 + 8 * s + t:2 + 8 * s + t], cTs[:, :, 1 + s:2 + s], AF.Silu, [cTsb], [csTb])

    class AdaJob:
        def __init__(self, wname, l, c0, nj, dst_t, dst_b, bias_fn, jmap=None):
            self.wname, self.l, self.c0, self.nj = wname, l, c0, nj
            self.dst_t, self.dst_b, self.bias_fn = dst_t, dst_b, bias_fn
            self.jmap = jmap or (lambda j: j)
            self.done = 0
            self.total = nj // 2

        def step(self):
            i = self.done
            self.done += 1
            prev_tag = fw.tag
            fw.tag = "adaln"
            wt, wb_ = WNEXT(("c", self.wname, self.l, self.c0 + i * 256))
            wv = wt[:, :].rearrange("p (k c) -> p k c", c=256)
            for mc in range(2):
                j = 2 * i + mc
                pt, pb = PF()
                for k in range(KC):
                    mm(pt[:, 0:33], wv[:, k, mc * 128:(mc + 1) * 128], csT[:, k, :], k == 0, k == KC - 1, [wb_, csTb], [pb])
                bcol, bb = self.bias_fn(j)
                jd = self.jmap(j)
                act(self.dst_t[:, jd, :], pt[:, 0:33], AF.Identity, [pb, bb], [self.dst_b[jd]], bias=bcol)
            fw.tag = prev_tag

    ada = {"jobs": [], "k": 0}

    def ada_queue(job):
        ada["jobs"].append(job)

    def ada_total():
        return sum(jb.total for jb in ada["jobs"]), sum(jb.done for jb in ada["jobs"])

    def ada_tick(frac):
        tot, done = ada_total()
        want = min(tot, int(np.ceil(tot * frac)))
        for jb in ada["jobs"]:
            while done < want and jb.done < jb.total:
                jb.step()
                done += 1

    def ada_flush():
        ada_tick(1.0)
        ada["jobs"] = []

    def ada_step(n):
        for jb in ada["jobs"]:
            while n > 0 and jb.done < jb.total:
                jb.step()
                n -= 1

    def ada_job_std(l, part, slot):
        return AdaJob("w_ada", l, part * 6144, 48, modall, modallb,
                      lambda j: (bada[:, l, part * 48 + j:part * 48 + j + 1], badab),
                      jmap=lambda j: j if j < 32 else 32 + 16 * slot + (j - 32))

    def ada_job_sa(l, part):
        return AdaJob("w_ada", l, part * 6144, 32, modall, modallb,
                      lambda j: (bada[:, l, part * 48 + j:part * 48 + j + 1], badab))

    def ada_job_gate(l, part, slot):
        return AdaJob("w_ada", l, part * 6144 + 4096, 16, modall, modallb,
                      lambda j: (bada[:, l, part * 48 + 32 + j:part * 48 + 33 + j], badab),
                      jmap=lambda j: 32 + 16 * slot + j)

    def ada_finish(job):
        while job.done < job.total:
            job.step()

    def make_amod(gt, gbuf, gcol0):
        mod, modb = M["t"], M["b"]
        for kc in range(KC):
            ts(DVE, mod[:, 16 + kc, :], mod[:, 16 + kc, :], 1.0, gt[:, gcol0 + kc:gcol0 + kc + 1], ALU.add, ALU.mult,
               [modb[16 + kc], gbuf], [modb[16 + kc]])

    def rms_stats(stk, ncols, with_nb):
        fw.tag = "norm"
        rstd, rstdb = sb("rstd", [128, NTX], F32, stk)
        sqs = [sb(f"sq{i}", [128, NTX], BF16, stk) for i in range(2)]
        tiles = NTILES_X if with_nb else NTILES
        pts = [PF() for _ in tiles]
        for kc in range(KC):
            sqt, sqb = sqs[kc % 2]
            act(sqt[:, 0:NT], x[:, kc, :], AF.Square, xb[kc], [sqb])
            if with_nb:
                act(sqt[:, NT:NTX], xnb[:, kc, :], AF.Square, [xnbb], [sqb])
            for (n0, nn), (pt, pb) in zip(tiles, pts):
                mm(pt[:, 0:nn], onesb[:, :], sqt[:, n0:n0 + nn], kc == 0, kc == KC - 1, [onesbb, sqb], [pb], sig=True)
        for (n0, nn), (pt, pb) in zip(tiles, pts):
            ts(DVE, rstd[:, n0:n0 + nn], pt[:, 0:nn], 1.0 / D, EPS, ALU.mult, ALU.add, [pb], [rstdb])
        act(rstd[:, 0:ncols], rstd[:, 0:ncols], AF.Sqrt, [rstdb], [rstdb])
        fw.op(DVE, lambda: nc.vector.reciprocal(rstd[:, 0:ncols], rstd[:, 0:ncols]), R=[rstdb], W=[rstdb])
        return rstd, rstdb

    def norm_mod(with_nb):
        stk = contextlib.ExitStack()
        mod, modb = M["t"], M["b"]
        ncols = NTX if with_nb else NT
        rstd, rstdb = rms_stats(stk, ncols, with_nb)
        ntmp = [sb(f"ntmp{i}", [128, NTX], F32, stk) for i in range(2)]
        for kc in range(KC):
            nt_, ntb = ntmp[kc % 2]
            A = mod[:, 16 + kc, :]
            Ab = modb[16 + kc]
            tt(DVE, nt_[:, 0:NT], x[:, kc, :], rstd[:, 0:NT], ALU.mult, xb[kc] + [rstdb], [ntb])
            if with_nb:
                tt(DVE, nt_[:, NT:NTX], xnb[:, kc, :], rstd[:, NT:NTX], ALU.mult, [xnbb, rstdb], [ntb])
            act(hT[:, kc, 0:TP], nt_[:, 0:TP], AF.Identity, [ntb, Ab, modb[kc]], [hb[kc]], bias=mod[:, kc, 0:1], scale=A[:, 0:1])
            if with_nb:
                act(hT[:, kc, NT:NTX], nt_[:, NT:NTX], AF.Identity, [ntb, Ab, modb[kc]], [hb[kc]], bias=mod[:, kc, 0:1], scale=A[:, 0:1])
            tt(DVE, nt_[:, TP:NT], nt_[:, TP:NT], A[:, 1:33], ALU.mult, [ntb, Ab], [ntb])
            tt(DVE, hT[:, kc, TP:NT], nt_[:, TP:NT], mod[:, kc, 1:33], ALU.add, [ntb, modb[kc]], [hb[kc]])
            if kc % 4 == 1 and M["g"] is not None:
                ada_step(1)
        fw.barrier()
        stk.close()

    def resid_add(pt, pb, m, n0, nn, gj):
        mod, modb = M["t"], M["b"]
        nt_i = ntile_of(n0)
        if n0 < TP:
            stt(DVE, x[:, m, n0:n0 + nn], pt[:, 0:nn], mod[:, gj, 0:1], x[:, m, n0:n0 + nn], ALU.mult, ALU.add,
                [pb, modb[gj], xb[m][nt_i]], [xb[m][nt_i]])
        else:
            tt(DVE, small[:, 0:32], pt[:, 0:32], mod[:, gj, 1:33], ALU.mult, [pb, modb[gj]], [smallb])
            tt(DVE, x[:, m, TP:NT], x[:, m, TP:NT], small[:, 0:32], ALU.add, [smallb, xb[m][2]], [xb[m][2]])

    def proj_rows(pieces, kchunks, src, srcb):
        for m in range(KC):
            pts = [PF() for _ in NTILES]
            for kc in range(kchunks):
                wt, wb_ = pieces[kc // 2]
                wv = wt[:, :].rearrange("p (k c) -> p k c", c=2048)
                for (n0, nn), (pt, pb) in zip(NTILES, pts):
                    mm(pt[:, 0:nn], wv[:, kc % 2, m * 128:(m + 1) * 128], src[:, kc, n0:n0 + nn], kc == 0, kc == kchunks - 1,
                       [wb_, srcb], [pb])
            for (n0, nn), (pt, pb) in zip(NTILES, pts):
                resid_add(pt, pb, m, n0, nn, M["g"] + m)

    def ffn(l):
        bt, gt = x_fb[l]
        fw.dma(SP, bt.ap().rearrange("p (k t) -> p k t", t=2), x[:, :, TP - 2:TP], R=[xb[c][1] for c in range(KC)], W=[dbuf(bt)])
        fw.allgather(bt.ap().opt(), gt.ap().opt(), R=[dbuf(bt)], W=[dbuf(gt)])
        fw.dma(SP, xnb[:], gt.ap()[0:128, :].rearrange("p (k t) -> p k t", t=2), R=[dbuf(gt)], W=[xnbb])
        make_amod(vecl, veclb, 112)
        norm_mod(True)
        st = contextlib.ExitStack()
        uext = [sb(f"uext{i}", [128, UE], F32, st) for i in range(2)]
        acc = [sb(f"acc{i}", [128, 1064], F32, st) for i in range(2)]
        sa, sab = sb("sa", [128, 1064], F32, st)
        zg = [sb(f"zg{i}", [128, 4, NT], BF16, st) for i in range(2)]
        cst, cstb = sb("cst", [128, 64, 8], F32, st)
        convo, convob = sb("convo", [128, 64, 10], F32, st)
        extra_slot(st)
        fw.dma(SP, cst[:], d_convs[l, :, :, :], W=[cstb])
        for i in range(2):
            fw.op(DVE, lambda i=i: nc.vector.memset(uext[i][0][:], 0.0), W=[uext[i][1]])
        def down(gi):
            zt_, zb_ = zg[gi % 2]
            fw.tag = "ffn.down"
            pieces = [WNEXT(("r", "ffn_w_down", l, gi * 512 + r * 256)) for r in range(2)]
            proj_rows(pieces, 4, zt_, zb_)

        for grp in range(8):
            zt, zb = zg[grp % 2]
            for jj in range(4):
                j = grp * 4 + jj
                if jj == 1 and grp > 0:
                    down(grp - 1)
                fw.tag = "ffn.up"
                ada_tick((j + 1) / 30.0)
                wt, wb_ = WNEXT(("c2", "ffn_w_up", l, j))
                wv = wt[:, :].rearrange("p (k c) -> p k c", c=256)
                for half in range(2):
                    ut, ub = uext[half]
                    at, ab = acc[half]
                    ch = half * 32 + j
                    w0 = vecl[:, 128 + ch:129 + ch]
                    w1 = vecl[:, 192 + ch:193 + ch]
                    w2 = vecl[:, 256 + ch:257 + ch]
                    cb = vecl[:, 320 + ch:321 + ch]
                    cp(ACT, ut[:, 1026:1066].rearrange("p (s t) -> p s t", t=10)[:, :, 0:2],
                       cst[:, ch, :].rearrange("p (s t) -> p s t", t=2), [cstb], [ub])
                    for (n0, nn) in NTILES_X:
                        pt, pb = PF()
                        for k in range(KC):
                            mm(pt[:, 0:nn], wv[:, k, half * 128:(half + 1) * 128], hT[:, k, n0:n0 + nn], k == 0, k == KC - 1,
                               [wb_, hb[k]], [pb])
                        if n0 < TP:
                            cp(ACT, ut[:, 2 + n0:2 + n0 + nn], pt[:, 0:nn], [pb], [ub])
                        else:
                            cp(ACT, ut[:, 1026:1066].rearrange("p (s t) -> p s t", t=10)[:, :, 2:10],
                               pt[:, 0:32].rearrange("p (s t) -> p s t", t=8), [pb], [ub])
                            ts(DVE, ut[:, 0:2], pt[:, 32:34], flag[:, 0:1], None, ALU.mult, None, [pb, flagb], [ub])
                    ts(DVE, at[:, :], ut[:, 2:1066], w2, cb, ALU.mult, ALU.add, [ub, veclb], [ab])
                    stt(DVE, at[:, :], ut[:, 1:1065], w1, at[:, :], ALU.mult, ALU.add, [ub, veclb, ab], [ab])
                    stt(DVE, at[:, :], ut[:, 0:1064], w0, at[:, :], ALU.mult, ALU.add, [ub, veclb, ab], [ab])
                    cp(ACT, convo[:, ch, :].rearrange("p (s t) -> p s t", t=2),
                       ut[:, 1024:1074].rearrange("p (s t) -> p s t", t=10)[:, :, 0:2], [ub], [convob])
                a_t, a_b = acc[0]
                b_t, b_b = acc[1]
                act(sa[:, :], a_t[:, :], AF.Silu, [a_b], [sab])
                tt(DVE, zt[:, jj, 0:TP], sa[:, 0:TP], b_t[:, 0:TP], ALU.mult, [sab, b_b], [zb])
                tt(DVE, zt[:, jj, TP:NT].rearrange("p (s t) -> p s t", t=8),
                   sa[:, 1024:1064].rearrange("p (s t) -> p s t", t=10)[:, :, 2:10],
                   b_t[:, 1024:1064].rearrange("p (s t) -> p s t", t=10)[:, :, 2:10], ALU.mult, [sab, b_b], [zb])
        down(7)
        ada_flush()
        fw.dma(SP, o_conv[l, :, :], convo[:, :, :].rearrange("p c t -> p (c t)"), R=[convob])
        fw.barrier()
        wcur["l"] = list(wsl)
        st.close()

    def retention(l):
        make_amod(vecl, veclb, 96)
        norm_mod(False)
        st = contextlib.ExitStack()
        rope, ropeb = sb("rope", [128, 2, NT], F32, st)
        cmask, cmaskb = sb("cmask", [128, 160], F32, st)
        rsc, rscb = sb("rsc", [128, 64], F32, st)
        qd, qdb = sb("qd", [128, 160], F32, st)
        q2T, q2b = sb("q2T", [128, 2, NT], BF16, st)
        kT, kTb = sb("kT", [128, 2, NT], BF16, st)
        ktok, ktokb = sb("ktok", [128, 9, 256], BF16, st)
        ktf = [sb(f"ktf{i}", [128, 256], BF16, st) for i in range(2)]
        v, _ = sb("v", [128, 9, 512], BF16, st)
        vbs = [Buf(f"v{t}") for t in range(9)]
        gnsg, gnsgb = sb("gnsg", [128, 9, 512], BF16, st)
        S, _ = sb("S", [128, 2, 512], F32, st)
        Sb, _ = sb("Sb", [128, 2, 512], BF16, st)
        S2, _ = sb("S2", [128, 2, 512], F32, st)
        Sbuf = [Buf("S_0"), Buf("S_1")]
        Sbb = [Buf("Sb_0"), Buf("Sb_1")]
        S2buf = [Buf("S2_0"), Buf("S2_1")]
        gT, gTb = sb("gT", [128, 4, NT], BF16, st)
        ra = [sb(f"ra{i}", [128, 512], F32, st) for i in range(2)]
        scr = [dict(PT=sb(f"PT{i}", [128, 128], BF16, st), gated=sb(f"gated{i}", [128, 512], BF16, st),
                    st6=sb(f"st6{i}", [128, 8], F32, st), tn=ra[i]) for i in range(2)]
        q2m, q2mb = sb("q2m", [128, 2, 32], BF16, st)
        ktm, ktmb = sb("ktm", [32, 256], BF16, st)
        fw.dma(SP, rope[:], d_ropeR[:, :, :], W=[ropeb])
        fw.dma(SP, cmask[:], d_cmask[:, :], W=[cmaskb])
        fw.dma(SP, rsc[:], d_rsc[:, :], W=[rscb])
        cosT = rope[:, 0, :]
        sinT = rope[:, 1, :]

        def rope_proj(wt, wb_, out_t, out_b, scale_q):
            wv = wt[:, :].rearrange("p (k c) -> p k c", c=256)
            for (n0, nn) in NTILES:
                p1, p1b = PF()
                p2, p2b = PF()
                for mc, (pt, pb) in enumerate(((p1, p1b), (p2, p2b))):
                    for k in range(KC):
                        mm(pt[:, 0:nn], wv[:, k, mc * 128:(mc + 1) * 128], hT[:, k, n0:n0 + nn], k == 0, k == KC - 1,
                           [wb_, hb[k]], [pb])
                a_, ab_ = ra[0]
                b_, bb_ = ra[1]
                cs = cosT[:, n0:n0 + nn]
                sn = sinT[:, n0:n0 + nn]
                if nn == 512:
                    qdv = qd[:, 0:128].unsqueeze(1).to_broadcast([128, 4, 128])
                    av = a_[:, 0:512].rearrange("p (c t) -> p c t", t=128)
                else:
                    qdv = qd[:, 128:160]
                    av = a_[:, 0:32]
                for half in range(2):
                    pa, pab = (p1, p1b) if half == 0 else (p2, p2b)
                    pc, pcb = (p2, p2b) if half == 0 else (p1, p1b)
                    tt(DVE, a_[:, 0:nn], pa[:, 0:nn], cs, ALU.mult, [pab, ropeb], [ab_])
                    tt(DVE, b_[:, 0:nn], pc[:, 0:nn], sn, ALU.mult, [pcb, ropeb], [bb_])
                    op = ALU.subtract if half == 0 else ALU.add
                    if scale_q:
                        tt(DVE, a_[:, 0:nn], a_[:, 0:nn], b_[:, 0:nn], op, [ab_, bb_], [ab_])
                        ov = out_t[:, half, n0:n0 + nn]
                        if nn == 512:
                            ov = ov.rearrange("p (c t) -> p c t", t=128)
                        tt(DVE, ov, av, qdv, ALU.mult, [ab_, qdb], [out_b])
                    else:
                        tt(DVE, out_t[:, half, n0:n0 + nn], a_[:, 0:nn], b_[:, 0:nn], op, [ab_, bb_], [out_b])

        def tok_proj(evac, c0w):
            for half in range(2):
                wt, wb_ = WNEXT(("c", "ret_w_in", l, c0w + half * 256))
                wv = wt[:, :].rearrange("p (k c) -> p k c", c=256)
                for t, (c0, cn) in enumerate(TTILES):
                    pt, pb = PF()
                    for k in range(KC):
                        mm(pt[0:cn, 0:256], hT[:, k, c0:c0 + cn], wv[:, k, :], k == 0, k == KC - 1, [wb_, hb[k]], [pb])
                    evac(t, cn, half, pt, pb)

        for h in range(RH):
            gam = GAM[h]
            kdec_c = rsc[:, h:h + 1]
            ksc_c = rsc[:, 8 + h:9 + h]
            kdec_s = rsc[0:32, 16 + h:17 + h]
            ksc_s = rsc[0:32, 24 + h:25 + h]
            fw.dma(SP, qd[:], d_qdec[h, :, :], W=[qdb])
            sbufs = [(S2, S2buf), (S, Sbuf), (S2, S2buf), (S, Sbuf)]
            fw.dma(SP, S2[:], d_sret[l, 0, h, :, :].rearrange("(c p) e -> p c e", p=128), W=S2buf)
            fw.tag = "ret.K"
            ada_tick((h + 0.3) / 7.5)
            wt, wb_ = WNEXT(("c", "ret_w_in", l, 2048 + h * 256))
            rope_proj(wt, wb_, kT, kTb, False)
            fw.tag = "ret.ktr"
            for t, (c0, cn) in enumerate(TTILES):
                pt, pb = PB()
                for dc in range(2):
                    tr(pt[0:cn, dc * 128:(dc + 1) * 128], kT[:, dc, c0:c0 + cn], [kTb], [pb], sig=(dc == 1))
                act(ktok[0:cn, t, :], pt[0:cn, 0:256], AF.Identity, [pb, rscb], [ktokb], scale=(kdec_c if cn == 128 else kdec_s))
            fw.tag = "ret.V"
            pS = [PF(), PF()]
            for _, pbb in pS:
                reserved.add(bank_index(pbb))

            def evac_v(t, cn, half, pt, pb):
                cp(ACT, v[0:cn, t, half * 256:(half + 1) * 256], pt[0:cn, 0:256], [pb], [vbs[t]])
                if half == 1 and t >= 1:
                    tl = t - 1
                    ptag = fw.tag
                    fw.tag = "ret.Sloc"
                    kf, kfb = ktf[tl % 2]
                    ts(DVE, kf[:, :], ktok[:, tl, :], float(gam ** (128 * (7 - tl))), None, ALU.mult, None, [ktokb], [kfb])
                    for dc in range(2):
                        mm(pS[dc][0][:, :], kf[:, dc * 128:(dc + 1) * 128], v[:, tl, :], tl == 0, tl == 7, [kfb, vbs[tl]], [pS[dc][1]], sig=True)
                    fw.tag = ptag

            tok_proj(evac_v, 4096 + h * 512)
            ada_tick((h + 0.6) / 7.5)
            fw.tag = "ret.Sloc"
            for dc in range(2):
                cp(ACT, S[:, dc, :], pS[dc][0][:, :], [pS[dc][1]], [Sbuf[dc]])
            for _, pbb in pS:
                reserved.discard(bank_index(pbb))
            bt, gt = x_sb[l][h]
            fw.dma(SP, bt.ap().rearrange("p (c e) -> p c e", c=2), S[:], R=Sbuf, W=[dbuf(bt)])
            fw.allgather(bt.ap().opt(), gt.ap().opt(), R=[dbuf(bt)], W=[dbuf(gt)])
            fw.dma(SP, S[:], gt.ap()[0:128, :].rearrange("p (c e) -> p c e", c=2), R=[dbuf(gt)], W=Sbuf)
            for dc in range(2):
                ts(DVE, S[:, dc, :], S[:, dc, :], flag[:, 0:1], None, ALU.mult, None, [Sbuf[dc], flagb], [Sbuf[dc]])
                cp(ACT, Sb[:, dc, :], S[:, dc, :], [Sbuf[dc]], [Sbb[dc]])
            fw.tag = "ret.Q"
            wt, wb_ = WNEXT(("c", "ret_w_in", l, h * 256))
            rope_proj(wt, wb_, q2T, q2b, True)
            fw.tag = "ret.G"
            tok_proj(lambda t, cn, half, pt, pb: act(gnsg[0:cn, t, half * 256:(half + 1) * 256], pt[0:cn, 0:256], AF.Silu, [pb], [gnsgb]),
                     8192 + h * 512)
            ada_tick((h + 1.0) / 7.5)

            def gn_chain(t, cn, po, pob, sc_):
                (st6_, st6b_), (tn_, tnb_), (gated_, gatedb_) = sc_["st6"], sc_["tn"], sc_["gated"]
                fw.op(DVE, lambda: nc.vector.bn_stats(st6_[0:cn, 0:6], po[0:cn, :]), R=[pob], W=[st6b_])
                fw.op(DVE, lambda: nc.vector.bn_aggr(st6_[0:cn, 6:8], st6_[0:cn, 0:6]), R=[st6b_], W=[st6b_])
                ts(DVE, st6_[0:cn, 7:8], st6_[0:cn, 7:8], EPS, None, ALU.add, None, [st6b_], [st6b_])
                act(st6_[0:cn, 7:8], st6_[0:cn, 7:8], AF.Sqrt, [st6b_], [st6b_])
                fw.op(DVE, lambda: nc.vector.reciprocal(st6_[0:cn, 7:8], st6_[0:cn, 7:8]), R=[st6b_], W=[st6b_])
                ts(DVE, tn_[0:cn, :], po[0:cn, :], st6_[0:cn, 6:7], st6_[0:cn, 7:8], ALU.subtract, ALU.mult, [pob, st6b_], [tnb_])
                tt(DVE, gated_[0:cn, :], tn_[0:cn, :], gnsg[0:cn, t, :], ALU.mult, [tnb_, gnsgb], [gatedb_])

            def stage_b(t, c0, cn, sc_):
                gated_, gatedb_ = sc_["gated"]
                pt, pb = PB()
                for ec in range(4):
                    tr(pt[:, ec * 128:ec * 128 + cn], gated_[0:cn, ec * 128:(ec + 1) * 128], [gatedb_], [pb], sig=(ec == 3))
                for ec in range(4):
                    gcol = vecl[:, 384 + h * 4 + ec:385 + h * 4 + ec]
                    act(gT[:, ec, c0:c0 + cn], pt[:, ec * 128:ec * 128 + cn], AF.Identity, [pb, veclb], [gTb], scale=gcol)

            fw.tag = "ret.scan"
            pend = None
            for t in range(8):
                c0 = t * 128
                sc_ = scr[t % 2]
                PT_, PTb_ = sc_["PT"]
                psc, pscb = PF()
                pS = [PF(), PF()]
                po, pob = PF()
                for dc in range(2):
                    mm(psc[:, 0:128], kT[:, dc, c0:c0 + 128], q2T[:, dc, c0:c0 + 128], dc == 0, dc == 1, [kTb, q2b], [pscb])
                stt(DVE, PT_[:, :], psc[:, 0:128], ksc_c, cmask[:, 0:128], ALU.mult, ALU.mult, [pscb, rscb, cmaskb], [PTb_])
                for dc in range(2):
                    mm(pS[dc][0][:, :], ktok[:, t, dc * 128:(dc + 1) * 128], v[:, t, :], True, True, [ktokb, vbs[t]], [pS[dc][1]])
                mm(po[:, :], PT_[:, :], v[:, t, :], True, False, [PTb_, vbs[t]], [pob])
                for dc in range(2):
                    mm(po[:, :], q2T[:, dc, c0:c0 + 128], Sb[:, dc, :], False, dc == 1, [q2b, Sbb[dc]], [pob])
                for dc in range(2):
                    stt(DVE, S[:, dc, :], S[:, dc, :], float(gam ** 128), pS[dc][0][:, :], ALU.mult, ALU.add, [Sbuf[dc], pS[dc][1]], [Sbuf[dc]])
                    cp(ACT, Sb[:, dc, :], S[:, dc, :], [Sbuf[dc]], [Sbb[dc]])
                gn_chain(t, 128, po, pob, sc_)
                if pend is not None:
                    stage_b(*pend)
                pend = (t, c0, 128, sc_)
            fw.dma(SP, o_retp[l, h, :, :].rearrange("(c p) e -> p c e", p=128), S[:], R=Sbuf)

            fw.tag = "ret.samp"
            c0 = TP
            sc_ = scr[0]
            PT_, PTb_ = sc_["PT"]
            psc, pscb = PF()
            po, pob = PF()
            pS = [PF(), PF()]
            for dc in range(2):
                mm(psc[0:32, 0:32], kT[:, dc, c0:c0 + 32], q2T[:, dc, c0:c0 + 32], dc == 0, dc == 1, [kTb, q2b], [pscb])
            stt(DVE, PT_[0:32, 0:32], psc[0:32, 0:32], ksc_s, cmask[0:32, 128:160], ALU.mult, ALU.mult, [pscb, rscb, cmaskb], [PTb_])
            mm(po[0:32, :], PT_[0:32, 0:32], v[0:32, 8, :], True, False, [PTb_, vbs[8]], [pob])
            stage_b(*pend)
            for s in range(4):
                Sx, Sxb = sbufs[s]
                if s + 1 < 4:
                    Sn, Snb = sbufs[s + 1]
                    fw.dma(SP, Sn[:], d_sret[l, s + 1, h, :, :].rearrange("(c p) e -> p c e", p=128), W=Snb)
                for dc in range(2):
                    cp(ACT, Sb[:, dc, :], Sx[:, dc, :], [Sxb[dc]], [Sbb[dc]])
                fw.op(DVE, lambda: nc.vector.memset(q2m[:], 0.0), W=[q2mb])
                cp(DVE, q2m[:, :, 8 * s:8 * s + 8], q2T[:, :, c0 + 8 * s:c0 + 8 * s + 8], [q2b], [q2mb])
                for dc in range(2):
                    mm(po[0:32, :], q2m[:, dc, :], Sb[:, dc, :], False, (s == 3 and dc == 1), [q2mb, Sbb[dc]], [pob])
                ts(DVE, ktm[:, :], ktok[0:32, 8, :], rsc[0:32, 32 + s:33 + s], None, ALU.mult, None, [ktokb, rscb], [ktmb])
                for dc in range(2):
                    mm(pS[dc][0][:, :], ktm[:, dc * 128:(dc + 1) * 128], v[0:32, 8, :], True, True, [ktmb, vbs[8]], [pS[dc][1]])
                for dc in range(2):
                    stt(DVE, Sx[:, dc, :], Sx[:, dc, :], float(gam ** 8), pS[dc][0][:, :], ALU.mult, ALU.add, [Sxb[dc], pS[dc][1]], [Sxb[dc]])
                fw.dma(SP, o_rets[l, s, h, :, :].rearrange("(c p) e -> p c e", p=128), Sx[:], R=Sxb)
            gn_chain(8, 32, po, pob, sc_)
            stage_b(8, c0, 32, sc_)
            fw.tag = "ret.out"
            if h == 0:
                ada_finish(att["gatejob"])
            pieces = [WNEXT(("r", "ret_w_out", l, h * 512 + r * 256)) for r in range(2)]
            proj_rows(pieces, 4, gT, gTb)
        ada_flush()
        fw.barrier()
        st.close()

    att = {}

    def rope_apply(buf_t, buf_b, nh, t, cn, ropeA, ropeAb, rt):
        sv = buf_t[0:cn, 0:nh * 64].rearrange("p (h d) -> p h d", d=64)
        x1 = sv[:, :, 0:8]
        x2 = sv[:, :, 8:16]
        cs = ropeA[0:cn, t, 0, :].unsqueeze(1).to_broadcast([cn, nh, 8])
        sn = ropeA[0:cn, t, 1, :].unsqueeze(1).to_broadcast([cn, nh, 8])
        (r0, r0b), (r1, r1b) = rt
        a = r0[0:cn, 0, 0:nh, :]
        b = r0[0:cn, 1, 0:nh, :]
        c = r1[0:cn, 0, 0:nh, :]
        d = r1[0:cn, 1, 0:nh, :]
        tt(DVE, a, x1, cs, ALU.mult, [buf_b, ropeAb], [r0b])
        tt(DVE, b, x2, sn, ALU.mult, [buf_b, ropeAb], [r0b])
        tt(DVE, c, x2, cs, ALU.mult, [buf_b, ropeAb], [r1b])
        tt(DVE, d, x1, sn, ALU.mult, [buf_b, ropeAb], [r1b])
        tt(DVE, x1, a, b, ALU.subtract, [r0b], [buf_b])
        tt(DVE, x2, c, d, ALU.add, [r1b], [buf_b])

    def kv_stage(stk):
        kTa, kTab, vA, vAb = att["kTa"], att["kTab"], att["vA"], att["vAb"]
        saveM = dict(M)
        M["t"], M["b"] = att["kvmod"]
        M["g"] = None
        make_amod(vecg, vecgb, 0)
        norm_mod(False)
        M.update(saveM)
        st = contextlib.ExitStack()
        ropeA, ropeAb = sb("ropeA", [128, 9, 2, 8], F32, st)
        kf, kfb = sb("kf", [128, 512], F32, st)
        kr, krb = sb("kr", [128, 2, 256], BF16, st)
        rt = [sb(f"rt{i}", [128, 2, 8, 8], F32, st) for i in range(2)]
        fw.dma(SP, ropeA[:], d_ropeA[:, :, :, :], W=[ropeAb])
        fw.dma(POOL, kTa[:, :, 128 + TP:128 + TP + 512], d_ckT[:, :, :], W=[kTab])
        for s in range(4):
            fw.dma(POOL, vA[:, 10 + s, :], d_cv[:, s, :], W=[vAb])
        fw.tag = "kv"
        wk_ = WNEXT(("c", "w_kv", None, 0))
        wv_ = WNEXT(("c", "w_kv", None, 256))

        def finish_tile(tile_idx, keycol0, cn):
            cp(ACT, vA[0:cn, tile_idx, :], kf[0:cn, 256:512], [kfb], [vAb])
            for g in range(4):
                for dup in range(2):
                    o0 = (g % 2) * 128 + dup * 64
                    cp(DVE, kr[0:cn, g // 2, o0:o0 + 64], kf[0:cn, g * 64:(g + 1) * 64], [kfb], [krb])
            pt, pb = PB()
            for g in range(4):
                tr(pt[:, g * 128:g * 128 + cn], kr[0:cn, g // 2, (g % 2) * 128:(g % 2) * 128 + 128], [krb], [pb], sig=(g == 3))
            cp(ACT, kTa[:, :, keycol0:keycol0 + cn], pt[:, 0:512].rearrange("p (g c) -> p g c", c=128)[:, :, 0:cn], [pb], [kTab])

        bt, gt = x_kv
        for t, (c0, cn) in enumerate(TTILES):
            for half, (wt, wb_) in enumerate((wk_, wv_)):
                wv2 = wt[:, :].rearrange("p (k c) -> p k c", c=256)
                pt, pb = PF()
                for k in range(KC):
                    mm(pt[0:cn, 0:256], hT[:, k, c0:c0 + cn], wv2[:, k, :], k == 0, k == KC - 1, [wb_, hb[k]], [pb])
                cp(ACT, kf[0:cn, half * 256:(half + 1) * 256], pt[0:cn, 0:256], [pb], [kfb])
            rope_apply(kf, kfb, 4, t, cn, ropeA, ropeAb, rt)
            if t < 8:
                finish_tile(1 + t, 128 + c0, 128)
            else:
                finish_tile(9, 128 + TP + 512, 32)
            if t == 7:
                fw.dma(SP, o_wkp[:, :], kf[:, 0:256], R=[kfb])
                fw.dma(SP, o_wvp[:, :], kf[:, 256:512], R=[kfb])
                fw.dma(SP, bt.ap(), kf[:, :], R=[kfb], W=[dbuf(bt)])
                fw.allgather(bt.ap().opt(), gt.ap().opt(), R=[dbuf(bt)], W=[dbuf(gt)])
            if t == 8:
                for s in range(4):
                    fw.dma(SP, o_wks[s, 120:128, :], kf[8 * s:8 * s + 8, 0:256], R=[kfb])
                    fw.dma(SP, o_wvs[s, 120:128, :], kf[8 * s:8 * s + 8, 256:512], R=[kfb])
                    fw.dma(SP, o_wks[s, 0:120, :], d_cknat[s, 8:128, :])
                    fw.dma(SP, o_wvs[s, 0:120, :], d_cvnat[s, 8:128, :])
        fw.dma(SP, kf[:, :], gt.ap()[0:128, :], R=[dbuf(gt)], W=[kfb])
        finish_tile(0, 0, 128)
        fw.barrier()
        st.close()

    def attention(l, j):
        make_amod(vecl, veclb, 96)
        norm_mod(False)
        st = contextlib.ExitStack()
        kTa, kTab, vA, vAb = att["kTa"], att["kTab"], att["vA"], att["vAb"]
        ropeA, ropeAb = sb("ropeA", [128, 9, 2, 8], F32, st)
        rt = [sb(f"rt{i}", [128, 2, 8, 8], F32, st) for i in range(2)]
        qf, qfb = sb("qf", [128, 512], F32, st)
        qbf, qbfb = sb("qbf", [128, 512], BF16, st)
        qT, qTb = sb("qT", [128, 4, NT], BF16, st)
        oT, oTb = sb("oT", [128, 4, NT], BF16, st)
        amask, amaskb = sb("amask", [128, 2, 256], F32, st)
        smask, smaskb = sb("smask", [32, 544], F32, st)
        amk, amkb = sb("amk", [128, 2, 256], BF16, st)
        smk, smkb = sb("smk", [32, 544], BF16, st)
        sinks, sinksb = sb("sinks", [128, 2, 32], F32, st)
        pes = [sb(f"pe{i}", [128, 4, 256], BF16, st) for i in range(2)]
        pTs = [sb(f"pT{i}", [128, 8, 128], BF16, st) for i in range(2)]
        sms = [sb(f"sm{i}", [128, 32], F32, st) for i in range(2)]
        pss = [sb(f"pss{i}", [32, 544], BF16, st) for i in range(2)]
        fw.dma(SP, ropeA[:], d_ropeA[:, :, :, :], W=[ropeAb])
        fw.dma(SP, amask[:], d_amask[:, :, :], W=[amaskb])
        fw.dma(SP, smask[:], d_smask[:, :], W=[smaskb])
        fw.dma(SP, sinks[:], d_sinks[:, :, :], W=[sinksb])
        extra_slot(st)
        cp(DVE, amk[:], amask[:], [amaskb], [amkb])
        cp(DVE, smk[:], smask[:], [smaskb], [smkb])
        ctr = {"i": 0}

        for g in range(4):
            fw.tag = "att.q"
            ada_tick((g + 0.5) / 3.6)
            wq = [WNEXT(("c", "att_w_q", j, g * 512 + half * 256)) for half in range(2)]
            for t, (c0, cn) in enumerate(TTILES):
                for half in range(2):
                    wt, wb_ = wq[half]
                    wv2 = wt[:, :].rearrange("p (k c) -> p k c", c=256)
                    pt, pb = PF()
                    for k in range(KC):
                        mm(pt[0:cn, 0:256], hT[:, k, c0:c0 + cn], wv2[:, k, :], k == 0, k == KC - 1, [wb_, hb[k]], [pb])
                    cp(ACT, qf[0:cn, half * 256:(half + 1) * 256], pt[0:cn, 0:256], [pb], [qfb])
                rope_apply(qf, qfb, 8, t, cn, ropeA, ropeAb, rt)
                cp(ACT, qbf[0:cn, :], qf[0:cn, :], [qfb], [qbfb])
                pt, pb = PB()
                for cc in range(4):
                    tr(pt[:, cc * 128:cc * 128 + cn], qbf[0:cn, cc * 128:(cc + 1) * 128], [qbfb], [pb], sig=(cc == 3))
                cp(ACT, qT[:, :, c0:c0 + cn], pt[:, 0:512].rearrange("p (c t) -> p c t", t=128)[:, :, 0:cn], [pb], [qTb])
            def stage_a(t, bh):
                c0 = t * 128
                wm = 0 if t == 0 else 1
                ix = ctr["i"] % 2
                ctr["i"] += 1
                pe_, peb = pes[ix]
                sm, smb = sms[ix]
                banks = [PF(), PF()]
                for i, (ps, psb_) in enumerate(banks):
                    hp = 2 * bh + i
                    for hh in range(2):
                        base = hh * 64
                        mm(ps[:, hh * 256:(hh + 1) * 256], qT[base:base + 64, hp, c0:c0 + 128], kTa[base:base + 64, g, c0:c0 + 256],
                           True, False, [qTb, kTab], [psb_])
                        mm(ps[:, hh * 256:(hh + 1) * 256], identb[:, :], amk[:, wm, :], False, True, [identbb, amkb], [psb_])
                h0 = g * 8 + 4 * bh
                sk4 = sinks[:, j, h0:h0 + 4]
                for i, (ps, psb_) in enumerate(banks):
                    fw.op(DVE, lambda ps=ps, i=i: nc.vector.tensor_reduce(
                        sm[:, 2 * i:2 * i + 2], ps[:, 0:512].rearrange("p (h k) -> p h k", k=256), mybir.AxisListType.X, ALU.max),
                        R=[psb_], W=[smb])
                stt(DVE, sm[:, 4:8], sm[:, 0:4], 0.125, sk4, ALU.mult, ALU.max, [smb, sinksb], [smb])
                ts(DVE, sm[:, 8:12], sm[:, 4:8], -1.0, None, ALU.mult, None, [smb], [smb])
                tt(DVE, sm[:, 16:20], sk4, sm[:, 4:8], ALU.subtract, [sinksb, smb], [smb])
                for i, (ps, psb_) in enumerate(banks):
                    for hh in range(2):
                        hd = 2 * i + hh
                        act(pe_[:, hd, :], ps[:, hh * 256:(hh + 1) * 256], AF.Exp, [psb_, smb], [peb, smb],
                            bias=sm[:, 8 + hd:9 + hd], scale=0.125, accum=sm[:, 12 + hd:13 + hd])
                act(sm[:, 20:24], sm[:, 16:20], AF.Exp, [smb], [smb])
                tt(DVE, sm[:, 24:28], sm[:, 12:16], sm[:, 20:24], ALU.add, [smb], [smb])
                fw.op(DVE, lambda: nc.vector.reciprocal(sm[:, 28:32], sm[:, 24:28]), R=[smb], W=[smb])
                tt(DVE, pe_[:, :, :], pe_[:, :, :], sm[:, 28:32].unsqueeze(2).to_broadcast([128, 4, 256]), ALU.mult, [peb, smb], [peb])
                return (t, bh, ix)

            def stage_bb(t, bh, ix):
                c0 = t * 128
                pe_, peb = pes[ix]
                pT, pTb = pTs[ix]
                ptT, ptTb = PB()
                for hd in range(4):
                    for kb in range(2):
                        o0 = (hd * 2 + kb) * 128
                        tr(ptT[:, o0:o0 + 128], pe_[:, hd, kb * 128:(kb + 1) * 128], [peb], [ptTb], sig=(hd == 3 and kb == 1))
                cp(ACT, pT[:, :, :], ptT[:, 0:1024].rearrange("p (a q) -> p a q", q=128), [ptTb], [pTb])
                for i in range(2):
                    hp = 2 * bh + i
                    po, pob = PF()
                    for hh in range(2):
                        hd = 2 * i + hh
                        for kb in range(2):
                            mm(po[hh * 64:(hh + 1) * 64, 0:128], vA[:, t + kb, g * 64:(g + 1) * 64], pT[:, hd * 2 + kb, :], kb == 0, kb == 1,
                               [vAb, pTb], [pob], tp=(0, hh * 64))
                    cp(ACT, oT[:, hp, c0:c0 + 128], po[:, 0:128], [pob], [oTb])

            fw.tag = "att.blk"
            pend = None
            for t in range(8):
                for bh in range(2):
                    cur = stage_a(t, bh)
                    if pend is not None:
                        stage_bb(*pend)
                    pend = cur
            stage_bb(*pend)
            ada_tick((g + 1.0) / 3.6)
            fw.tag = "att.samp"
            c0 = TP
            K0 = 128 + TP

            def samp_a(hp, hh):
                ix = ctr["i"] % 2
                ctr["i"] += 1
                pn, pnb = pss[ix]
                sm, smb = sms[ix]
                hq = g * 8 + hp * 2 + hh
                base = hh * 64
                ps, psb_ = PF()
                ps2, ps2b = PF()
                mm(ps[0:32, 0:512], qT[base:base + 64, hp, c0:c0 + 32], kTa[base:base + 64, g, K0:K0 + 512], True, False, [qTb, kTab], [psb_])
                mm(ps[0:32, 0:512], identb[0:32, 0:32], smk[0:32, 0:512], False, True, [identbb, smkb], [psb_])
                mm(ps2[0:32, 0:32], qT[base:base + 64, hp, c0:c0 + 32], kTa[base:base + 64, g, K0 + 512:K0 + 544], True, False, [qTb, kTab], [ps2b])
                mm(ps2[0:32, 0:32], identb[0:32, 0:32], smk[0:32, 512:544], False, True, [identbb, smkb], [ps2b])
                fw.op(DVE, lambda: nc.vector.tensor_reduce(sm[0:32, 0:1], ps[0:32, 0:512], mybir.AxisListType.X, ALU.max), R=[psb_], W=[smb])
                fw.op(DVE, lambda: nc.vector.tensor_reduce(sm[0:32, 1:2], ps2[0:32, 0:32], mybir.AxisListType.X, ALU.max), R=[ps2b], W=[smb])
                tt(DVE, sm[0:32, 2:3], sm[0:32, 0:1], sm[0:32, 1:2], ALU.max, [smb], [smb])
                ts(DVE, sm[0:32, 4:5], sm[0:32, 2:3], 0.125, sinks[0:32, j, hq:hq + 1], ALU.mult, ALU.max, [smb, sinksb], [smb])
                ts(DVE, sm[0:32, 8:9], sm[0:32, 4:5], -1.0, None, ALU.mult, None, [smb], [smb])
                act(pn[0:32, 0:512], ps[0:32, 0:512], AF.Exp, [psb_, smb], [pnb, smb], bias=sm[0:32, 8:9], scale=0.125, accum=sm[0:32, 12:13])
                act(pn[0:32, 512:544], ps2[0:32, 0:32], AF.Exp, [ps2b, smb], [pnb, smb], bias=sm[0:32, 8:9], scale=0.125, accum=sm[0:32, 13:14])
                tt(DVE, sm[0:32, 16:17], sinks[0:32, j, hq:hq + 1], sm[0:32, 4:5], ALU.subtract, [sinksb, smb], [smb])
                act(sm[0:32, 20:21], sm[0:32, 16:17], AF.Exp, [smb], [smb])
                tt(DVE, sm[0:32, 24:25], sm[0:32, 12:13], sm[0:32, 13:14], ALU.add, [smb], [smb])
                tt(DVE, sm[0:32, 25:26], sm[0:32, 24:25], sm[0:32, 20:21], ALU.add, [smb], [smb])
                fw.op(DVE, lambda: nc.vector.reciprocal(sm[0:32, 28:29], sm[0:32, 25:26]), R=[smb], W=[smb])
                ts(DVE, pn[0:32, :], pn[0:32, :], sm[0:32, 28:29], None, ALU.mult, None, [pnb, smb], [pnb])
                return (hp, hh, ix)

            spo = {}

            def samp_b(hp, hh, ix):
                pn, pnb = pss[ix]
                pT, pTb = pTs[ix]
                if hh == 0:
                    spo[hp] = PF()
                po, pob = spo[hp]
                ptT, ptTb = PB()
                for kb in range(5):
                    kn = 128 if kb < 4 else 32
                    tr(ptT[0:kn, kb * 32:(kb + 1) * 32], pn[0:32, kb * 128:kb * 128 + kn], [pnb], [ptTb], sig=(kb == 4))
                cp(ACT, pT[:, 0:4, 0:32], ptT[:, 0:128].rearrange("p (k q) -> p k q", q=32), [ptTb], [pTb])
                cp(ACT, pT[0:32, 4, 0:32], ptT[0:32, 128:160], [ptTb], [pTb])
                for kb in range(5):
                    kn = 128 if kb < 4 else 32
                    vt = vA[0:kn, (10 + kb) if kb < 4 else 9, g * 64:(g + 1) * 64]
                    mm(po[hh * 64:(hh + 1) * 64, 0:32], vt, pT[0:kn, kb, 0:32], kb == 0, kb == 4, [vAb, pTb], [pob], tp=(0, hh * 64))
                if hh == 1:
                    cp(ACT, oT[:, hp, c0:c0 + 32], po[:, 0:32], [pob], [oTb])

            pend = None
            for hp in range(4):
                for hh in range(2):
                    cur = samp_a(hp, hh)
                    if pend is not None:
                        samp_b(*pend)
                    pend = cur
            samp_b(*pend)
            fw.tag = "att.o"
            if g == 0:
                ada_finish(att["gatejob"])
            pieces = [WNEXT(("r", "att_w_o", j, g * 512 + r * 256)) for r in range(2)]
            proj_rows(pieces, 4, oT, oTb)
        ada_flush()
        fw.barrier()
        wcur["l"] = list(wsl)
        st.close()

    fw.dma(SP, bada[:], d_bada[:, :, :], W=[badab])
    kvstack = contextlib.ExitStack()
    kvmodstack = contextlib.ExitStack()
    ada_queue(ada_job_sa(0, 0))
    ada_flush()
    k = 0
    for l in range(DEPTH):
        fw.dma(SP, vecl[:], d_vecl[l, :, :], W=[veclb])
        M.update(t=modall, b=modallb, g=32 + 16 * (k % 2))
        att["gatejob"] = ada_job_gate(l, 0, k % 2)
        ada_queue(att["gatejob"])
        ada_queue(ada_job_std(l, 1, (k + 1) % 2))
        if l < N_A:
            retention(l)
        else:
            if l == N_A:
                kv_stage(kvstack)
                kvmodstack.close()
            attention(l, l - N_A)
        k += 1
        M.update(t=modall, b=modallb, g=32 + 16 * (k % 2))
        if l + 1 < DEPTH:
            if l + 1 == N_A:
                kTa, kTab = sb("kTa", [128, 4, 128 + TP + 512 + 32], BF16, kvstack)
                vA, vAb = sb("vA", [128, 14, 256], BF16, kvstack)
                att.update(dict(kTa=kTa, kTab=kTab, vA=vA, vAb=vAb))
                kvm, _ = sb("kvmod", [128, 32, 33], F32, kvmodstack)
                kvmb = [Buf(f"kvmod{j}") for j in range(32)]
                att["kvmod"] = (kvm, kvmb)
                ada_queue(AdaJob("kv_w_ada", None, 0, 32, kvm, kvmb, lambda j: (vecg[:, 16 + j:17 + j], vecgb)))
            ada_queue(ada_job_sa(l + 1, 0))
        ffn(l)
        k += 1
    kvstack.close()
    stk = contextlib.ExitStack()
    rstd, rstdb = rms_stats(stk, NT, False)
    for kc in range(KC):
        stt(DVE, x[:, kc, :], x[:, kc, :], vecg[:, 48 + kc:49 + kc], rstd[:, 0:NT], ALU.mult, ALU.mult, xb[kc] + [vecgb, rstdb], xb[kc])
        fw.dma(SP, o_yT[:, kc, :], x[:, kc, :], R=xb[kc])
    fw.finish()
    stk.close()
    es.close()
    assert wstate["n"] == NP, (wstate["n"], NP)
    ok, stuck, val = fw.simulate()
    assert ok, ("semaphore deadlock in emitted program", stuck)
    build_program.stats = {q.name: (q.cnt, len(q.log)) for q in fw.queues}
    build_program.maxsem = max(val.values())
    build_program.specs = specs
    build_program.tags = {q.name: q.tags for q in fw.queues}
    return nc


def _piece_cols(Wm, cols):
    sub = Wm[:, cols]
    return np.ascontiguousarray(sub.reshape(KC, 128, 256).transpose(1, 0, 2)).reshape(128, WCOLS)


def _piece_rows(Wm, r0):
    sub = Wm[r0:r0 + 256, :]
    return np.ascontiguousarray(sub.reshape(2, 128, 2048).transpose(1, 0, 2)).reshape(128, WCOLS)


def build_weights(inp, specs):
    out = np.empty((len(specs), 128, WCOLS), np.float32)
    for i, (kind, name, idx, a0) in enumerate(specs):
        Wm = inp[name] if idx is None else inp[name][idx]
        if kind == "c":
            out[i] = _piece_cols(Wm, np.arange(a0, a0 + 256))
        elif kind == "c2":
            out[i] = _piece_cols(Wm, np.concatenate([np.arange(a0 * 128, (a0 + 1) * 128), np.arange(4096 + a0 * 128, 4096 + (a0 + 1) * 128)]))
        else:
            out[i] = _piece_rows(Wm, a0)
    return out


def fm(vec):
    return np.ascontiguousarray(np.asarray(vec, np.float32).reshape(-1, 128).T)


def build_inputs(inp, specs):
    f32 = np.float32
    wts = build_weights(inp, specs)
    vecl = np.zeros((DEPTH, 128, 416), f32)
    for l in range(DEPTH):
        vecl[l, :, 0:96] = fm(inp["b_ada"][l])
        vecl[l, :, 96:112] = fm(inp["norm_mix"][l])
        vecl[l, :, 112:128] = fm(inp["norm_ffn"][l])
        for tap in range(3):
            vecl[l, :, 128 + tap * 64:128 + (tap + 1) * 64] = fm(inp["ffn_conv_w"][l, tap])
        vecl[l, :, 320:384] = fm(inp["ffn_conv_b"][l])
        if l < N_A:
            vecl[l, :, 384:416] = fm(inp["ret_gn"][l])
    bada = np.ascontiguousarray(np.stack([fm(inp["b_ada"][l]) for l in range(DEPTH)], 1))
    vecg = np.zeros((128, 64), f32)
    vecg[:, 0:16] = fm(inp["kv_norm"])
    vecg[:, 16:48] = fm(inp["kv_b_ada"])
    vecg[:, 48:64] = fm(inp["norm_f"])
    gam = np.array(GAM, np.float64)
    m = np.arange(128, dtype=np.float64)
    rsc = np.zeros((128, 64), np.float64)
    for h in range(RH):
        rsc[:, h] = gam[h] ** (127 - m) / 16.0
        rsc[:, 8 + h] = gam[h] ** (-(m + 1)) / 16.0
        ms = (np.arange(32) % 8).astype(np.float64)
        rsc[:32, 16 + h] = gam[h] ** (7 - ms) / 16.0
        rsc[:32, 24 + h] = gam[h] ** (-(ms + 1)) / 16.0
    for s in range(4):
        rsc[8 * s:8 * s + 8, 32 + s] = 1.0
    rsc = rsc.astype(f32)
    cmask = np.zeros((128, 160), f32)
    cmask[:, 0:128] = (m[:, None] <= m[None, :]).astype(f32)
    i32 = np.arange(32)
    cmask[:32, 128:160] = ((i32[:, None] // 8 == i32[None, :] // 8) & (i32[:, None] <= i32[None, :])).astype(f32)
    inv_r = (10000.0 ** (-np.arange(128, dtype=f32) * 2.0 / 256.0)).astype(f32)
    inv_a = (500000.0 ** (-np.arange(8, dtype=f32) * 2.0 / 16.0)).astype(f32)
    qi = np.arange(128)[:, None]
    kj = np.arange(256)[None, :]
    band = np.where(((kj < 128) & (kj >= qi)) | ((kj >= 128) & (kj - 128 <= qi)), 0.0, NEG).astype(f32)
    first = np.where((kj >= 128) & (kj - 128 <= qi), 0.0, NEG).astype(f32)
    sm_ = np.full((32, 544), NEG, f32)
    for s in range(4):
        for i in range(8):
            r = 8 * s + i
            for jx in range(136):
                if i <= jx <= 128 + i:
                    if jx < 128:
                        sm_[r, s * 128 + jx] = 0.0
                    else:
                        sm_[r, 512 + 8 * s + (jx - 128)] = 0.0
    sinks = np.ascontiguousarray(np.broadcast_to(np.asarray(inp["att_sinks"], f32).reshape(1, 2, 32), (128, 2, 32)))

    maps = []
    for c in range(NCORES):
        s_, half = c // 2, c % 2
        t0 = half * TP
        ss = slice(4 * c, 4 * c + 4)
        xcat = np.concatenate([inp["x_prompt"][s_, t0:t0 + TP, :], inp["x_sample"][ss].reshape(TS, D)], 0)
        xT = np.ascontiguousarray(xcat.reshape(NT, KC, 128).transpose(2, 1, 0))
        ccat = np.concatenate([inp["c_prompt"][s_:s_ + 1], inp["c_sample"][ss]], 0)
        cT = np.ascontiguousarray(ccat.reshape(5, KC, 128).transpose(2, 1, 0))
        pos = np.concatenate([np.arange(t0, t0 + TP), np.tile(PAST + np.arange(8), 4)]).astype(f32)
        ang = pos[None, :] * inv_r[:, None]
        ropeR = np.stack([np.cos(ang), np.sin(ang)], 1).astype(f32)
        lpos = np.concatenate([np.arange(128), np.tile(np.arange(8), 4)]).astype(np.float64)
        qdec = np.ascontiguousarray(np.broadcast_to((gam[:, None] ** (lpos[None, :] + 1.0))[:, None, :], (RH, 128, 160))).astype(f32)
        angA = pos[:, None] * inv_a[None, :]
        ropeA = np.zeros((128, 9, 2, 8), f32)
        for t, (c0, cn) in enumerate(TTILES):
            ropeA[:cn, t, 0, :] = np.cos(angA[c0:c0 + cn]).astype(f32)
            ropeA[:cn, t, 1, :] = np.sin(angA[c0:c0 + cn]).astype(f32)
        amask = np.stack([first if half == 0 else band, band], 1).astype(f32)
        ck = np.asarray(inp["cache_win_k"][ss], f32)
        cvv = np.asarray(inp["cache_win_v"][ss], f32)
        ckT = np.zeros((128, 4, 512), f32)
        for g in range(4):
            kt = ck[:, :, g, :].transpose(2, 0, 1).reshape(64, 512)
            ckT[0:64, g, :] = kt
            ckT[64:128, g, :] = kt
        cv = np.ascontiguousarray(cvv.reshape(4, 128, 256).transpose(1, 0, 2))
        convs = np.ascontiguousarray(
            np.asarray(inp["state_conv"][:, ss], f32).reshape(DEPTH, 4, 2, 64, 128).transpose(0, 4, 3, 1, 2)).reshape(DEPTH, 128, 64, 8)
        maps.append(dict(
            xT=xT, cT=cT, flag=np.full((128, 1), float(half), f32), wts=wts, vecl=vecl, vecg=vecg, ropeR=ropeR, qdec=qdec,
            rsc=rsc, cmask=cmask, sret=np.ascontiguousarray(inp["state_ret"][:, ss]), convs=convs, ropeA=ropeA,
            amask=amask, smask=sm_, ckT=ckT, cv=cv, cknat=np.ascontiguousarray(ck.reshape(4, 128, 256)),
            cvnat=np.ascontiguousarray(cvv.reshape(4, 128, 256)), sinks=sinks, ident=np.eye(128, dtype=f32), bada=bada))
    return maps


def assemble(res):
    f32 = np.float32
    y_p = np.zeros((4, 2048, D), f32)
    y_s = np.zeros((32, 8, D), f32)
    ret_p = np.zeros((N_A, 4, RH, 256, 512), f32)
    ret_s = np.zeros((N_A, 32, RH, 256, 512), f32)
    wk_p = np.zeros((4, 128, 4, 64), f32)
    wv_p = np.zeros((4, 128, 4, 64), f32)
    wk_s = np.zeros((32, 128, 4, 64), f32)
    wv_s = np.zeros((32, 128, 4, 64), f32)
    cv_p = np.zeros((DEPTH, 4, 2, 8192), f32)
    cv_s = np.zeros((DEPTH, 32, 2, 8192), f32)
    for c in range(NCORES):
        r = res[c]
        s_, half = c // 2, c % 2
        t0 = half * TP
        yt = r["yT"].transpose(2, 1, 0).reshape(NT, D)
        y_p[s_, t0:t0 + TP] = yt[:TP]
        y_s[4 * c:4 * c + 4] = yt[TP:].reshape(4, 8, D)
        ret_s[:, 4 * c:4 * c + 4] = r["rets"]
        wk_s[4 * c:4 * c + 4] = r["wks"].reshape(4, 128, 4, 64)
        wv_s[4 * c:4 * c + 4] = r["wvs"].reshape(4, 128, 4, 64)
        co = r["convo"].reshape(DEPTH, 128, 64, 5, 2).transpose(0, 3, 4, 2, 1).reshape(DEPTH, 5, 2, 8192)
        cv_s[:, 4 * c:4 * c + 4] = co[:, 1:5]
        if half == 1:
            ret_p[:, s_] = r["retp"]
            wk_p[s_] = r["wkp"].reshape(128, 4, 64)
            wv_p[s_] = r["wvp"].reshape(128, 4, 64)
            cv_p[:, s_] = co[:, 0]
    return (y_p, y_s, ret_p, ret_s, wk_p, wv_p, wk_s, wv_s, cv_p, cv_s)


_CACHE = {}


def kernel(**inputs):
    inp = {k: np.asarray(v) for k, v in inputs.items()}
    if "nc" not in _CACHE:
        _CACHE["nc"] = build_program()
    nc = _CACHE["nc"]
    maps = build_inputs(inp, build_program.specs)
    res = run_bass_kernel_spmd(nc, maps, core_ids=list(range(NCORES)))
    return assemble(res.results)
```

```python
import contextlib
import numpy as np
import concourse.bass as bass
import concourse.mybir as mybir
from concourse.bass_utils import run_bass_kernel_spmd

F32 = mybir.dt.float32
BF16 = mybir.dt.bfloat16
ALU = mybir.AluOpType
AF = mybir.ActivationFunctionType

D = 2048
KC = 16
TP = 1024
TS = 32
NT = TP + TS
NTX = NT + 2
DEPTH = 4
N_A = 2
RH = 8
PAST = 16384
EPS = 1e-6
NEG = -1e30
UE = 1074
NTILES = [(0, 512), (512, 512), (1024, 32)]
NTILES_X = [(0, 512), (512, 512), (1024, 34)]
TTILES = [(i * 128, 128) for i in range(8)] + [(1024, 32)]
GAM = [1.0 - 2.0 ** (-5.0 - h) for h in range(RH)]
NCORES = 8
WCOLS = 4096


class Buf:
    __slots__ = ("name", "w", "r")

    def __init__(self, name):
        self.name = name
        self.w = None
        self.r = {}


class Q:
    def __init__(self, fw, name, eng, is_pe=False):
        self.fw = fw
        self.name = name
        self.eng = eng
        self.is_pe = is_pe
        self.sid = fw.new_sem(name)
        self.cnt = 0
        self.seen = {}
        self.dsems = []
        self.dcnt = []
        self.drr = 0
        self.log = []
        self.tags = []

    def wait(self, ev, same_ok=False):
        if ev is None:
            return
        s, v = ev
        if s == self.sid and (same_ok or self.is_pe):
            return
        if self.seen.get(s, 0) >= v:
            return
        self.eng.wait_ge(self.fw.sems[s], v)
        self.seen[s] = v
        self.log.append(("w", s, v))


class FW:
    def __init__(self, nc):
        self.nc = nc
        self.sems = []
        self.tag = ""
        self.kind = ""
        self.pe = Q(self, "pe", nc.tensor, is_pe=True)
        self.act = Q(self, "act", nc.scalar)
        self.dve = Q(self, "dve", nc.vector)
        self.pool = Q(self, "pool", nc.gpsimd)
        self.sp = Q(self, "sp", nc.sync)
        for q, n in ((self.pool, 6), (self.sp, 10)):
            for i in range(n):
                q.dsems.append(self.new_sem(f"{q.name}_d{i}"))
                q.dcnt.append(0)
        self.cc_sid = self.new_sem("cc")
        self.cc_cnt = 0
        self.queues = [self.pe, self.act, self.dve, self.pool, self.sp]

    def new_sem(self, name):
        self.sems.append(self.nc.alloc_semaphore("s_" + name))
        return len(self.sems) - 1

    def _deps(self, q, R, W):
        for b in R:
            q.wait(b.w)
        for b in W:
            q.wait(b.w, same_ok=True)
            for s, v in b.r.items():
                q.wait((s, v), same_ok=True)

    def _commit(self, ev, R, W):
        s, v = ev
        for b in R:
            if b.r.get(s, 0) < v:
                b.r[s] = v
        for b in W:
            b.w = ev
            b.r = {}

    def op(self, q, emit, R=(), W=(), sig=True):
        self._deps(q, R, W)
        ins = emit()
        q.tags.append((self.tag, self.kind))
        ev = (q.sid, q.cnt + 1)
        if sig:
            ins.then_inc(self.sems[q.sid], 1)
            q.cnt += 1
            q.log.append(("i", q.sid, 1))
        self._commit(ev, R, W)
        return ev

    def dma(self, q, out, in_, R=(), W=()):
        self._deps(q, R, W)
        j = q.drr
        q.drr = (q.drr + 1) % len(q.dsems)
        sid = q.dsems[j]
        q.wait((sid, 16 * q.dcnt[j]))
        ins = q.eng.dma_start(out=out, in_=in_)
        q.dcnt[j] += 1
        ins.then_inc(self.sems[sid], 16)
        q.log.append(("i", sid, 16))
        ev = (sid, 16 * q.dcnt[j])
        self._commit(ev, R, W)
        return ev

    def allgather(self, in_ap, out_ap, R, W):
        q = self.pool
        self._deps(q, R, W)
        q.wait((self.cc_sid, self.cc_cnt))
        ins = self.nc.gpsimd.collective_compute(
            "AllGather", ALU.bypass, replica_groups=[[0, 1], [2, 3], [4, 5], [6, 7]],
            ins=[in_ap], outs=[out_ap])
        self.cc_cnt += 1
        ins.then_inc(self.sems[self.cc_sid], 1)
        q.log.append(("i", self.cc_sid, 1))
        ev = (self.cc_sid, self.cc_cnt)
        self._commit(ev, R, W)
        return ev

    def barrier(self):
        evs = []
        for q in self.queues:
            if q.cnt:
                evs.append((q.sid, q.cnt))
            for sid, c in zip(q.dsems, q.dcnt):
                if c:
                    evs.append((sid, 16 * c))
        if self.cc_cnt:
            evs.append((self.cc_sid, self.cc_cnt))
        for q in self.queues:
            for ev in evs:
                q.wait(ev, same_ok=True)

    def simulate(self):
        val = {}
        pos = {q.name: 0 for q in self.queues}
        progress = True
        while progress:
            progress = False
            for q in self.queues:
                p = pos[q.name]
                while p < len(q.log):
                    k, s_, v = q.log[p]
                    if k == "w":
                        if val.get(s_, 0) < v:
                            break
                    else:
                        val[s_] = val.get(s_, 0) + v
                    p += 1
                    progress = True
                pos[q.name] = p
        stuck = {q.name: (pos[q.name], len(q.log), q.log[pos[q.name]] if pos[q.name] < len(q.log) else None) for q in self.queues}
        ok = all(pos[q.name] == len(q.log) for q in self.queues)
        return ok, stuck, val

    def finish(self):
        q = self.sp
        for qq in self.queues:
            if qq.cnt:
                q.wait((qq.sid, qq.cnt), same_ok=True)
            for sid, c in zip(qq.dsems, qq.dcnt):
                if c:
                    q.wait((sid, 16 * c))
        if self.cc_cnt:
            q.wait((self.cc_sid, self.cc_cnt))


def n_weight_pieces():
    per_ret = 24 + RH * 8 + 24 + 8 * 6
    per_att = 24 + 4 * 4 + 24 + 8 * 6
    return 2 * per_ret + 16 + 2 + 2 * per_att


def build_program():
    nc = bass.Bass("TRN2", target_bir_lowering=False)
    fw = FW(nc)
    PE, ACT, DVE, POOL, SP = fw.pe, fw.act, fw.dve, fw.pool, fw.sp
    NP = n_weight_pieces()

    def din(name, shape, dt=F32):
        return nc.dram_tensor(name, list(shape), dt, kind="ExternalInput")

    def dout(name, shape, dt=F32):
        return nc.dram_tensor(name, list(shape), dt, kind="ExternalOutput")

    d_xT = din("xT", [128, KC, NT])
    d_cT = din("cT", [128, KC, 5])
    d_flag = din("flag", [128, 1])
    d_wts = din("wts", [NP, 128, WCOLS])
    d_vecl = din("vecl", [DEPTH, 128, 416])
    d_vecg = din("vecg", [128, 64])
    d_ropeR = din("ropeR", [128, 2, NT])
    d_qdec = din("qdec", [RH, 128, 160])
    d_rsc = din("rsc", [128, 64])
    d_cmask = din("cmask", [128, 160])
    d_sret = din("sret", [N_A, 4, RH, 256, 512])
    d_convs = din("convs", [DEPTH, 128, 64, 8])
    d_ropeA = din("ropeA", [128, 9, 2, 8])
    d_amask = din("amask", [128, 2, 256])
    d_smask = din("smask", [32, 544])
    d_ckT = din("ckT", [128, 4, 512])
    d_cv = din("cv", [128, 4, 256])
    d_cknat = din("cknat", [4, 128, 256])
    d_cvnat = din("cvnat", [4, 128, 256])
    d_sinks = din("sinks", [128, 2, 32])
    d_ident = din("ident", [128, 128])
    d_bada = din("bada", [128, DEPTH, 96])

    o_yT = dout("yT", [128, KC, NT])
    o_retp = dout("retp", [N_A, RH, 256, 512])
    o_rets = dout("rets", [N_A, 4, RH, 256, 512])
    o_wkp = dout("wkp", [128, 256])
    o_wvp = dout("wvp", [128, 256])
    o_wks = dout("wks", [4, 128, 256])
    o_wvs = dout("wvs", [4, 128, 256])
    o_conv = dout("convo", [DEPTH, 128, 640])

    x_sb = [[(nc.dram_tensor(f"xs_b{l}_{h}", [128, 1024], F32), nc.dram_tensor(f"xs_g{l}_{h}", [256, 1024], F32))
             for h in range(RH)] for l in range(N_A)]
    x_fb = [(nc.dram_tensor(f"xf_b{l}", [128, 32], F32), nc.dram_tensor(f"xf_g{l}", [256, 32], F32)) for l in range(DEPTH)]
    x_kv = (nc.dram_tensor("xkv_b", [128, 512], F32), nc.dram_tensor("xkv_g", [256, 512], F32))
    dbufs = {}

    def dbuf(t):
        if t.name not in dbufs:
            dbufs[t.name] = Buf(t.name)
        return dbufs[t.name]

    es = contextlib.ExitStack()

    uniq = [0]

    def sb(name, shape, dt, stack=None):
        uniq[0] += 1
        t = (stack or es).enter_context(nc.sbuf_tensor(f"s_{name}_{uniq[0]}", list(shape), dt))
        return t, Buf(name)

    x, _ = sb("x", [128, KC, NT], F32)
    xb = [[Buf(f"x{c}_{n}") for n in range(3)] for c in range(KC)]
    hT, _ = sb("hT", [128, KC, NTX], BF16)
    hb = [Buf(f"h{c}") for c in range(KC)]
    modall, _ = sb("modall", [128, 64, 33], F32)
    modallb = [Buf(f"mod_{j}") for j in range(64)]
    M = {"t": modall, "b": modallb, "g": 32}
    bada, badab = sb("bada", [128, DEPTH, 96], F32)
    NWS = 3
    wsl = [sb(f"w{i}", [128, WCOLS], BF16) for i in range(NWS)]
    vecl, veclb = sb("vecl", [128, 416], F32)
    vecg, vecgb = sb("vecg", [128, 64], F32)
    identb, identbb = sb("identb", [128, 128], BF16)
    identf, identfb = sb("identf", [128, 128], F32)
    onesb, onesbb = sb("onesb", [128, 128], BF16)
    csT, csTb = sb("csT", [128, KC, 33], BF16)
    cTs, cTsb = sb("cTs", [128, KC, 5], F32)
    flag, flagb = sb("flag", [128, 1], F32)
    xnb, xnbb = sb("xnb", [128, KC, 2], F32)
    small, smallb = sb("small", [128, 64], F32)

    psf = [(nc.alloc_psum_tensor(f"psf{i}", [128, 512], F32), Buf(f"psf{i}")) for i in range(6)]
    psb = [(nc.alloc_psum_tensor(f"psb{i}", [128, 1024], BF16), Buf(f"psb{i}")) for i in range(2)]
    rr = {"f": 0, "b": 0}

    reserved = set()

    def PF():
        while True:
            i = rr["f"]
            rr["f"] = (i + 1) % 6
            if i not in reserved:
                return psf[i]

    def bank_index(pb):
        return [b for _, b in psf].index(pb)

    def PB():
        i = rr["b"]
        rr["b"] = (i + 1) % 2
        return psb[i]

    wstate = {"n": 0}

    specs = []

    wcur = {"l": list(wsl)}

    def extra_slot(stack):
        if nc.sbuf_bytes_remaining >= 2 * WCOLS + 128:
            wcur["l"] = list(wsl) + [sb("wx", [128, WCOLS], BF16, stack)]

    def WNEXT(spec, cols=WCOLS):
        i = wstate["n"]
        wstate["n"] += 1
        specs.append(spec)
        t, b = wcur["l"][i % len(wcur["l"])]
        fw.dma(POOL, t[:, 0:cols], d_wts[i, :, 0:cols], W=[b])
        return t, b

    def mm(out, lhsT, rhs, start, stop, R, W, tp=None, sig=None):
        sg = stop if sig is None else sig
        fw.kind = 'm' if tp is None else 'p'
        if tp is None:
            fw.op(PE, lambda: nc.tensor.matmul(out, lhsT, rhs, start=start, stop=stop), R=R, W=W, sig=sg)
        else:
            fw.op(PE, lambda: nc.tensor.matmul(out, lhsT, rhs, start=start, stop=stop, tile_position=tp), R=R, W=W, sig=sg)

    def tr(out, in_, R, W, sig=True):
        kp = in_.shape[0]
        fw.kind = 't'
        fw.op(PE, lambda: nc.tensor.transpose(out, in_, identb[0:kp, 0:kp]), R=list(R) + [identbb], W=W, sig=sig)

    def act(out, in_, func, R, W, bias=0.0, scale=1.0, accum=None):
        if accum is None:
            return fw.op(ACT, lambda: nc.scalar.activation(out, in_, func, bias=bias, scale=scale), R=R, W=W)
        return fw.op(ACT, lambda: nc.scalar.activation(out, in_, func, bias=bias, scale=scale, accum_out=accum), R=R, W=W)

    def tt(q, out, a, b, op, R, W):
        return fw.op(q, lambda: q.eng.tensor_tensor(out, a, b, op), R=R, W=W)

    def ts(q, out, a, s1, s2, op0, op1, R, W):
        if s2 is None:
            return fw.op(q, lambda: q.eng.tensor_scalar(out, a, s1, None, op0), R=R, W=W)
        return fw.op(q, lambda: q.eng.tensor_scalar(out, a, s1, s2, op0, op1), R=R, W=W)

    def stt(q, out, a, s, b, op0, op1, R, W):
        return fw.op(q, lambda: q.eng.scalar_tensor_tensor(out, a, s, b, op0, op1), R=R, W=W)

    def cp(q, out, in_, R, W):
        if q is ACT:
            return act(out, in_, AF.Copy, R, W)
        return fw.op(q, lambda: q.eng.tensor_copy(out, in_), R=R, W=W)

    def ntile_of(c0):
        return 0 if c0 < 512 else (1 if c0 < 1024 else 2)

    for c in range(KC):
        fw.dma(SP, x[:, c, :], d_xT[:, c, :], W=xb[c])
    fw.dma(SP, cTs[:], d_cT[:, :, :], W=[cTsb])
    fw.dma(SP, flag[:], d_flag[:, :], W=[flagb])
    fw.dma(SP, vecg[:], d_vecg[:, :], W=[vecgb])
    fw.dma(SP, identf[:], d_ident[:, :], W=[identfb])
    fw.op(DVE, lambda: nc.vector.memset(onesb[:], 1.0), W=[onesbb])
    cp(DVE, identb[:], identf[:], [identfb], [identbb])
    act(csT[:, :, 0:1], cTs[:, :, 0:1], AF.Silu, [cTsb], [csTb])
    for s in range(4):
        for t in range(8):
            act(csT[:, :, 1 + 8 * s + t:2 + 8 * s + t], cTs[:, :, 1 + s:2 + s], AF.Silu, [cTsb], [csTb])

    class AdaJob:
        def __init__(self, wname, l, c0, nj, dst_t, dst_b, bias_fn, jmap=None):
            self.wname, self.l, self.c0, self.nj = wname, l, c0, nj
            self.dst_t, self.dst_b, self.bias_fn = dst_t, dst_b, bias_fn
            self.jmap = jmap or (lambda j: j)
            self.done = 0
            self.total = nj // 2

        def step(self):
            i = self.done
            self.done += 1
            prev_tag = fw.tag
            fw.tag = "adaln"
            wt, wb_ = WNEXT(("c", self.wname, self.l, self.c0 + i * 256))
            wv = wt[:, :].rearrange("p (k c) -> p k c", c=256)
            for mc in range(2):
                j = 2 * i + mc
                pt, pb = PF()
                for k in range(KC):
                    mm(pt[:, 0:33], wv[:, k, mc * 128:(mc + 1) * 128], csT[:, k, :], k == 0, k == KC - 1, [wb_, csTb], [pb])
                bcol, bb = self.bias_fn(j)
                jd = self.jmap(j)
                act(self.dst_t[:, jd, :], pt[:, 0:33], AF.Identity, [pb, bb], [self.dst_b[jd]], bias=bcol)
            fw.tag = prev_tag

    ada = {"jobs": [], "k": 0}

    def ada_queue(job):
        ada["jobs"].append(job)

    def ada_total():
        return sum(jb.total for jb in ada["jobs"]), sum(jb.done for jb in ada["jobs"])

    def ada_tick(frac):
        tot, done = ada_total()
        want = min(tot, int(np.ceil(tot * frac)))
        for jb in ada["jobs"]:
            while done < want and jb.done < jb.total:
                jb.step()
                done += 1

    def ada_flush():
        ada_tick(1.0)
        ada["jobs"] = []

    def ada_step(n):
        for jb in ada["jobs"]:
            while n > 0 and jb.done < jb.total:
                jb.step()
                n -= 1

    def ada_job_std(l, part, slot):
        return AdaJob("w_ada", l, part * 6144, 48, modall, modallb,
                      lambda j: (bada[:, l, part * 48 + j:part * 48 + j + 1], badab),
                      jmap=lambda j: j if j < 32 else 32 + 16 * slot + (j - 32))

    def ada_job_sa(l, part):
        return AdaJob("w_ada", l, part * 6144, 32, modall, modallb,
                      lambda j: (bada[:, l, part * 48 + j:part * 48 + j + 1], badab))

    def ada_job_gate(l, part, slot):
        return AdaJob("w_ada", l, part * 6144 + 4096, 16, modall, modallb,
                      lambda j: (bada[:, l, part * 48 + 32 + j:part * 48 + 33 + j], badab),
                      jmap=lambda j: 32 + 16 * slot + j)

    def ada_finish(job):
        while job.done < job.total:
            job.step()

    def make_amod(gt, gbuf, gcol0):
        mod, modb = M["t"], M["b"]
        for kc in range(KC):
            ts(DVE, mod[:, 16 + kc, :], mod[:, 16 + kc, :], 1.0, gt[:, gcol0 + kc:gcol0 + kc + 1], ALU.add, ALU.mult,
               [modb[16 + kc], gbuf], [modb[16 + kc]])

    def rms_stats(stk, ncols, with_nb):
        fw.tag = "norm"
        rstd, rstdb = sb("rstd", [128, NTX], F32, stk)
        sqs = [sb(f"sq{i}", [128, NTX], BF16, stk) for i in range(2)]
        tiles = NTILES_X if with_nb else NTILES
        pts = [PF() for _ in tiles]
        for kc in range(KC):
            sqt, sqb = sqs[kc % 2]
            act(sqt[:, 0:NT], x[:, kc, :], AF.Square, xb[kc], [sqb])
            if with_nb:
                act(sqt[:, NT:NTX], xnb[:, kc, :], AF.Square, [xnbb], [sqb])
            for (n0, nn), (pt, pb) in zip(tiles, pts):
                mm(pt[:, 0:nn], onesb[:, :], sqt[:, n0:n0 + nn], kc == 0, kc == KC - 1, [onesbb, sqb], [pb], sig=True)
        for (n0, nn), (pt, pb) in zip(tiles, pts):
            ts(DVE, rstd[:, n0:n0 + nn], pt[:, 0:nn], 1.0 / D, EPS, ALU.mult, ALU.add, [pb], [rstdb])
        act(rstd[:, 0:ncols], rstd[:, 0:ncols], AF.Sqrt, [rstdb], [rstdb])
        fw.op(DVE, lambda: nc.vector.reciprocal(rstd[:, 0:ncols], rstd[:, 0:ncols]), R=[rstdb], W=[rstdb])
        return rstd, rstdb

    def norm_mod(with_nb):
        stk = contextlib.ExitStack()
        mod, modb = M["t"], M["b"]
        ncols = NTX if with_nb else NT
        rstd, rstdb = rms_stats(stk, ncols, with_nb)
        ntmp = [sb(f"ntmp{i}", [128, NTX], F32, stk) for i in range(2)]
        for kc in range(KC):
            nt_, ntb = ntmp[kc % 2]
            A = mod[:, 16 + kc, :]
            Ab = modb[16 + kc]
            tt(DVE, nt_[:, 0:NT], x[:, kc, :], rstd[:, 0:NT], ALU.mult, xb[kc] + [rstdb], [ntb])
            if with_nb:
                tt(DVE, nt_[:, NT:NTX], xnb[:, kc, :], rstd[:, NT:NTX], ALU.mult, [xnbb, rstdb], [ntb])
            act(hT[:, kc, 0:TP], nt_[:, 0:TP], AF.Identity, [ntb, Ab, modb[kc]], [hb[kc]], bias=mod[:, kc, 0:1], scale=A[:, 0:1])
            if with_nb:
                act(hT[:, kc, NT:NTX], nt_[:, NT:NTX], AF.Identity, [ntb, Ab, modb[kc]], [hb[kc]], bias=mod[:, kc, 0:1], scale=A[:, 0:1])
            tt(DVE, nt_[:, TP:NT], nt_[:, TP:NT], A[:, 1:33], ALU.mult, [ntb, Ab], [ntb])
            tt(DVE, hT[:, kc, TP:NT], nt_[:, TP:NT], mod[:, kc, 1:33], ALU.add, [ntb, modb[kc]], [hb[kc]])
            if kc % 4 == 1 and M["g"] is not None:
                ada_step(1)
        fw.barrier()
        stk.close()

    def resid_add(pt, pb, m, n0, nn, gj):
        mod, modb = M["t"], M["b"]
        nt_i = ntile_of(n0)
        if n0 < TP:
            stt(DVE, x[:, m, n0:n0 + nn], pt[:, 0:nn], mod[:, gj, 0:1], x[:, m, n0:n0 + nn], ALU.mult, ALU.add,
                [pb, modb[gj], xb[m][nt_i]], [xb[m][nt_i]])
        else:
            tt(DVE, small[:, 0:32], pt[:, 0:32], mod[:, gj, 1:33], ALU.mult, [pb, modb[gj]], [smallb])
            tt(DVE, x[:, m, TP:NT], x[:, m, TP:NT], small[:, 0:32], ALU.add, [smallb, xb[m][2]], [xb[m][2]])

    def proj_rows(pieces, kchunks, src, srcb):
        for m in range(KC):
            pts = [PF() for _ in NTILES]
            for kc in range(kchunks):
                wt, wb_ = pieces[kc // 2]
                wv = wt[:, :].rearrange("p (k c) -> p k c", c=2048)
                for (n0, nn), (pt, pb) in zip(NTILES, pts):
                    mm(pt[:, 0:nn], wv[:, kc % 2, m * 128:(m + 1) * 128], src[:, kc, n0:n0 + nn], kc == 0, kc == kchunks - 1,
                       [wb_, srcb], [pb])
            for (n0, nn), (pt, pb) in zip(NTILES, pts):
                resid_add(pt, pb, m, n0, nn, M["g"] + m)

    def ffn(l):
        bt, gt = x_fb[l]
        fw.dma(SP, bt.ap().rearrange("p (k t) -> p k t", t=2), x[:, :, TP - 2:TP], R=[xb[c][1] for c in range(KC)], W=[dbuf(bt)])
        fw.allgather(bt.ap().opt(), gt.ap().opt(), R=[dbuf(bt)], W=[dbuf(gt)])
        fw.dma(SP, xnb[:], gt.ap()[0:128, :].rearrange("p (k t) -> p k t", t=2), R=[dbuf(gt)], W=[xnbb])
        make_amod(vecl, veclb, 112)
        norm_mod(True)
        st = contextlib.ExitStack()
        uext = [sb(f"uext{i}", [128, UE], F32, st) for i in range(2)]
        acc = [sb(f"acc{i}", [128, 1064], F32, st) for i in range(2)]
        sa, sab = sb("sa", [128, 1064], F32, st)
        zg = [sb(f"zg{i}", [128, 4, NT], BF16, st) for i in range(2)]
        cst, cstb = sb("cst", [128, 64, 8], F32, st)
        convo, convob = sb("convo", [128, 64, 10], F32, st)
        extra_slot(st)
        fw.dma(SP, cst[:], d_convs[l, :, :, :], W=[cstb])
        for i in range(2):
            fw.op(DVE, lambda i=i: nc.vector.memset(uext[i][0][:], 0.0), W=[uext[i][1]])
        def down(gi):
            zt_, zb_ = zg[gi % 2]
            fw.tag = "ffn.down"
            pieces = [WNEXT(("r", "ffn_w_down", l, gi * 512 + r * 256)) for r in range(2)]
            proj_rows(pieces, 4, zt_, zb_)

        for grp in range(8):
            zt, zb = zg[grp % 2]
            for jj in range(4):
                j = grp * 4 + jj
                if jj == 1 and grp > 0:
                    down(grp - 1)
                fw.tag = "ffn.up"
                ada_tick((j + 1) / 30.0)
                wt, wb_ = WNEXT(("c2", "ffn_w_up", l, j))
                wv = wt[:, :].rearrange("p (k c) -> p k c", c=256)
                for half in range(2):
                    ut, ub = uext[half]
                    at, ab = acc[half]
                    ch = half * 32 + j
                    w0 = vecl[:, 128 + ch:129 + ch]
                    w1 = vecl[:, 192 + ch:193 + ch]
                    w2 = vecl[:, 256 + ch:257 + ch]
                    cb = vecl[:, 320 + ch:321 + ch]
                    cp(ACT, ut[:, 1026:1066].rearrange("p (s t) -> p s t", t=10)[:, :, 0:2],
                       cst[:, ch, :].rearrange("p (s t) -> p s t", t=2), [cstb], [ub])
                    for (n0, nn) in NTILES_X:
                        pt, pb = PF()
                        for k in range(KC):
                            mm(pt[:, 0:nn], wv[:, k, half * 128:(half + 1) * 128], hT[:, k, n0:n0 + nn], k == 0, k == KC - 1,
                               [wb_, hb[k]], [pb])
                        if n0 < TP:
                            cp(ACT, ut[:, 2 + n0:2 + n0 + nn], pt[:, 0:nn], [pb], [ub])
                        else:
                            cp(ACT, ut[:, 1026:1066].rearrange("p (s t) -> p s t", t=10)[:, :, 2:10],
                               pt[:, 0:32].rearrange("p (s t) -> p s t", t=8), [pb], [ub])
                            ts(DVE, ut[:, 0:2], pt[:, 32:34], flag[:, 0:1], None, ALU.mult, None, [pb, flagb], [ub])
                    ts(DVE, at[:, :], ut[:, 2:1066], w2, cb, ALU.mult, ALU.add, [ub, veclb], [ab])
                    stt(DVE, at[:, :], ut[:, 1:1065], w1, at[:, :], ALU.mult, ALU.add, [ub, veclb, ab], [ab])
                    stt(DVE, at[:, :], ut[:, 0:1064], w0, at[:, :], ALU.mult, ALU.add, [ub, veclb, ab], [ab])
                    cp(ACT, convo[:, ch, :].rearrange("p (s t) -> p s t", t=2),
                       ut[:, 1024:1074].rearrange("p (s t) -> p s t", t=10)[:, :, 0:2], [ub], [convob])
                a_t, a_b = acc[0]
                b_t, b_b = acc[1]
                act(sa[:, :], a_t[:, :], AF.Silu, [a_b], [sab])
                tt(DVE, zt[:, jj, 0:TP], sa[:, 0:TP], b_t[:, 0:TP], ALU.mult, [sab, b_b], [zb])
                tt(DVE, zt[:, jj, TP:NT].rearrange("p (s t) -> p s t", t=8),
                   sa[:, 1024:1064].rearrange("p (s t) -> p s t", t=10)[:, :, 2:10],
                   b_t[:, 1024:1064].rearrange("p (s t) -> p s t", t=10)[:, :, 2:10], ALU.mult, [sab, b_b], [zb])
        down(7)
        ada_flush()
        fw.dma(SP, o_conv[l, :, :], convo[:, :, :].rearrange("p c t -> p (c t)"), R=[convob])
        fw.barrier()
        wcur["l"] = list(wsl)
        st.close()

    def retention(l):
        make_amod(vecl, veclb, 96)
        norm_mod(False)
        st = contextlib.ExitStack()
        rope, ropeb = sb("rope", [128, 2, NT], F32, st)
        cmask, cmaskb = sb("cmask", [128, 160], F32, st)
        rsc, rscb = sb("rsc", [128, 64], F32, st)
        qd, qdb = sb("qd", [128, 160], F32, st)
        q2T, q2b = sb("q2T", [128, 2, NT], BF16, st)
        kT, kTb = sb("kT", [128, 2, NT], BF16, st)
        ktok, ktokb = sb("ktok", [128, 9, 256], BF16, st)
        ktf = [sb(f"ktf{i}", [128, 256], BF16, st) for i in range(2)]
        v, _ = sb("v", [128, 9, 512], BF16, st)
        vbs = [Buf(f"v{t}") for t in range(9)]
        gnsg, gnsgb = sb("gnsg", [128, 9, 512], BF16, st)
        S, _ = sb("S", [128, 2, 512], F32, st)
        Sb, _ = sb("Sb", [128, 2, 512], BF16, st)
        S2, _ = sb("S2", [128, 2, 512], F32, st)
        Sbuf = [Buf("S_0"), Buf("S_1")]
        Sbb = [Buf("Sb_0"), Buf("Sb_1")]
        S2buf = [Buf("S2_0"), Buf("S2_1")]
        gT, gTb = sb("gT", [128, 4, NT], BF16, st)
        ra = [sb(f"ra{i}", [128, 512], F32, st) for i in range(2)]
        scr = [dict(PT=sb(f"PT{i}", [128, 128], BF16, st), gated=sb(f"gated{i}", [128, 512], BF16, st),
                    st6=sb(f"st6{i}", [128, 8], F32, st), tn=ra[i]) for i in range(2)]
        q2m, q2mb = sb("q2m", [128, 2, 32], BF16, st)
        ktm, ktmb = sb("ktm", [32, 256], BF16, st)
        fw.dma(SP, rope[:], d_ropeR[:, :, :], W=[ropeb])
        fw.dma(SP, cmask[:], d_cmask[:, :], W=[cmaskb])
        fw.dma(SP, rsc[:], d_rsc[:, :], W=[rscb])
        cosT = rope[:, 0, :]
        sinT = rope[:, 1, :]

        def rope_proj(wt, wb_, out_t, out_b, scale_q):
            wv = wt[:, :].rearrange("p (k c) -> p k c", c=256)
            for (n0, nn) in NTILES:
                p1, p1b = PF()
                p2, p2b = PF()
                for mc, (pt, pb) in enumerate(((p1, p1b), (p2, p2b))):
                    for k in range(KC):
                        mm(pt[:, 0:nn], wv[:, k, mc * 128:(mc + 1) * 128], hT[:, k, n0:n0 + nn], k == 0, k == KC - 1,
                           [wb_, hb[k]], [pb])
                a_, ab_ = ra[0]
                b_, bb_ = ra[1]
                cs = cosT[:, n0:n0 + nn]
                sn = sinT[:, n0:n0 + nn]
                if nn == 512:
                    qdv = qd[:, 0:128].unsqueeze(1).to_broadcast([128, 4, 128])
                    av = a_[:, 0:512].rearrange("p (c t) -> p c t", t=128)
                else:
                    qdv = qd[:, 128:160]
                    av = a_[:, 0:32]
                for half in range(2):
                    pa, pab = (p1, p1b) if half == 0 else (p2, p2b)
                    pc, pcb = (p2, p2b) if half == 0 else (p1, p1b)
                    tt(DVE, a_[:, 0:nn], pa[:, 0:nn], cs, ALU.mult, [pab, ropeb], [ab_])
                    tt(DVE, b_[:, 0:nn], pc[:, 0:nn], sn, ALU.mult, [pcb, ropeb], [bb_])
                    op = ALU.subtract if half == 0 else ALU.add
                    if scale_q:
                        tt(DVE, a_[:, 0:nn], a_[:, 0:nn], b_[:, 0:nn], op, [ab_, bb_], [ab_])
                        ov = out_t[:, half, n0:n0 + nn]
                        if nn == 512:
                            ov = ov.rearrange("p (c t) -> p c t", t=128)
                        tt(DVE, ov, av, qdv, ALU.mult, [ab_, qdb], [out_b])
                    else:
                        tt(DVE, out_t[:, half, n0:n0 + nn], a_[:, 0:nn], b_[:, 0:nn], op, [ab_, bb_], [out_b])

        def tok_proj(evac, c0w):
            for half in range(2):
                wt, wb_ = WNEXT(("c", "ret_w_in", l, c0w + half * 256))
                wv = wt[:, :].rearrange("p (k c) -> p k c", c=256)
                for t, (c0, cn) in enumerate(TTILES):
                    pt, pb = PF()
                    for k in range(KC):
                        mm(pt[0:cn, 0:256], hT[:, k, c0:c0 + cn], wv[:, k, :], k == 0, k == KC - 1, [wb_, hb[k]], [pb])
                    evac(t, cn, half, pt, pb)

        for h in range(RH):
            gam = GAM[h]
            kdec_c = rsc[:, h:h + 1]
            ksc_c = rsc[:, 8 + h:9 + h]
            kdec_s = rsc[0:32, 16 + h:17 + h]
            ksc_s = rsc[0:32, 24 + h:25 + h]
            fw.dma(SP, qd[:], d_qdec[h, :, :], W=[qdb])
            sbufs = [(S2, S2buf), (S, Sbuf), (S2, S2buf), (S, Sbuf)]
            fw.dma(SP, S2[:], d_sret[l, 0, h, :, :].rearrange("(c p) e -> p c e", p=128), W=S2buf)
            fw.tag = "ret.K"
            ada_tick((h + 0.3) / 7.5)
            wt, wb_ = WNEXT(("c", "ret_w_in", l, 2048 + h * 256))
            rope_proj(wt, wb_, kT, kTb, False)
            fw.tag = "ret.ktr"
            for t, (c0, cn) in enumerate(TTILES):
                pt, pb = PB()
                for dc in range(2):
                    tr(pt[0:cn, dc * 128:(dc + 1) * 128], kT[:, dc, c0:c0 + cn], [kTb], [pb], sig=(dc == 1))
                act(ktok[0:cn, t, :], pt[0:cn, 0:256], AF.Identity, [pb, rscb], [ktokb], scale=(kdec_c if cn == 128 else kdec_s))
            fw.tag = "ret.V"
            pS = [PF(), PF()]
            for _, pbb in pS:
                reserved.add(bank_index(pbb))

            def evac_v(t, cn, half, pt, pb):
                cp(ACT, v[0:cn, t, half * 256:(half + 1) * 256], pt[0:cn, 0:256], [pb], [vbs[t]])
                if half == 1 and t >= 1:
                    tl = t - 1
                    ptag = fw.tag
                    fw.tag = "ret.Sloc"
                    kf, kfb = ktf[tl % 2]
                    ts(DVE, kf[:, :], ktok[:, tl, :], float(gam ** (128 * (7 - tl))), None, ALU.mult, None, [ktokb], [kfb])
                    for dc in range(2):
                        mm(pS[dc][0][:, :], kf[:, dc * 128:(dc + 1) * 128], v[:, tl, :], tl == 0, tl == 7, [kfb, vbs[tl]], [pS[dc][1]], sig=True)
                    fw.tag = ptag

            tok_proj(evac_v, 4096 + h * 512)
            ada_tick((h + 0.6) / 7.5)
            wq_pre = WNEXT(("c", "ret_w_in", l, h * 256))
            fw.tag = "ret.Sloc"
            for dc in range(2):
                cp(ACT, S[:, dc, :], pS[dc][0][:, :], [pS[dc][1]], [Sbuf[dc]])
            for _, pbb in pS:
                reserved.discard(bank_index(pbb))
            bt, gt = x_sb[l][h]
            fw.dma(SP, bt.ap().rearrange("p (c e) -> p c e", c=2), S[:], R=Sbuf, W=[dbuf(bt)])
            fw.allgather(bt.ap().opt(), gt.ap().opt(), R=[dbuf(bt)], W=[dbuf(gt)])
            fw.dma(SP, S[:], gt.ap()[0:128, :].rearrange("p (c e) -> p c e", c=2), R=[dbuf(gt)], W=Sbuf)
            for dc in range(2):
                ts(DVE, S[:, dc, :], S[:, dc, :], flag[:, 0:1], None, ALU.mult, None, [Sbuf[dc], flagb], [Sbuf[dc]])
                cp(ACT, Sb[:, dc, :], S[:, dc, :], [Sbuf[dc]], [Sbb[dc]])
            fw.tag = "ret.Q"
            wt, wb_ = wq_pre
            rope_proj(wt, wb_, q2T, q2b, True)
            fw.tag = "ret.G"
            tok_proj(lambda t, cn, half, pt, pb: act(gnsg[0:cn, t, half * 256:(half + 1) * 256], pt[0:cn, 0:256], AF.Silu, [pb], [gnsgb]),
                     8192 + h * 512)
            ada_tick((h + 1.0) / 7.5)

            def gn_chain(t, cn, po, pob, sc_):
                (st6_, st6b_), (tn_, tnb_), (gated_, gatedb_) = sc_["st6"], sc_["tn"], sc_["gated"]
                fw.op(DVE, lambda: nc.vector.bn_stats(st6_[0:cn, 0:6], po[0:cn, :]), R=[pob], W=[st6b_])
                fw.op(DVE, lambda: nc.vector.bn_aggr(st6_[0:cn, 6:8], st6_[0:cn, 0:6]), R=[st6b_], W=[st6b_])
                ts(DVE, st6_[0:cn, 7:8], st6_[0:cn, 7:8], EPS, None, ALU.add, None, [st6b_], [st6b_])
                act(st6_[0:cn, 7:8], st6_[0:cn, 7:8], AF.Sqrt, [st6b_], [st6b_])
                fw.op(DVE, lambda: nc.vector.reciprocal(st6_[0:cn, 7:8], st6_[0:cn, 7:8]), R=[st6b_], W=[st6b_])
                ts(DVE, tn_[0:cn, :], po[0:cn, :], st6_[0:cn, 6:7], st6_[0:cn, 7:8], ALU.subtract, ALU.mult, [pob, st6b_], [tnb_])
                tt(DVE, gated_[0:cn, :], tn_[0:cn, :], gnsg[0:cn, t, :], ALU.mult, [tnb_, gnsgb], [gatedb_])

            def stage_b(t, c0, cn, sc_):
                gated_, gatedb_ = sc_["gated"]
                pt, pb = PB()
                for ec in range(4):
                    tr(pt[:, ec * 128:ec * 128 + cn], gated_[0:cn, ec * 128:(ec + 1) * 128], [gatedb_], [pb], sig=(ec == 3))
                for ec in range(4):
                    gcol = vecl[:, 384 + h * 4 + ec:385 + h * 4 + ec]
                    act(gT[:, ec, c0:c0 + cn], pt[:, ec * 128:ec * 128 + cn], AF.Identity, [pb, veclb], [gTb], scale=gcol)

            fw.tag = "ret.scan"
            pend = None
            for t in range(8):
                c0 = t * 128
                sc_ = scr[t % 2]
                PT_, PTb_ = sc_["PT"]
                psc, pscb = PF()
                pS = [PF(), PF()]
                po, pob = PF()
                for dc in range(2):
                    mm(psc[:, 0:128], kT[:, dc, c0:c0 + 128], q2T[:, dc, c0:c0 + 128], dc == 0, dc == 1, [kTb, q2b], [pscb])
                stt(DVE, PT_[:, :], psc[:, 0:128], ksc_c, cmask[:, 0:128], ALU.mult, ALU.mult, [pscb, rscb, cmaskb], [PTb_])
                for dc in range(2):
                    mm(pS[dc][0][:, :], ktok[:, t, dc * 128:(dc + 1) * 128], v[:, t, :], True, True, [ktokb, vbs[t]], [pS[dc][1]])
                mm(po[:, :], PT_[:, :], v[:, t, :], True, False, [PTb_, vbs[t]], [pob])
                for dc in range(2):
                    mm(po[:, :], q2T[:, dc, c0:c0 + 128], Sb[:, dc, :], False, dc == 1, [q2b, Sbb[dc]], [pob])
                for dc in range(2):
                    stt(DVE, S[:, dc, :], S[:, dc, :], float(gam ** 128), pS[dc][0][:, :], ALU.mult, ALU.add, [Sbuf[dc], pS[dc][1]], [Sbuf[dc]])
                    cp(ACT, Sb[:, dc, :], S[:, dc, :], [Sbuf[dc]], [Sbb[dc]])
                gn_chain(t, 128, po, pob, sc_)
                if pend is not None:
                    stage_b(*pend)
                pend = (t, c0, 128, sc_)
            fw.dma(SP, o_retp[l, h, :, :].rearrange("(c p) e -> p c e", p=128), S[:], R=Sbuf)

            fw.tag = "ret.samp"
            c0 = TP
            sc_ = scr[0]
            PT_, PTb_ = sc_["PT"]
            psc, pscb = PF()
            po, pob = PF()
            pS = [PF(), PF()]
            for dc in range(2):
                mm(psc[0:32, 0:32], kT[:, dc, c0:c0 + 32], q2T[:, dc, c0:c0 + 32], dc == 0, dc == 1, [kTb, q2b], [pscb])
            stt(DVE, PT_[0:32, 0:32], psc[0:32, 0:32], ksc_s, cmask[0:32, 128:160], ALU.mult, ALU.mult, [pscb, rscb, cmaskb], [PTb_])
            mm(po[0:32, :], PT_[0:32, 0:32], v[0:32, 8, :], True, False, [PTb_, vbs[8]], [pob])
            stage_b(*pend)
            for s in range(4):
                Sx, Sxb = sbufs[s]
                if s + 1 < 4:
                    Sn, Snb = sbufs[s + 1]
                    fw.dma(SP, Sn[:], d_sret[l, s + 1, h, :, :].rearrange("(c p) e -> p c e", p=128), W=Snb)
                for dc in range(2):
                    cp(ACT, Sb[:, dc, :], Sx[:, dc, :], [Sxb[dc]], [Sbb[dc]])
                fw.op(DVE, lambda: nc.vector.memset(q2m[:], 0.0), W=[q2mb])
                cp(DVE, q2m[:, :, 8 * s:8 * s + 8], q2T[:, :, c0 + 8 * s:c0 + 8 * s + 8], [q2b], [q2mb])
                for dc in range(2):
                    mm(po[0:32, :], q2m[:, dc, :], Sb[:, dc, :], False, (s == 3 and dc == 1), [q2mb, Sbb[dc]], [pob])
                ts(DVE, ktm[:, :], ktok[0:32, 8, :], rsc[0:32, 32 + s:33 + s], None, ALU.mult, None, [ktokb, rscb], [ktmb])
                for dc in range(2):
                    mm(pS[dc][0][:, :], ktm[:, dc * 128:(dc + 1) * 128], v[0:32, 8, :], True, True, [ktmb, vbs[8]], [pS[dc][1]])
                for dc in range(2):
                    stt(DVE, Sx[:, dc, :], Sx[:, dc, :], float(gam ** 8), pS[dc][0][:, :], ALU.mult, ALU.add, [Sxb[dc], pS[dc][1]], [Sxb[dc]])
                fw.dma(SP, o_rets[l, s, h, :, :].rearrange("(c p) e -> p c e", p=128), Sx[:], R=Sxb)
            gn_chain(8, 32, po, pob, sc_)
            stage_b(8, c0, 32, sc_)
            fw.tag = "ret.out"
            if h == 0:
                ada_finish(att["gatejob"])
            pieces = [WNEXT(("r", "ret_w_out", l, h * 512 + r * 256)) for r in range(2)]
            proj_rows(pieces, 4, gT, gTb)
        ada_flush()
        fw.barrier()
        st.close()

    att = {}

    def rope_apply(buf_t, buf_b, nh, t, cn, ropeA, ropeAb, rt):
        sv = buf_t[0:cn, 0:nh * 64].rearrange("p (h d) -> p h d", d=64)
        x1 = sv[:, :, 0:8]
        x2 = sv[:, :, 8:16]
        cs = ropeA[0:cn, t, 0, :].unsqueeze(1).to_broadcast([cn, nh, 8])
        sn = ropeA[0:cn, t, 1, :].unsqueeze(1).to_broadcast([cn, nh, 8])
        (r0, r0b), (r1, r1b) = rt
        a = r0[0:cn, 0, 0:nh, :]
        b = r0[0:cn, 1, 0:nh, :]
        c = r1[0:cn, 0, 0:nh, :]
        d = r1[0:cn, 1, 0:nh, :]
        tt(DVE, a, x1, cs, ALU.mult, [buf_b, ropeAb], [r0b])
        tt(DVE, b, x2, sn, ALU.mult, [buf_b, ropeAb], [r0b])
        tt(DVE, c, x2, cs, ALU.mult, [buf_b, ropeAb], [r1b])
        tt(DVE, d, x1, sn, ALU.mult, [buf_b, ropeAb], [r1b])
        tt(DVE, x1, a, b, ALU.subtract, [r0b], [buf_b])
        tt(DVE, x2, c, d, ALU.add, [r1b], [buf_b])

    def kv_stage(stk):
        kTa, kTab, vA, vAb = att["kTa"], att["kTab"], att["vA"], att["vAb"]
        saveM = dict(M)
        M["t"], M["b"] = att["kvmod"]
        M["g"] = None
        make_amod(vecg, vecgb, 0)
        norm_mod(False)
        M.update(saveM)
        st = contextlib.ExitStack()
        ropeA, ropeAb = sb("ropeA", [128, 9, 2, 8], F32, st)
        kf, kfb = sb("kf", [128, 512], F32, st)
        kr, krb = sb("kr", [128, 2, 256], BF16, st)
        rt = [sb(f"rt{i}", [128, 2, 8, 8], F32, st) for i in range(2)]
        fw.dma(SP, ropeA[:], d_ropeA[:, :, :, :], W=[ropeAb])
        fw.dma(POOL, kTa[:, :, 128 + TP:128 + TP + 512], d_ckT[:, :, :], W=[kTab])
        for s in range(4):
            fw.dma(POOL, vA[:, 10 + s, :], d_cv[:, s, :], W=[vAb])
        fw.tag = "kv"
        wk_ = WNEXT(("c", "w_kv", None, 0))
        wv_ = WNEXT(("c", "w_kv", None, 256))

        def finish_tile(tile_idx, keycol0, cn):
            cp(ACT, vA[0:cn, tile_idx, :], kf[0:cn, 256:512], [kfb], [vAb])
            for g in range(4):
                for dup in range(2):
                    o0 = (g % 2) * 128 + dup * 64
                    cp(DVE, kr[0:cn, g // 2, o0:o0 + 64], kf[0:cn, g * 64:(g + 1) * 64], [kfb], [krb])
            pt, pb = PB()
            for g in range(4):
                tr(pt[:, g * 128:g * 128 + cn], kr[0:cn, g // 2, (g % 2) * 128:(g % 2) * 128 + 128], [krb], [pb], sig=(g == 3))
            cp(ACT, kTa[:, :, keycol0:keycol0 + cn], pt[:, 0:512].rearrange("p (g c) -> p g c", c=128)[:, :, 0:cn], [pb], [kTab])

        bt, gt = x_kv
        for t, (c0, cn) in enumerate(TTILES):
            for half, (wt, wb_) in enumerate((wk_, wv_)):
                wv2 = wt[:, :].rearrange("p (k c) -> p k c", c=256)
                pt, pb = PF()
                for k in range(KC):
                    mm(pt[0:cn, 0:256], hT[:, k, c0:c0 + cn], wv2[:, k, :], k == 0, k == KC - 1, [wb_, hb[k]], [pb])
                cp(ACT, kf[0:cn, half * 256:(half + 1) * 256], pt[0:cn, 0:256], [pb], [kfb])
            rope_apply(kf, kfb, 4, t, cn, ropeA, ropeAb, rt)
            if t < 8:
                finish_tile(1 + t, 128 + c0, 128)
            else:
                finish_tile(9, 128 + TP + 512, 32)
            if t == 7:
                fw.dma(SP, o_wkp[:, :], kf[:, 0:256], R=[kfb])
                fw.dma(SP, o_wvp[:, :], kf[:, 256:512], R=[kfb])
                fw.dma(SP, bt.ap(), kf[:, :], R=[kfb], W=[dbuf(bt)])
                fw.allgather(bt.ap().opt(), gt.ap().opt(), R=[dbuf(bt)], W=[dbuf(gt)])
            if t == 8:
                for s in range(4):
                    fw.dma(SP, o_wks[s, 120:128, :], kf[8 * s:8 * s + 8, 0:256], R=[kfb])
                    fw.dma(SP, o_wvs[s, 120:128, :], kf[8 * s:8 * s + 8, 256:512], R=[kfb])
                    fw.dma(SP, o_wks[s, 0:120, :], d_cknat[s, 8:128, :])
                    fw.dma(SP, o_wvs[s, 0:120, :], d_cvnat[s, 8:128, :])
        fw.dma(SP, kf[:, :], gt.ap()[0:128, :], R=[dbuf(gt)], W=[kfb])
        finish_tile(0, 0, 128)
        fw.barrier()
        st.close()

    def attention(l, j):
        make_amod(vecl, veclb, 96)
        norm_mod(False)
        st = contextlib.ExitStack()
        kTa, kTab, vA, vAb = att["kTa"], att["kTab"], att["vA"], att["vAb"]
        ropeA, ropeAb = sb("ropeA", [128, 9, 2, 8], F32, st)
        rt = [sb(f"rt{i}", [128, 2, 8, 8], F32, st) for i in range(2)]
        qfs = [sb(f"qf{i}", [128, 512], F32, st) for i in range(2)]
        qbfs = [sb(f"qbf{i}", [128, 512], BF16, st) for i in range(2)]
        qT, qTb = sb("qT", [128, 4, NT], BF16, st)
        oT, oTb = sb("oT", [128, 4, NT], BF16, st)
        amask, amaskb = sb("amask", [128, 2, 256], F32, st)
        smask, smaskb = sb("smask", [32, 544], F32, st)
        amk, amkb = sb("amk", [128, 2, 256], BF16, st)
        smk, smkb = sb("smk", [32, 544], BF16, st)
        sinks, sinksb = sb("sinks", [128, 2, 32], F32, st)
        pes = [sb(f"pe{i}", [128, 4, 256], BF16, st) for i in range(2)]
        pTs = [sb(f"pT{i}", [128, 8, 128], BF16, st) for i in range(2)]
        sms = [sb(f"sm{i}", [128, 32], F32, st) for i in range(2)]
        pss = [sb(f"pss{i}", [32, 544], BF16, st) for i in range(2)]
        fw.dma(SP, ropeA[:], d_ropeA[:, :, :, :], W=[ropeAb])
        fw.dma(SP, amask[:], d_amask[:, :, :], W=[amaskb])
        fw.dma(SP, smask[:], d_smask[:, :], W=[smaskb])
        fw.dma(SP, sinks[:], d_sinks[:, :, :], W=[sinksb])
        extra_slot(st)
        cp(DVE, amk[:], amask[:], [amaskb], [amkb])
        cp(DVE, smk[:], smask[:], [smaskb], [smkb])
        ctr = {"i": 0}

        for g in range(4):
            fw.tag = "att.q"
            ada_tick((g + 0.5) / 3.6)
            wq = [WNEXT(("c", "att_w_q", j, g * 512 + half * 256)) for half in range(2)]
            def q_tr(t, c0, cn):
                qbf, qbfb = qbfs[t % 2]
                pt, pb = PB()
                for cc in range(4):
                    tr(pt[:, cc * 128:cc * 128 + cn], qbf[0:cn, cc * 128:(cc + 1) * 128], [qbfb], [pb], sig=(cc == 3))
                cp(ACT, qT[:, :, c0:c0 + cn], pt[:, 0:512].rearrange("p (c t) -> p c t", t=128)[:, :, 0:cn], [pb], [qTb])

            qpend = None
            for t, (c0, cn) in enumerate(TTILES):
                qf, qfb = qfs[t % 2]
                qbf, qbfb = qbfs[t % 2]
                for half in range(2):
                    wt, wb_ = wq[half]
                    wv2 = wt[:, :].rearrange("p (k c) -> p k c", c=256)
                    pt, pb = PF()
                    for k in range(KC):
                        mm(pt[0:cn, 0:256], hT[:, k, c0:c0 + cn], wv2[:, k, :], k == 0, k == KC - 1, [wb_, hb[k]], [pb])
                    cp(ACT, qf[0:cn, half * 256:(half + 1) * 256], pt[0:cn, 0:256], [pb], [qfb])
                rope_apply(qf, qfb, 8, t, cn, ropeA, ropeAb, rt)
                cp(ACT, qbf[0:cn, :], qf[0:cn, :], [qfb], [qbfb])
                if qpend is not None:
                    q_tr(*qpend)
                qpend = (t, c0, cn)
            q_tr(*qpend)
            def stage_a(t, bh):
                c0 = t * 128
                wm = 0 if t == 0 else 1
                ix = ctr["i"] % 2
                ctr["i"] += 1
                pe_, peb = pes[ix]
                sm, smb = sms[ix]
                banks = [PF(), PF()]
                for i, (ps, psb_) in enumerate(banks):
                    hp = 2 * bh + i
                    for hh in range(2):
                        base = hh * 64
                        mm(ps[:, hh * 256:(hh + 1) * 256], qT[base:base + 64, hp, c0:c0 + 128], kTa[base:base + 64, g, c0:c0 + 256],
                           True, False, [qTb, kTab], [psb_])
                        mm(ps[:, hh * 256:(hh + 1) * 256], identb[:, :], amk[:, wm, :], False, True, [identbb, amkb], [psb_])
                h0 = g * 8 + 4 * bh
                sk4 = sinks[:, j, h0:h0 + 4]
                for i, (ps, psb_) in enumerate(banks):
                    fw.op(DVE, lambda ps=ps, i=i: nc.vector.tensor_reduce(
                        sm[:, 2 * i:2 * i + 2], ps[:, 0:512].rearrange("p (h k) -> p h k", k=256), mybir.AxisListType.X, ALU.max),
                        R=[psb_], W=[smb])
                stt(DVE, sm[:, 4:8], sm[:, 0:4], 0.125, sk4, ALU.mult, ALU.max, [smb, sinksb], [smb])
                ts(DVE, sm[:, 8:12], sm[:, 4:8], -1.0, None, ALU.mult, None, [smb], [smb])
                tt(DVE, sm[:, 16:20], sk4, sm[:, 4:8], ALU.subtract, [sinksb, smb], [smb])
                for i, (ps, psb_) in enumerate(banks):
                    for hh in range(2):
                        hd = 2 * i + hh
                        act(pe_[:, hd, :], ps[:, hh * 256:(hh + 1) * 256], AF.Exp, [psb_, smb], [peb, smb],
                            bias=sm[:, 8 + hd:9 + hd], scale=0.125, accum=sm[:, 12 + hd:13 + hd])
                act(sm[:, 20:24], sm[:, 16:20], AF.Exp, [smb], [smb])
                tt(DVE, sm[:, 24:28], sm[:, 12:16], sm[:, 20:24], ALU.add, [smb], [smb])
                fw.op(DVE, lambda: nc.vector.reciprocal(sm[:, 28:32], sm[:, 24:28]), R=[smb], W=[smb])
                tt(DVE, pe_[:, :, :], pe_[:, :, :], sm[:, 28:32].unsqueeze(2).to_broadcast([128, 4, 256]), ALU.mult, [peb, smb], [peb])
                return (t, bh, ix)

            def stage_bb(t, bh, ix):
                c0 = t * 128
                pe_, peb = pes[ix]
                pT, pTb = pTs[ix]
                ptT, ptTb = PB()
                for hd in range(4):
                    for kb in range(2):
                        o0 = (hd * 2 + kb) * 128
                        tr(ptT[:, o0:o0 + 128], pe_[:, hd, kb * 128:(kb + 1) * 128], [peb], [ptTb], sig=(hd == 3 and kb == 1))
                cp(ACT, pT[:, :, :], ptT[:, 0:1024].rearrange("p (a q) -> p a q", q=128), [ptTb], [pTb])
                for i in range(2):
                    hp = 2 * bh + i
                    po, pob = PF()
                    for hh in range(2):
                        hd = 2 * i + hh
                        for kb in range(2):
                            mm(po[hh * 64:(hh + 1) * 64, 0:128], vA[:, t + kb, g * 64:(g + 1) * 64], pT[:, hd * 2 + kb, :], kb == 0, kb == 1,
                               [vAb, pTb], [pob], tp=(0, hh * 64))
                    cp(ACT, oT[:, hp, c0:c0 + 128], po[:, 0:128], [pob], [oTb])

            fw.tag = "att.blk"
            pend = None
            for t in range(8):
                for bh in range(2):
                    cur = stage_a(t, bh)
                    if pend is not None:
                        stage_bb(*pend)
                    pend = cur
            stage_bb(*pend)
            ada_tick((g + 1.0) / 3.6)
            fw.tag = "att.samp"
            c0 = TP
            K0 = 128 + TP

            def samp_a(hp, hh):
                ix = ctr["i"] % 2
                ctr["i"] += 1
                pn, pnb = pss[ix]
                sm, smb = sms[ix]
                hq = g * 8 + hp * 2 + hh
                base = hh * 64
                ps, psb_ = PF()
                ps2, ps2b = PF()
                mm(ps[0:32, 0:512], qT[base:base + 64, hp, c0:c0 + 32], kTa[base:base + 64, g, K0:K0 + 512], True, False, [qTb, kTab], [psb_])
                mm(ps[0:32, 0:512], identb[0:32, 0:32], smk[0:32, 0:512], False, True, [identbb, smkb], [psb_])
                mm(ps2[0:32, 0:32], qT[base:base + 64, hp, c0:c0 + 32], kTa[base:base + 64, g, K0 + 512:K0 + 544], True, False, [qTb, kTab], [ps2b])
                mm(ps2[0:32, 0:32], identb[0:32, 0:32], smk[0:32, 512:544], False, True, [identbb, smkb], [ps2b])
                fw.op(DVE, lambda: nc.vector.tensor_reduce(sm[0:32, 0:1], ps[0:32, 0:512], mybir.AxisListType.X, ALU.max), R=[psb_], W=[smb])
                fw.op(DVE, lambda: nc.vector.tensor_reduce(sm[0:32, 1:2], ps2[0:32, 0:32], mybir.AxisListType.X, ALU.max), R=[ps2b], W=[smb])
                tt(DVE, sm[0:32, 2:3], sm[0:32, 0:1], sm[0:32, 1:2], ALU.max, [smb], [smb])
                ts(DVE, sm[0:32, 4:5], sm[0:32, 2:3], 0.125, sinks[0:32, j, hq:hq + 1], ALU.mult, ALU.max, [smb, sinksb], [smb])
                ts(DVE, sm[0:32, 8:9], sm[0:32, 4:5], -1.0, None, ALU.mult, None, [smb], [smb])
                act(pn[0:32, 0:512], ps[0:32, 0:512], AF.Exp, [psb_, smb], [pnb, smb], bias=sm[0:32, 8:9], scale=0.125, accum=sm[0:32, 12:13])
                act(pn[0:32, 512:544], ps2[0:32, 0:32], AF.Exp, [ps2b, smb], [pnb, smb], bias=sm[0:32, 8:9], scale=0.125, accum=sm[0:32, 13:14])
                tt(DVE, sm[0:32, 16:17], sinks[0:32, j, hq:hq + 1], sm[0:32, 4:5], ALU.subtract, [sinksb, smb], [smb])
                act(sm[0:32, 20:21], sm[0:32, 16:17], AF.Exp, [smb], [smb])
                tt(DVE, sm[0:32, 24:25], sm[0:32, 12:13], sm[0:32, 13:14], ALU.add, [smb], [smb])
                tt(DVE, sm[0:32, 25:26], sm[0:32, 24:25], sm[0:32, 20:21], ALU.add, [smb], [smb])
                fw.op(DVE, lambda: nc.vector.reciprocal(sm[0:32, 28:29], sm[0:32, 25:26]), R=[smb], W=[smb])
                ts(DVE, pn[0:32, :], pn[0:32, :], sm[0:32, 28:29], None, ALU.mult, None, [pnb, smb], [pnb])
                return (hp, hh, ix)

            spo = {}

            def samp_b(hp, hh, ix):
                pn, pnb = pss[ix]
                pT, pTb = pTs[ix]
                if hh == 0:
                    spo[hp] = PF()
                po, pob = spo[hp]
                ptT, ptTb = PB()
                for kb in range(5):
                    kn = 128 if kb < 4 else 32
                    tr(ptT[0:kn, kb * 32:(kb + 1) * 32], pn[0:32, kb * 128:kb * 128 + kn], [pnb], [ptTb], sig=(kb == 4))
                cp(ACT, pT[:, 0:4, 0:32], ptT[:, 0:128].rearrange("p (k q) -> p k q", q=32), [ptTb], [pTb])
                cp(ACT, pT[0:32, 4, 0:32], ptT[0:32, 128:160], [ptTb], [pTb])
                for kb in range(5):
                    kn = 128 if kb < 4 else 32
                    vt = vA[0:kn, (10 + kb) if kb < 4 else 9, g * 64:(g + 1) * 64]
                    mm(po[hh * 64:(hh + 1) * 64, 0:32], vt, pT[0:kn, kb, 0:32], kb == 0, kb == 4, [vAb, pTb], [pob], tp=(0, hh * 64))
                if hh == 1:
                    cp(ACT, oT[:, hp, c0:c0 + 32], po[:, 0:32], [pob], [oTb])

            pend = None
            for hp in range(4):
                for hh in range(2):
                    cur = samp_a(hp, hh)
                    if pend is not None:
                        samp_b(*pend)
                    pend = cur
            samp_b(*pend)
            fw.tag = "att.o"
            if g == 0:
                ada_finish(att["gatejob"])
            pieces = [WNEXT(("r", "att_w_o", j, g * 512 + r * 256)) for r in range(2)]
            proj_rows(pieces, 4, oT, oTb)
        ada_flush()
        fw.barrier()
        wcur["l"] = list(wsl)
        st.close()

    fw.dma(SP, bada[:], d_bada[:, :, :], W=[badab])
    kvstack = contextlib.ExitStack()
    kvmodstack = contextlib.ExitStack()
    ada_queue(ada_job_sa(0, 0))
    ada_flush()
    k = 0
    for l in range(DEPTH):
        fw.dma(SP, vecl[:], d_vecl[l, :, :], W=[veclb])
        M.update(t=modall, b=modallb, g=32 + 16 * (k % 2))
        att["gatejob"] = ada_job_gate(l, 0, k % 2)
        ada_queue(att["gatejob"])
        ada_queue(ada_job_std(l, 1, (k + 1) % 2))
        if l < N_A:
            retention(l)
        else:
            if l == N_A:
                kv_stage(kvstack)
                kvmodstack.close()
            attention(l, l - N_A)
        k += 1
        M.update(t=modall, b=modallb, g=32 + 16 * (k % 2))
        if l + 1 < DEPTH:
            if l + 1 == N_A:
                kTa, kTab = sb("kTa", [128, 4, 128 + TP + 512 + 32], BF16, kvstack)
                vA, vAb = sb("vA", [128, 14, 256], BF16, kvstack)
                att.update(dict(kTa=kTa, kTab=kTab, vA=vA, vAb=vAb))
                kvm, _ = sb("kvmod", [128, 32, 33], F32, kvmodstack)
                kvmb = [Buf(f"kvmod{j}") for j in range(32)]
                att["kvmod"] = (kvm, kvmb)
                ada_queue(AdaJob("kv_w_ada", None, 0, 32, kvm, kvmb, lambda j: (vecg[:, 16 + j:17 + j], vecgb)))
            ada_queue(ada_job_sa(l + 1, 0))
        ffn(l)
        k += 1
    kvstack.close()
    stk = contextlib.ExitStack()
    rstd, rstdb = rms_stats(stk, NT, False)
    for kc in range(KC):
        stt(DVE, x[:, kc, :], x[:, kc, :], vecg[:, 48 + kc:49 + kc], rstd[:, 0:NT], ALU.mult, ALU.mult, xb[kc] + [vecgb, rstdb], xb[kc])
        fw.dma(SP, o_yT[:, kc, :], x[:, kc, :], R=xb[kc])
    fw.finish()
    stk.close()
    es.close()
    assert wstate["n"] == NP, (wstate["n"], NP)
    ok, stuck, val = fw.simulate()
    assert ok, ("semaphore deadlock in emitted program", stuck)
    build_program.stats = {q.name: (q.cnt, len(q.log)) for q in fw.queues}
    build_program.maxsem = max(val.values())
    build_program.specs = specs
    build_program.tags = {q.name: q.tags for q in fw.queues}
    return nc


def _piece_cols(Wm, cols):
    sub = Wm[:, cols]
    return np.ascontiguousarray(sub.reshape(KC, 128, 256).transpose(1, 0, 2)).reshape(128, WCOLS)


def _piece_rows(Wm, r0):
    sub = Wm[r0:r0 + 256, :]
    return np.ascontiguousarray(sub.reshape(2, 128, 2048).transpose(1, 0, 2)).reshape(128, WCOLS)


def build_weights(inp, specs):
    out = np.empty((len(specs), 128, WCOLS), np.float32)
    for i, (kind, name, idx, a0) in enumerate(specs):
        Wm = inp[name] if idx is None else inp[name][idx]
        if kind == "c":
            out[i] = _piece_cols(Wm, np.arange(a0, a0 + 256))
        elif kind == "c2":
            out[i] = _piece_cols(Wm, np.concatenate([np.arange(a0 * 128, (a0 + 1) * 128), np.arange(4096 + a0 * 128, 4096 + (a0 + 1) * 128)]))
        else:
            out[i] = _piece_rows(Wm, a0)
    return out


def fm(vec):
    return np.ascontiguousarray(np.asarray(vec, np.float32).reshape(-1, 128).T)


def build_inputs(inp, specs):
    f32 = np.float32
    wts = build_weights(inp, specs)
    vecl = np.zeros((DEPTH, 128, 416), f32)
    for l in range(DEPTH):
        vecl[l, :, 0:96] = fm(inp["b_ada"][l])
        vecl[l, :, 96:112] = fm(inp["norm_mix"][l])
        vecl[l, :, 112:128] = fm(inp["norm_ffn"][l])
        for tap in range(3):
            vecl[l, :, 128 + tap * 64:128 + (tap + 1) * 64] = fm(inp["ffn_conv_w"][l, tap])
        vecl[l, :, 320:384] = fm(inp["ffn_conv_b"][l])
        if l < N_A:
            vecl[l, :, 384:416] = fm(inp["ret_gn"][l])
    bada = np.ascontiguousarray(np.stack([fm(inp["b_ada"][l]) for l in range(DEPTH)], 1))
    vecg = np.zeros((128, 64), f32)
    vecg[:, 0:16] = fm(inp["kv_norm"])
    vecg[:, 16:48] = fm(inp["kv_b_ada"])
    vecg[:, 48:64] = fm(inp["norm_f"])
    gam = np.array(GAM, np.float64)
    m = np.arange(128, dtype=np.float64)
    rsc = np.zeros((128, 64), np.float64)
    for h in range(RH):
        rsc[:, h] = gam[h] ** (127 - m) / 16.0
        rsc[:, 8 + h] = gam[h] ** (-(m + 1)) / 16.0
        ms = (np.arange(32) % 8).astype(np.float64)
        rsc[:32, 16 + h] = gam[h] ** (7 - ms) / 16.0
        rsc[:32, 24 + h] = gam[h] ** (-(ms + 1)) / 16.0
    for s in range(4):
        rsc[8 * s:8 * s + 8, 32 + s] = 1.0
    rsc = rsc.astype(f32)
    cmask = np.zeros((128, 160), f32)
    cmask[:, 0:128] = (m[:, None] <= m[None, :]).astype(f32)
    i32 = np.arange(32)
    cmask[:32, 128:160] = ((i32[:, None] // 8 == i32[None, :] // 8) & (i32[:, None] <= i32[None, :])).astype(f32)
    inv_r = (10000.0 ** (-np.arange(128, dtype=f32) * 2.0 / 256.0)).astype(f32)
    inv_a = (500000.0 ** (-np.arange(8, dtype=f32) * 2.0 / 16.0)).astype(f32)
    qi = np.arange(128)[:, None]
    kj = np.arange(256)[None, :]
    band = np.where(((kj < 128) & (kj >= qi)) | ((kj >= 128) & (kj - 128 <= qi)), 0.0, NEG).astype(f32)
    first = np.where((kj >= 128) & (kj - 128 <= qi), 0.0, NEG).astype(f32)
    sm_ = np.full((32, 544), NEG, f32)
    for s in range(4):
        for i in range(8):
            r = 8 * s + i
            for jx in range(136):
                if i <= jx <= 128 + i:
                    if jx < 128:
                        sm_[r, s * 128 + jx] = 0.0
                    else:
                        sm_[r, 512 + 8 * s + (jx - 128)] = 0.0
    sinks = np.ascontiguousarray(np.broadcast_to(np.asarray(inp["att_sinks"], f32).reshape(1, 2, 32), (128, 2, 32)))

    maps = []
    for c in range(NCORES):
        s_, half = c // 2, c % 2
        t0 = half * TP
        ss = slice(4 * c, 4 * c + 4)
        xcat = np.concatenate([inp["x_prompt"][s_, t0:t0 + TP, :], inp["x_sample"][ss].reshape(TS, D)], 0)
        xT = np.ascontiguousarray(xcat.reshape(NT, KC, 128).transpose(2, 1, 0))
        ccat = np.concatenate([inp["c_prompt"][s_:s_ + 1], inp["c_sample"][ss]], 0)
        cT = np.ascontiguousarray(ccat.reshape(5, KC, 128).transpose(2, 1, 0))
        pos = np.concatenate([np.arange(t0, t0 + TP), np.tile(PAST + np.arange(8), 4)]).astype(f32)
        ang = pos[None, :] * inv_r[:, None]
        ropeR = np.stack([np.cos(ang), np.sin(ang)], 1).astype(f32)
        lpos = np.concatenate([np.arange(128), np.tile(np.arange(8), 4)]).astype(np.float64)
        qdec = np.ascontiguousarray(np.broadcast_to((gam[:, None] ** (lpos[None, :] + 1.0))[:, None, :], (RH, 128, 160))).astype(f32)
        angA = pos[:, None] * inv_a[None, :]
        ropeA = np.zeros((128, 9, 2, 8), f32)
        for t, (c0, cn) in enumerate(TTILES):
            ropeA[:cn, t, 0, :] = np.cos(angA[c0:c0 + cn]).astype(f32)
            ropeA[:cn, t, 1, :] = np.sin(angA[c0:c0 + cn]).astype(f32)
        amask = np.stack([first if half == 0 else band, band], 1).astype(f32)
        ck = np.asarray(inp["cache_win_k"][ss], f32)
        cvv = np.asarray(inp["cache_win_v"][ss], f32)
        ckT = np.zeros((128, 4, 512), f32)
        for g in range(4):
            kt = ck[:, :, g, :].transpose(2, 0, 1).reshape(64, 512)
            ckT[0:64, g, :] = kt
            ckT[64:128, g, :] = kt
        cv = np.ascontiguousarray(cvv.reshape(4, 128, 256).transpose(1, 0, 2))
        convs = np.ascontiguousarray(
            np.asarray(inp["state_conv"][:, ss], f32).reshape(DEPTH, 4, 2, 64, 128).transpose(0, 4, 3, 1, 2)).reshape(DEPTH, 128, 64, 8)
        maps.append(dict(
            xT=xT, cT=cT, flag=np.full((128, 1), float(half), f32), wts=wts, vecl=vecl, vecg=vecg, ropeR=ropeR, qdec=qdec,
            rsc=rsc, cmask=cmask, sret=np.ascontiguousarray(inp["state_ret"][:, ss]), convs=convs, ropeA=ropeA,
            amask=amask, smask=sm_, ckT=ckT, cv=cv, cknat=np.ascontiguousarray(ck.reshape(4, 128, 256)),
            cvnat=np.ascontiguousarray(cvv.reshape(4, 128, 256)), sinks=sinks, ident=np.eye(128, dtype=f32), bada=bada))
    return maps


def assemble(res):
    f32 = np.float32
    y_p = np.zeros((4, 2048, D), f32)
    y_s = np.zeros((32, 8, D), f32)
    ret_p = np.zeros((N_A, 4, RH, 256, 512), f32)
    ret_s = np.zeros((N_A, 32, RH, 256, 512), f32)
    wk_p = np.zeros((4, 128, 4, 64), f32)
    wv_p = np.zeros((4, 128, 4, 64), f32)
    wk_s = np.zeros((32, 128, 4, 64), f32)
    wv_s = np.zeros((32, 128, 4, 64), f32)
    cv_p = np.zeros((DEPTH, 4, 2, 8192), f32)
    cv_s = np.zeros((DEPTH, 32, 2, 8192), f32)
    for c in range(NCORES):
        r = res[c]
        s_, half = c // 2, c % 2
        t0 = half * TP
        yt = r["yT"].transpose(2, 1, 0).reshape(NT, D)
        y_p[s_, t0:t0 + TP] = yt[:TP]
        y_s[4 * c:4 * c + 4] = yt[TP:].reshape(4, 8, D)
        ret_s[:, 4 * c:4 * c + 4] = r["rets"]
        wk_s[4 * c:4 * c + 4] = r["wks"].reshape(4, 128, 4, 64)
        wv_s[4 * c:4 * c + 4] = r["wvs"].reshape(4, 128, 4, 64)
        co = r["convo"].reshape(DEPTH, 128, 64, 5, 2).transpose(0, 3, 4, 2, 1).reshape(DEPTH, 5, 2, 8192)
        cv_s[:, 4 * c:4 * c + 4] = co[:, 1:5]
        if half == 1:
            ret_p[:, s_] = r["retp"]
            wk_p[s_] = r["wkp"].reshape(128, 4, 64)
            wv_p[s_] = r["wvp"].reshape(128, 4, 64)
            cv_p[:, s_] = co[:, 0]
    return (y_p, y_s, ret_p, ret_s, wk_p, wv_p, wk_s, wv_s, cv_p, cv_s)


_CACHE = {}


def kernel(**inputs):
    inp = {k: np.asarray(v) for k, v in inputs.items()}
    if "nc" not in _CACHE:
        _CACHE["nc"] = build_program()
    nc = _CACHE["nc"]
    maps = build_inputs(inp, build_program.specs)
    res = run_bass_kernel_spmd(nc, maps, core_ids=list(range(NCORES)))
    return assemble(res.results)
```

```python
import contextlib
import numpy as np
import concourse.bass as bass
import concourse.mybir as mybir
from concourse.bass_utils import run_bass_kernel_spmd

F32 = mybir.dt.float32
BF16 = mybir.dt.bfloat16
ALU = mybir.AluOpType
AF = mybir.ActivationFunctionType

D = 2048
KC = 16
TP = 1024
TS = 32
NT = TP + TS
NTX = NT + 2
DEPTH = 4
N_A = 2
RH = 8
PAST = 16384
EPS = 1e-6
NEG = -1e30
UE = 1074
NTILES = [(0, 512), (512, 512), (1024, 32)]
NTILES_X = [(0, 512), (512, 512), (1024, 34)]
TTILES = [(i * 128, 128) for i in range(8)] + [(1024, 32)]
GAM = [1.0 - 2.0 ** (-5.0 - h) for h in range(RH)]
NCORES = 8
WCOLS = 4096


class Buf:
    __slots__ = ("name", "w", "r")

    def __init__(self, name):
        self.name = name
        self.w = None
        self.r = {}


class Q:
    def __init__(self, fw, name, eng, is_pe=False):
        self.fw = fw
        self.name = name
        self.eng = eng
        self.is_pe = is_pe
        self.sid = fw.new_sem(name)
        self.cnt = 0
        self.seen = {}
        self.dsems = []
        self.dcnt = []
        self.drr = 0
        self.log = []
        self.tags = []

    def wait(self, ev, same_ok=False):
        if ev is None:
            return
        s, v = ev
        if s == self.sid and (same_ok or self.is_pe):
            return
        if self.seen.get(s, 0) >= v:
            return
        self.eng.wait_ge(self.fw.sems[s], v)
        self.seen[s] = v
        self.log.append(("w", s, v))


class FW:
    def __init__(self, nc):
        self.nc = nc
        self.sems = []
        self.tag = ""
        self.kind = ""
        self.pe = Q(self, "pe", nc.tensor, is_pe=True)
        self.act = Q(self, "act", nc.scalar)
        self.dve = Q(self, "dve", nc.vector)
        self.pool = Q(self, "pool", nc.gpsimd)
        self.sp = Q(self, "sp", nc.sync)
        for q, n in ((self.pool, 6), (self.sp, 10)):
            for i in range(n):
                q.dsems.append(self.new_sem(f"{q.name}_d{i}"))
                q.dcnt.append(0)
        self.cc_sid = self.new_sem("cc")
        self.cc_cnt = 0
        self.queues = [self.pe, self.act, self.dve, self.pool, self.sp]

    def new_sem(self, name):
        self.sems.append(self.nc.alloc_semaphore("s_" + name))
        return len(self.sems) - 1

    def _deps(self, q, R, W):
        for b in R:
            q.wait(b.w)
        for b in W:
            q.wait(b.w, same_ok=True)
            for s, v in b.r.items():
                q.wait((s, v), same_ok=True)

    def _commit(self, ev, R, W):
        s, v = ev
        for b in R:
            if b.r.get(s, 0) < v:
                b.r[s] = v
        for b in W:
            b.w = ev
            b.r = {}

    def op(self, q, emit, R=(), W=(), sig=True):
        self._deps(q, R, W)
        ins = emit()
        q.tags.append((self.tag, self.kind))
        ev = (q.sid, q.cnt + 1)
        if sig:
            ins.then_inc(self.sems[q.sid], 1)
            q.cnt += 1
            q.log.append(("i", q.sid, 1))
        self._commit(ev, R, W)
        return ev

    def dma(self, q, out, in_, R=(), W=()):
        self._deps(q, R, W)
        j = q.drr
        q.drr = (q.drr + 1) % len(q.dsems)
        sid = q.dsems[j]
        q.wait((sid, 16 * q.dcnt[j]))
        ins = q.eng.dma_start(out=out, in_=in_)
        q.dcnt[j] += 1
        ins.then_inc(self.sems[sid], 16)
        q.log.append(("i", sid, 16))
        ev = (sid, 16 * q.dcnt[j])
        self._commit(ev, R, W)
        return ev

    def allgather(self, in_ap, out_ap, R, W):
        q = self.pool
        self._deps(q, R, W)
        q.wait((self.cc_sid, self.cc_cnt))
        ins = self.nc.gpsimd.collective_compute(
            "AllGather", ALU.bypass, replica_groups=[[0, 1], [2, 3], [4, 5], [6, 7]],
            ins=[in_ap], outs=[out_ap])
        self.cc_cnt += 1
        ins.then_inc(self.sems[self.cc_sid], 1)
        q.log.append(("i", self.cc_sid, 1))
        ev = (self.cc_sid, self.cc_cnt)
        self._commit(ev, R, W)
        return ev

    def barrier(self):
        evs = []
        for q in self.queues:
            if q.cnt:
                evs.append((q.sid, q.cnt))
            for sid, c in zip(q.dsems, q.dcnt):
                if c:
                    evs.append((sid, 16 * c))
        if self.cc_cnt:
            evs.append((self.cc_sid, self.cc_cnt))
        for q in self.queues:
            for ev in evs:
                q.wait(ev, same_ok=True)

    def simulate(self):
        val = {}
        pos = {q.name: 0 for q in self.queues}
        progress = True
        while progress:
            progress = False
            for q in self.queues:
                p = pos[q.name]
                while p < len(q.log):
                    k, s_, v = q.log[p]
                    if k == "w":
                        if val.get(s_, 0) < v:
                            break
                    else:
                        val[s_] = val.get(s_, 0) + v
                    p += 1
                    progress = True
                pos[q.name] = p
        stuck = {q.name: (pos[q.name], len(q.log), q.log[pos[q.name]] if pos[q.name] < len(q.log) else None) for q in self.queues}
        ok = all(pos[q.name] == len(q.log) for q in self.queues)
        return ok, stuck, val

    def finish(self):
        q = self.sp
        for qq in self.queues:
            if qq.cnt:
                q.wait((qq.sid, qq.cnt), same_ok=True)
            for sid, c in zip(qq.dsems, qq.dcnt):
                if c:
                    q.wait((sid, 16 * c))
        if self.cc_cnt:
            q.wait((self.cc_sid, self.cc_cnt))


def n_weight_pieces():
    per_ret = 24 + RH * 8 + 24 + 8 * 6
    per_att = 24 + 4 * 4 + 24 + 8 * 6
    return 2 * per_ret + 16 + 2 + 2 * per_att


def build_program():
    nc = bass.Bass("TRN2", target_bir_lowering=False)
    fw = FW(nc)
    PE, ACT, DVE, POOL, SP = fw.pe, fw.act, fw.dve, fw.pool, fw.sp
    NP = n_weight_pieces()

    def din(name, shape, dt=F32):
        return nc.dram_tensor(name, list(shape), dt, kind="ExternalInput")

    def dout(name, shape, dt=F32):
        return nc.dram_tensor(name, list(shape), dt, kind="ExternalOutput")

    d_xT = din("xT", [128, KC, NT])
    d_cT = din("cT", [128, KC, 5])
    d_flag = din("flag", [128, 1])
    d_wts = din("wts", [NP, 128, WCOLS])
    d_vecl = din("vecl", [DEPTH, 128, 416])
    d_vecg = din("vecg", [128, 64])
    d_ropeR = din("ropeR", [128, 2, NT])
    d_qdec = din("qdec", [RH, 128, 160])
    d_rsc = din("rsc", [128, 64])
    d_cmask = din("cmask", [128, 160])
    d_sret = din("sret", [N_A, 4, RH, 256, 512])
    d_convs = din("convs", [DEPTH, 128, 64, 8])
    d_ropeA = din("ropeA", [128, 9, 2, 8])
    d_amask = din("amask", [128, 2, 256])
    d_smask = din("smask", [32, 544])
    d_ckT = din("ckT", [128, 4, 512])
    d_cv = din("cv", [128, 4, 256])
    d_cknat = din("cknat", [4, 128, 256])
    d_cvnat = din("cvnat", [4, 128, 256])
    d_sinks = din("sinks", [128, 2, 32])
    d_ident = din("ident", [128, 128])
    d_bada = din("bada", [128, DEPTH, 96])

    o_yT = dout("yT", [128, KC, NT])
    o_retp = dout("retp", [N_A, RH, 256, 512])
    o_rets = dout("rets", [N_A, 4, RH, 256, 512])
    o_wkp = dout("wkp", [128, 256])
    o_wvp = dout("wvp", [128, 256])
    o_wks = dout("wks", [4, 128, 256])
    o_wvs = dout("wvs", [4, 128, 256])
    o_conv = dout("convo", [DEPTH, 128, 640])

    x_sb = [[(nc.dram_tensor(f"xs_b{l}_{h}", [128, 1024], F32), nc.dram_tensor(f"xs_g{l}_{h}", [256, 1024], F32))
             for h in range(RH)] for l in range(N_A)]
    x_fb = [(nc.dram_tensor(f"xf_b{l}", [128, 32], F32), nc.dram_tensor(f"xf_g{l}", [256, 32], F32)) for l in range(DEPTH)]
    x_kv = (nc.dram_tensor("xkv_b", [128, 512], F32), nc.dram_tensor("xkv_g", [256, 512], F32))
    dbufs = {}

    def dbuf(t):
        if t.name not in dbufs:
            dbufs[t.name] = Buf(t.name)
        return dbufs[t.name]

    es = contextlib.ExitStack()

    uniq = [0]

    def sb(name, shape, dt, stack=None):
        uniq[0] += 1
        t = (stack or es).enter_context(nc.sbuf_tensor(f"s_{name}_{uniq[0]}", list(shape), dt))
        return t, Buf(name)

    x, _ = sb("x", [128, KC, NT], F32)
    xb = [[Buf(f"x{c}_{n}") for n in range(3)] for c in range(KC)]
    hT, _ = sb("hT", [128, KC, NTX], BF16)
    hb = [Buf(f"h{c}") for c in range(KC)]
    modall, _ = sb("modall", [128, 64, 33], F32)
    modallb = [Buf(f"mod_{j}") for j in range(64)]
    M = {"t": modall, "b": modallb, "g": 32}
    bada, badab = sb("bada", [128, DEPTH, 96], F32)
    NWS = 3
    wsl = [sb(f"w{i}", [128, WCOLS], BF16) for i in range(NWS)]
    vecl, veclb = sb("vecl", [128, 416], F32)
    vecg, vecgb = sb("vecg", [128, 64], F32)
    identb, identbb = sb("identb", [128, 128], BF16)
    identf, identfb = sb("identf", [128, 128], F32)
    onesb, onesbb = sb("onesb", [128, 128], BF16)
    csT, csTb = sb("csT", [128, KC, 33], BF16)
    cTs, cTsb = sb("cTs", [128, KC, 5], F32)
    flag, flagb = sb("flag", [128, 1], F32)
    xnb, xnbb = sb("xnb", [128, KC, 2], F32)
    small, smallb = sb("small", [128, 64], F32)

    psf = [(nc.alloc_psum_tensor(f"psf{i}", [128, 512], F32), Buf(f"psf{i}")) for i in range(6)]
    psb = [(nc.alloc_psum_tensor(f"psb{i}", [128, 1024], BF16), Buf(f"psb{i}")) for i in range(2)]
    rr = {"f": 0, "b": 0}

    reserved = set()

    def PF():
        while True:
            i = rr["f"]
            rr["f"] = (i + 1) % 6
            if i not in reserved:
                return psf[i]

    def bank_index(pb):
        return [b for _, b in psf].index(pb)

    def PB():
        i = rr["b"]
        rr["b"] = (i + 1) % 2
        return psb[i]

    wstate = {"n": 0}

    specs = []

    wcur = {"l": list(wsl)}

    def extra_slot(stack):
        if nc.sbuf_bytes_remaining >= 2 * WCOLS + 128:
            wcur["l"] = list(wsl) + [sb("wx", [128, WCOLS], BF16, stack)]

    def WNEXT(spec, cols=WCOLS):
        i = wstate["n"]
        wstate["n"] += 1
        specs.append(spec)
        t, b = wcur["l"][i % len(wcur["l"])]
        fw.dma(POOL, t[:, 0:cols], d_wts[i, :, 0:cols], W=[b])
        return t, b

    def mm(out, lhsT, rhs, start, stop, R, W, tp=None, sig=None):
        sg = stop if sig is None else sig
        fw.kind = 'm' if tp is None else 'p'
        if tp is None:
            fw.op(PE, lambda: nc.tensor.matmul(out, lhsT, rhs, start=start, stop=stop), R=R, W=W, sig=sg)
        else:
            fw.op(PE, lambda: nc.tensor.matmul(out, lhsT, rhs, start=start, stop=stop, tile_position=tp), R=R, W=W, sig=sg)

    def tr(out, in_, R, W, sig=True):
        kp = in_.shape[0]
        fw.kind = 't'
        fw.op(PE, lambda: nc.tensor.transpose(out, in_, identb[0:kp, 0:kp]), R=list(R) + [identbb], W=W, sig=sig)

    def act(out, in_, func, R, W, bias=0.0, scale=1.0, accum=None):
        if accum is None:
            return fw.op(ACT, lambda: nc.scalar.activation(out, in_, func, bias=bias, scale=scale), R=R, W=W)
        return fw.op(ACT, lambda: nc.scalar.activation(out, in_, func, bias=bias, scale=scale, accum_out=accum), R=R, W=W)

    def tt(q, out, a, b, op, R, W):
        return fw.op(q, lambda: q.eng.tensor_tensor(out, a, b, op), R=R, W=W)

    def ts(q, out, a, s1, s2, op0, op1, R, W):
        if s2 is None:
            return fw.op(q, lambda: q.eng.tensor_scalar(out, a, s1, None, op0), R=R, W=W)
        return fw.op(q, lambda: q.eng.tensor_scalar(out, a, s1, s2, op0, op1), R=R, W=W)

    def stt(q, out, a, s, b, op0, op1, R, W):
        return fw.op(q, lambda: q.eng.scalar_tensor_tensor(out, a, s, b, op0, op1), R=R, W=W)

    def cp(q, out, in_, R, W):
        if q is ACT:
            return act(out, in_, AF.Copy, R, W)
        return fw.op(q, lambda: q.eng.tensor_copy(out, in_), R=R, W=W)

    def ntile_of(c0):
        return 0 if c0 < 512 else (1 if c0 < 1024 else 2)

    for c in range(KC):
        fw.dma(SP, x[:, c, :], d_xT[:, c, :], W=xb[c])
    fw.dma(SP, cTs[:], d_cT[:, :, :], W=[cTsb])
    fw.dma(SP, flag[:], d_flag[:, :], W=[flagb])
    fw.dma(SP, vecg[:], d_vecg[:, :], W=[vecgb])
    fw.dma(SP, identf[:], d_ident[:, :], W=[identfb])
    fw.op(DVE, lambda: nc.vector.memset(onesb[:], 1.0), W=[onesbb])
    cp(DVE, identb[:], identf[:], [identfb], [identbb])
    act(csT[:, :, 0:1], cTs[:, :, 0:1], AF.Silu, [cTsb], [csTb])
    for s in range(4):
        for t in range(8):
            act(csT[:, :, 1 + 8 * s + t:2 + 8 * s + t], cTs[:, :, 1 + s:2 + s], AF.Silu, [cTsb], [csTb])

    class AdaJob:
        def __init__(self, wname, l, c0, nj, dst_t, dst_b, bias_fn, jmap=None):
            self.wname, self.l, self.c0, self.nj = wname, l, c0, nj
            self.dst_t, self.dst_b, self.bias_fn = dst_t, dst_b, bias_fn
            self.jmap = jmap or (lambda j: j)
            self.done = 0
            self.total = nj // 2

        def step(self):
            i = self.done
            self.done += 1
            prev_tag = fw.tag
            fw.tag = "adaln"
            wt, wb_ = WNEXT(("c", self.wname, self.l, self.c0 + i * 256))
            wv = wt[:, :].rearrange("p (k c) -> p k c", c=256)
            for mc in range(2):
                j = 2 * i + mc
                pt, pb = PF()
                for k in range(KC):
                    mm(pt[:, 0:33], wv[:, k, mc * 128:(mc + 1) * 128], csT[:, k, :], k == 0, k == KC - 1, [wb_, csTb], [pb])
                bcol, bb = self.bias_fn(j)
                jd = self.jmap(j)
                act(self.dst_t[:, jd, :], pt[:, 0:33], AF.Identity, [pb, bb], [self.dst_b[jd]], bias=bcol)
            fw.tag = prev_tag

    ada = {"jobs": [], "k": 0}

    def ada_queue(job):
        ada["jobs"].append(job)

    def ada_total():
        return sum(jb.total for jb in ada["jobs"]), sum(jb.done for jb in ada["jobs"])

    def ada_tick(frac):
        tot, done = ada_total()
        want = min(tot, int(np.ceil(tot * frac)))
        for jb in ada["jobs"]:
            while done < want and jb.done < jb.total:
                jb.step()
                done += 1

    def ada_flush():
        ada_tick(1.0)
        ada["jobs"] = []

    def ada_step(n):
        for jb in ada["jobs"]:
            while n > 0 and jb.done < jb.total:
                jb.step()
                n -= 1

    def ada_job_std(l, part, slot):
        return AdaJob("w_ada", l, part * 6144, 48, modall, modallb,
                      lambda j: (bada[:, l, part * 48 + j:part * 48 + j + 1], badab),
                      jmap=lambda j: j if j < 32 else 32 + 16 * slot + (j - 32))

    def ada_job_sa(l, part):
        return AdaJob("w_ada", l, part * 6144, 32, modall, modallb,
                      lambda j: (bada[:, l, part * 48 + j:part * 48 + j + 1], badab))

    def ada_job_gate(l, part, slot):
        return AdaJob("w_ada", l, part * 6144 + 4096, 16, modall, modallb,
                      lambda j: (bada[:, l, part * 48 + 32 + j:part * 48 + 33 + j], badab),
                      jmap=lambda j: 32 + 16 * slot + j)

    def ada_finish(job):
        while job.done < job.total:
            job.step()

    def make_amod(gt, gbuf, gcol0):
        mod, modb = M["t"], M["b"]
        for kc in range(KC):
            ts(DVE, mod[:, 16 + kc, :], mod[:, 16 + kc, :], 1.0, gt[:, gcol0 + kc:gcol0 + kc + 1], ALU.add, ALU.mult,
               [modb[16 + kc], gbuf], [modb[16 + kc]])

    def rms_stats(stk, ncols, with_nb):
        fw.tag = "norm"
        rstd, rstdb = sb("rstd", [128, NTX], F32, stk)
        sqs = [sb(f"sq{i}", [128, NTX], BF16, stk) for i in range(2)]
        tiles = NTILES_X if with_nb else NTILES
        pts = [PF() for _ in tiles]
        for kc in range(KC):
            sqt, sqb = sqs[kc % 2]
            act(sqt[:, 0:NT], x[:, kc, :], AF.Square, xb[kc], [sqb])
            if with_nb:
                act(sqt[:, NT:NTX], xnb[:, kc, :], AF.Square, [xnbb], [sqb])
            for (n0, nn), (pt, pb) in zip(tiles, pts):
                mm(pt[:, 0:nn], onesb[:, :], sqt[:, n0:n0 + nn], kc == 0, kc == KC - 1, [onesbb, sqb], [pb], sig=True)
        for (n0, nn), (pt, pb) in zip(tiles, pts):
            ts(DVE, rstd[:, n0:n0 + nn], pt[:, 0:nn], 1.0 / D, EPS, ALU.mult, ALU.add, [pb], [rstdb])
        act(rstd[:, 0:ncols], rstd[:, 0:ncols], AF.Sqrt, [rstdb], [rstdb])
        fw.op(DVE, lambda: nc.vector.reciprocal(rstd[:, 0:ncols], rstd[:, 0:ncols]), R=[rstdb], W=[rstdb])
        return rstd, rstdb

    def norm_mod(with_nb):
        stk = contextlib.ExitStack()
        mod, modb = M["t"], M["b"]
        ncols = NTX if with_nb else NT
        rstd, rstdb = rms_stats(stk, ncols, with_nb)
        ntmp = [sb(f"ntmp{i}", [128, NTX], F32, stk) for i in range(2)]
        for kc in range(KC):
            nt_, ntb = ntmp[kc % 2]
            A = mod[:, 16 + kc, :]
            Ab = modb[16 + kc]
            tt(DVE, nt_[:, 0:NT], x[:, kc, :], rstd[:, 0:NT], ALU.mult, xb[kc] + [rstdb], [ntb])
            if with_nb:
                tt(DVE, nt_[:, NT:NTX], xnb[:, kc, :], rstd[:, NT:NTX], ALU.mult, [xnbb, rstdb], [ntb])
            act(hT[:, kc, 0:TP], nt_[:, 0:TP], AF.Identity, [ntb, Ab, modb[kc]], [hb[kc]], bias=mod[:, kc, 0:1], scale=A[:, 0:1])
            if with_nb:
                act(hT[:, kc, NT:NTX], nt_[:, NT:NTX], AF.Identity, [ntb, Ab, modb[kc]], [hb[kc]], bias=mod[:, kc, 0:1], scale=A[:, 0:1])
            tt(DVE, nt_[:, TP:NT], nt_[:, TP:NT], A[:, 1:33], ALU.mult, [ntb, Ab], [ntb])
            tt(DVE, hT[:, kc, TP:NT], nt_[:, TP:NT], mod[:, kc, 1:33], ALU.add, [ntb, modb[kc]], [hb[kc]])
            if kc % 4 == 1 and M["g"] is not None:
                ada_step(1)
        fw.barrier()
        stk.close()

    def resid_add(pt, pb, m, n0, nn, gj):
        mod, modb = M["t"], M["b"]
        nt_i = ntile_of(n0)
        if n0 < TP:
            stt(DVE, x[:, m, n0:n0 + nn], pt[:, 0:nn], mod[:, gj, 0:1], x[:, m, n0:n0 + nn], ALU.mult, ALU.add,
                [pb, modb[gj], xb[m][nt_i]], [xb[m][nt_i]])
        else:
            tt(DVE, small[:, 0:32], pt[:, 0:32], mod[:, gj, 1:33], ALU.mult, [pb, modb[gj]], [smallb])
            tt(DVE, x[:, m, TP:NT], x[:, m, TP:NT], small[:, 0:32], ALU.add, [smallb, xb[m][2]], [xb[m][2]])

    def proj_rows(pieces, kchunks, src, srcb):
        for m in range(KC):
            pts = [PF() for _ in NTILES]
            for kc in range(kchunks):
                wt, wb_ = pieces[kc // 2]
                wv = wt[:, :].rearrange("p (k c) -> p k c", c=2048)
                for (n0, nn), (pt, pb) in zip(NTILES, pts):
                    mm(pt[:, 0:nn], wv[:, kc % 2, m * 128:(m + 1) * 128], src[:, kc, n0:n0 + nn], kc == 0, kc == kchunks - 1,
                       [wb_, srcb], [pb])
            for (n0, nn), (pt, pb) in zip(NTILES, pts):
                resid_add(pt, pb, m, n0, nn, M["g"] + m)

    def ffn(l):
        bt, gt = x_fb[l]
        fw.dma(SP, bt.ap().rearrange("p (k t) -> p k t", t=2), x[:, :, TP - 2:TP], R=[xb[c][1] for c in range(KC)], W=[dbuf(bt)])
        fw.allgather(bt.ap().opt(), gt.ap().opt(), R=[dbuf(bt)], W=[dbuf(gt)])
        fw.dma(SP, xnb[:], gt.ap()[0:128, :].rearrange("p (k t) -> p k t", t=2), R=[dbuf(gt)], W=[xnbb])
        make_amod(vecl, veclb, 112)
        norm_mod(True)
        st = contextlib.ExitStack()
        uext = [sb(f"uext{i}", [128, UE], F32, st) for i in range(2)]
        acc = [sb(f"acc{i}", [128, 1064], F32, st) for i in range(2)]
        sa, sab = sb("sa", [128, 1064], F32, st)
        zg = [sb(f"zg{i}", [128, 4, NT], BF16, st) for i in range(2)]
        cst, cstb = sb("cst", [128, 64, 8], F32, st)
        convo, convob = sb("convo", [128, 64, 10], F32, st)
        extra_slot(st)
        fw.dma(SP, cst[:], d_convs[l, :, :, :], W=[cstb])
        for i in range(2):
            fw.op(DVE, lambda i=i: nc.vector.memset(uext[i][0][:], 0.0), W=[uext[i][1]])
        def down(gi):
            zt_, zb_ = zg[gi % 2]
            fw.tag = "ffn.down"
            pieces = [WNEXT(("r", "ffn_w_down", l, gi * 512 + r * 256)) for r in range(2)]
            proj_rows(pieces, 4, zt_, zb_)

        for grp in range(8):
            zt, zb = zg[grp % 2]
            for jj in range(4):
                j = grp * 4 + jj
                if jj == 1 and grp > 0:
                    down(grp - 1)
                fw.tag = "ffn.up"
                ada_tick((j + 1) / 30.0)
                wt, wb_ = WNEXT(("c2", "ffn_w_up", l, j))
                wv = wt[:, :].rearrange("p (k c) -> p k c", c=256)
                for half in range(2):
                    ut, ub = uext[half]
                    at, ab = acc[half]
                    ch = half * 32 + j
                    w0 = vecl[:, 128 + ch:129 + ch]
                    w1 = vecl[:, 192 + ch:193 + ch]
                    w2 = vecl[:, 256 + ch:257 + ch]
                    cb = vecl[:, 320 + ch:321 + ch]
                    cp(ACT, ut[:, 1026:1066].rearrange("p (s t) -> p s t", t=10)[:, :, 0:2],
                       cst[:, ch, :].rearrange("p (s t) -> p s t", t=2), [cstb], [ub])
                    for (n0, nn) in NTILES_X:
                        pt, pb = PF()
                        for k in range(KC):
                            mm(pt[:, 0:nn], wv[:, k, half * 128:(half + 1) * 128], hT[:, k, n0:n0 + nn], k == 0, k == KC - 1,
                               [wb_, hb[k]], [pb])
                        if n0 < TP:
                            cp(ACT, ut[:, 2 + n0:2 + n0 + nn], pt[:, 0:nn], [pb], [ub])
                        else:
                            cp(ACT, ut[:, 1026:1066].rearrange("p (s t) -> p s t", t=10)[:, :, 2:10],
                               pt[:, 0:32].rearrange("p (s t) -> p s t", t=8), [pb], [ub])
                            ts(DVE, ut[:, 0:2], pt[:, 32:34], flag[:, 0:1], None, ALU.mult, None, [pb, flagb], [ub])
                    ts(DVE, at[:, :], ut[:, 2:1066], w2, cb, ALU.mult, ALU.add, [ub, veclb], [ab])
                    stt(DVE, at[:, :], ut[:, 1:1065], w1, at[:, :], ALU.mult, ALU.add, [ub, veclb, ab], [ab])
                    stt(DVE, at[:, :], ut[:, 0:1064], w0, at[:, :], ALU.mult, ALU.add, [ub, veclb, ab], [ab])
                    cp(ACT, convo[:, ch, :].rearrange("p (s t) -> p s t", t=2),
                       ut[:, 1024:1074].rearrange("p (s t) -> p s t", t=10)[:, :, 0:2], [ub], [convob])
                a_t, a_b = acc[0]
                b_t, b_b = acc[1]
                act(sa[:, :], a_t[:, :], AF.Silu, [a_b], [sab])
                tt(DVE, zt[:, jj, 0:TP], sa[:, 0:TP], b_t[:, 0:TP], ALU.mult, [sab, b_b], [zb])
                tt(DVE, zt[:, jj, TP:NT].rearrange("p (s t) -> p s t", t=8),
                   sa[:, 1024:1064].rearrange("p (s t) -> p s t", t=10)[:, :, 2:10],
                   b_t[:, 1024:1064].rearrange("p (s t) -> p s t", t=10)[:, :, 2:10], ALU.mult, [sab, b_b], [zb])
        down(7)
        ada_flush()
        fw.dma(SP, o_conv[l, :, :], convo[:, :, :].rearrange("p c t -> p (c t)"), R=[convob])
        fw.barrier()
        wcur["l"] = list(wsl)
        st.close()

    def retention(l):
        make_amod(vecl, veclb, 96)
        norm_mod(False)
        st = contextlib.ExitStack()
        rope, ropeb = sb("rope", [128, 2, NT], F32, st)
        cmask, cmaskb = sb("cmask", [128, 160], F32, st)
        rsc, rscb = sb("rsc", [128, 64], F32, st)
        qd, qdb = sb("qd", [128, 160], F32, st)
        q2T, q2b = sb("q2T", [128, 2, NT], BF16, st)
        kT, kTb = sb("kT", [128, 2, NT], BF16, st)
        ktok, ktokb = sb("ktok", [128, 9, 256], BF16, st)
        ktf = [sb(f"ktf{i}", [128, 256], BF16, st) for i in range(2)]
        v, _ = sb("v", [128, 9, 512], BF16, st)
        vbs = [Buf(f"v{t}") for t in range(9)]
        gnsg, gnsgb = sb("gnsg", [128, 9, 512], BF16, st)
        S, _ = sb("S", [128, 2, 512], F32, st)
        Sb, _ = sb("Sb", [128, 2, 512], BF16, st)
        S2, _ = sb("S2", [128, 2, 512], F32, st)
        Sbuf = [Buf("S_0"), Buf("S_1")]
        Sbb = [Buf("Sb_0"), Buf("Sb_1")]
        S2buf = [Buf("S2_0"), Buf("S2_1")]
        gT, gTb = sb("gT", [128, 4, NT], BF16, st)
        ra = [sb(f"ra{i}", [128, 512], F32, st) for i in range(2)]
        scr = [dict(PT=sb(f"PT{i}", [128, 128], BF16, st), gated=sb(f"gated{i}", [128, 512], BF16, st),
                    st6=sb(f"st6{i}", [128, 8], F32, st), tn=ra[i]) for i in range(2)]
        q2m, q2mb = sb("q2m", [128, 2, 32], BF16, st)
        ktm, ktmb = sb("ktm", [32, 256], BF16, st)
        fw.dma(SP, rope[:], d_ropeR[:, :, :], W=[ropeb])
        fw.dma(SP, cmask[:], d_cmask[:, :], W=[cmaskb])
        fw.dma(SP, rsc[:], d_rsc[:, :], W=[rscb])
        cosT = rope[:, 0, :]
        sinT = rope[:, 1, :]

        def rope_proj(wt, wb_, out_t, out_b, scale_q):
            wv = wt[:, :].rearrange("p (k c) -> p k c", c=256)
            for (n0, nn) in NTILES:
                p1, p1b = PF()
                p2, p2b = PF()
                for mc, (pt, pb) in enumerate(((p1, p1b), (p2, p2b))):
                    for k in range(KC):
                        mm(pt[:, 0:nn], wv[:, k, mc * 128:(mc + 1) * 128], hT[:, k, n0:n0 + nn], k == 0, k == KC - 1,
                           [wb_, hb[k]], [pb])
                a_, ab_ = ra[0]
                b_, bb_ = ra[1]
                cs = cosT[:, n0:n0 + nn]
                sn = sinT[:, n0:n0 + nn]
                if nn == 512:
                    qdv = qd[:, 0:128].unsqueeze(1).to_broadcast([128, 4, 128])
                    av = a_[:, 0:512].rearrange("p (c t) -> p c t", t=128)
                else:
                    qdv = qd[:, 128:160]
                    av = a_[:, 0:32]
                for half in range(2):
                    pa, pab = (p1, p1b) if half == 0 else (p2, p2b)
                    pc, pcb = (p2, p2b) if half == 0 else (p1, p1b)
                    tt(DVE, a_[:, 0:nn], pa[:, 0:nn], cs, ALU.mult, [pab, ropeb], [ab_])
                    tt(DVE, b_[:, 0:nn], pc[:, 0:nn], sn, ALU.mult, [pcb, ropeb], [bb_])
                    op = ALU.subtract if half == 0 else ALU.add
                    if scale_q:
                        tt(DVE, a_[:, 0:nn], a_[:, 0:nn], b_[:, 0:nn], op, [ab_, bb_], [ab_])
                        ov = out_t[:, half, n0:n0 + nn]
                        if nn == 512:
                            ov = ov.rearrange("p (c t) -> p c t", t=128)
                        tt(DVE, ov, av, qdv, ALU.mult, [ab_, qdb], [out_b])
                    else:
                        tt(DVE, out_t[:, half, n0:n0 + nn], a_[:, 0:nn], b_[:, 0:nn], op, [ab_, bb_], [out_b])

        def tok_proj(evac, c0w):
            for half in range(2):
                wt, wb_ = WNEXT(("c", "ret_w_in", l, c0w + half * 256))
                wv = wt[:, :].rearrange("p (k c) -> p k c", c=256)
                for t, (c0, cn) in enumerate(TTILES):
                    pt, pb = PF()
                    for k in range(KC):
                        mm(pt[0:cn, 0:256], hT[:, k, c0:c0 + cn], wv[:, k, :], k == 0, k == KC - 1, [wb_, hb[k]], [pb])
                    evac(t, cn, half, pt, pb)

        for h in range(RH):
            gam = GAM[h]
            kdec_c = rsc[:, h:h + 1]
            ksc_c = rsc[:, 8 + h:9 + h]
            kdec_s = rsc[0:32, 16 + h:17 + h]
            ksc_s = rsc[0:32, 24 + h:25 + h]
            fw.dma(SP, qd[:], d_qdec[h, :, :], W=[qdb])
            sbufs = [(S2, S2buf), (S, Sbuf), (S2, S2buf), (S, Sbuf)]
            fw.dma(SP, S2[:], d_sret[l, 0, h, :, :].rearrange("(c p) e -> p c e", p=128), W=S2buf)
            fw.tag = "ret.K"
            ada_tick((h + 0.3) / 7.5)
            wt, wb_ = WNEXT(("c", "ret_w_in", l, 2048 + h * 256))
            rope_proj(wt, wb_, kT, kTb, False)
            fw.tag = "ret.ktr"
            for t, (c0, cn) in enumerate(TTILES):
                pt, pb = PB()
                for dc in range(2):
                    tr(pt[0:cn, dc * 128:(dc + 1) * 128], kT[:, dc, c0:c0 + cn], [kTb], [pb], sig=(dc == 1))
                act(ktok[0:cn, t, :], pt[0:cn, 0:256], AF.Identity, [pb, rscb], [ktokb], scale=(kdec_c if cn == 128 else kdec_s))
            fw.tag = "ret.V"
            pS = [PF(), PF()]
            for _, pbb in pS:
                reserved.add(bank_index(pbb))

            def evac_v(t, cn, half, pt, pb):
                cp(ACT, v[0:cn, t, half * 256:(half + 1) * 256], pt[0:cn, 0:256], [pb], [vbs[t]])
                if half == 1 and t >= 1:
                    tl = t - 1
                    ptag = fw.tag
                    fw.tag = "ret.Sloc"
                    kf, kfb = ktf[tl % 2]
                    ts(DVE, kf[:, :], ktok[:, tl, :], float(gam ** (128 * (7 - tl))), None, ALU.mult, None, [ktokb], [kfb])
                    for dc in range(2):
                        mm(pS[dc][0][:, :], kf[:, dc * 128:(dc + 1) * 128], v[:, tl, :], tl == 0, tl == 7, [kfb, vbs[tl]], [pS[dc][1]], sig=True)
                    fw.tag = ptag

            tok_proj(evac_v, 4096 + h * 512)
            ada_tick((h + 0.6) / 7.5)
            wq_pre = WNEXT(("c", "ret_w_in", l, h * 256))
            fw.tag = "ret.Sloc"
            for dc in range(2):
                cp(ACT, S[:, dc, :], pS[dc][0][:, :], [pS[dc][1]], [Sbuf[dc]])
            for _, pbb in pS:
                reserved.discard(bank_index(pbb))
            bt, gt = x_sb[l][h]
            fw.dma(SP, bt.ap().rearrange("p (c e) -> p c e", c=2), S[:], R=Sbuf, W=[dbuf(bt)])
            fw.allgather(bt.ap().opt(), gt.ap().opt(), R=[dbuf(bt)], W=[dbuf(gt)])
            fw.tag = "ret.Q"
            wt, wb_ = wq_pre
            rope_proj(wt, wb_, q2T, q2b, True)
            fw.tag = "ret.G"
            tok_proj(lambda t, cn, half, pt, pb: act(gnsg[0:cn, t, half * 256:(half + 1) * 256], pt[0:cn, 0:256], AF.Silu, [pb], [gnsgb]),
                     8192 + h * 512)
            ada_tick((h + 1.0) / 7.5)
            fw.tag = "ret.Sloc"
            fw.dma(SP, S[:], gt.ap()[0:128, :].rearrange("p (c e) -> p c e", c=2), R=[dbuf(gt)], W=Sbuf)
            for dc in range(2):
                ts(DVE, S[:, dc, :], S[:, dc, :], flag[:, 0:1], None, ALU.mult, None, [Sbuf[dc], flagb], [Sbuf[dc]])
                cp(ACT, Sb[:, dc, :], S[:, dc, :], [Sbuf[dc]], [Sbb[dc]])

            def gn_chain(t, cn, po, pob, sc_):
                (st6_, st6b_), (tn_, tnb_), (gated_, gatedb_) = sc_["st6"], sc_["tn"], sc_["gated"]
                fw.op(DVE, lambda: nc.vector.bn_stats(st6_[0:cn, 0:6], po[0:cn, :]), R=[pob], W=[st6b_])
                fw.op(DVE, lambda: nc.vector.bn_aggr(st6_[0:cn, 6:8], st6_[0:cn, 0:6]), R=[st6b_], W=[st6b_])
                ts(DVE, st6_[0:cn, 7:8], st6_[0:cn, 7:8], EPS, None, ALU.add, None, [st6b_], [st6b_])
                act(st6_[0:cn, 7:8], st6_[0:cn, 7:8], AF.Sqrt, [st6b_], [st6b_])
                fw.op(DVE, lambda: nc.vector.reciprocal(st6_[0:cn, 7:8], st6_[0:cn, 7:8]), R=[st6b_], W=[st6b_])
                ts(DVE, tn_[0:cn, :], po[0:cn, :], st6_[0:cn, 6:7], st6_[0:cn, 7:8], ALU.subtract, ALU.mult, [pob, st6b_], [tnb_])
                tt(DVE, gated_[0:cn, :], tn_[0:cn, :], gnsg[0:cn, t, :], ALU.mult, [tnb_, gnsgb], [gatedb_])

            def stage_b(t, c0, cn, sc_):
                gated_, gatedb_ = sc_["gated"]
                pt, pb = PB()
                for ec in range(4):
                    tr(pt[:, ec * 128:ec * 128 + cn], gated_[0:cn, ec * 128:(ec + 1) * 128], [gatedb_], [pb], sig=(ec == 3))
                for ec in range(4):
                    gcol = vecl[:, 384 + h * 4 + ec:385 + h * 4 + ec]
                    act(gT[:, ec, c0:c0 + cn], pt[:, ec * 128:ec * 128 + cn], AF.Identity, [pb, veclb], [gTb], scale=gcol)

            fw.tag = "ret.scan"
            pend = None
            for t in range(8):
                c0 = t * 128
                sc_ = scr[t % 2]
                PT_, PTb_ = sc_["PT"]
                psc, pscb = PF()
                pS = [PF(), PF()]
                po, pob = PF()
                for dc in range(2):
                    mm(psc[:, 0:128], kT[:, dc, c0:c0 + 128], q2T[:, dc, c0:c0 + 128], dc == 0, dc == 1, [kTb, q2b], [pscb])
                stt(DVE, PT_[:, :], psc[:, 0:128], ksc_c, cmask[:, 0:128], ALU.mult, ALU.mult, [pscb, rscb, cmaskb], [PTb_])
                for dc in range(2):
                    mm(pS[dc][0][:, :], ktok[:, t, dc * 128:(dc + 1) * 128], v[:, t, :], True, True, [ktokb, vbs[t]], [pS[dc][1]])
                mm(po[:, :], PT_[:, :], v[:, t, :], True, False, [PTb_, vbs[t]], [pob])
                for dc in range(2):
                    mm(po[:, :], q2T[:, dc, c0:c0 + 128], Sb[:, dc, :], False, dc == 1, [q2b, Sbb[dc]], [pob])
                for dc in range(2):
                    stt(DVE, S[:, dc, :], S[:, dc, :], float(gam ** 128), pS[dc][0][:, :], ALU.mult, ALU.add, [Sbuf[dc], pS[dc][1]], [Sbuf[dc]])
                    cp(ACT, Sb[:, dc, :], S[:, dc, :], [Sbuf[dc]], [Sbb[dc]])
                gn_chain(t, 128, po, pob, sc_)
                if pend is not None:
                    stage_b(*pend)
                pend = (t, c0, 128, sc_)
            fw.dma(SP, o_retp[l, h, :, :].rearrange("(c p) e -> p c e", p=128), S[:], R=Sbuf)

            fw.tag = "ret.samp"
            c0 = TP
            sc_ = scr[0]
            PT_, PTb_ = sc_["PT"]
            psc, pscb = PF()
            po, pob = PF()
            pS = [PF(), PF()]
            for dc in range(2):
                mm(psc[0:32, 0:32], kT[:, dc, c0:c0 + 32], q2T[:, dc, c0:c0 + 32], dc == 0, dc == 1, [kTb, q2b], [pscb])
            stt(DVE, PT_[0:32, 0:32], psc[0:32, 0:32], ksc_s, cmask[0:32, 128:160], ALU.mult, ALU.mult, [pscb, rscb, cmaskb], [PTb_])
            mm(po[0:32, :], PT_[0:32, 0:32], v[0:32, 8, :], True, False, [PTb_, vbs[8]], [pob])
            stage_b(*pend)
            for s in range(4):
                Sx, Sxb = sbufs[s]
                if s + 1 < 4:
                    Sn, Snb = sbufs[s + 1]
                    fw.dma(SP, Sn[:], d_sret[l, s + 1, h, :, :].rearrange("(c p) e -> p c e", p=128), W=Snb)
                for dc in range(2):
                    cp(ACT, Sb[:, dc, :], Sx[:, dc, :], [Sxb[dc]], [Sbb[dc]])
                fw.op(DVE, lambda: nc.vector.memset(q2m[:], 0.0), W=[q2mb])
                cp(DVE, q2m[:, :, 8 * s:8 * s + 8], q2T[:, :, c0 + 8 * s:c0 + 8 * s + 8], [q2b], [q2mb])
                for dc in range(2):
                    mm(po[0:32, :], q2m[:, dc, :], Sb[:, dc, :], False, (s == 3 and dc == 1), [q2mb, Sbb[dc]], [pob])
                ts(DVE, ktm[:, :], ktok[0:32, 8, :], rsc[0:32, 32 + s:33 + s], None, ALU.mult, None, [ktokb, rscb], [ktmb])
                for dc in range(2):
                    mm(pS[dc][0][:, :], ktm[:, dc * 128:(dc + 1) * 128], v[0:32, 8, :], True, True, [ktmb, vbs[8]], [pS[dc][1]])
                for dc in range(2):
                    stt(DVE, Sx[:, dc, :], Sx[:, dc, :], float(gam ** 8), pS[dc][0][:, :], ALU.mult, ALU.add, [Sxb[dc], pS[dc][1]], [Sxb[dc]])
                fw.dma(SP, o_rets[l, s, h, :, :].rearrange("(c p) e -> p c e", p=128), Sx[:], R=Sxb)
            gn_chain(8, 32, po, pob, sc_)
            stage_b(8, c0, 32, sc_)
            fw.tag = "ret.out"
            if h == 0:
                ada_finish(att["gatejob"])
            pieces = [WNEXT(("r", "ret_w_out", l, h * 512 + r * 256)) for r in range(2)]
            proj_rows(pieces, 4, gT, gTb)
        ada_flush()
        fw.barrier()
        st.close()

    att = {}

    def rope_apply(buf_t, buf_b, nh, t, cn, ropeA, ropeAb, rt):
        sv = buf_t[0:cn, 0:nh * 64].rearrange("p (h d) -> p h d", d=64)
        x1 = sv[:, :, 0:8]
        x2 = sv[:, :, 8:16]
        cs = ropeA[0:cn, t, 0, :].unsqueeze(1).to_broadcast([cn, nh, 8])
        sn = ropeA[0:cn, t, 1, :].unsqueeze(1).to_broadcast([cn, nh, 8])
        (r0, r0b), (r1, r1b) = rt
        a = r0[0:cn, 0, 0:nh, :]
        b = r0[0:cn, 1, 0:nh, :]
        c = r1[0:cn, 0, 0:nh, :]
        d = r1[0:cn, 1, 0:nh, :]
        tt(DVE, a, x1, cs, ALU.mult, [buf_b, ropeAb], [r0b])
        tt(DVE, b, x2, sn, ALU.mult, [buf_b, ropeAb], [r0b])
        tt(DVE, c, x2, cs, ALU.mult, [buf_b, ropeAb], [r1b])
        tt(DVE, d, x1, sn, ALU.mult, [buf_b, ropeAb], [r1b])
        tt(DVE, x1, a, b, ALU.subtract, [r0b], [buf_b])
        tt(DVE, x2, c, d, ALU.add, [r1b], [buf_b])

    def kv_stage(stk):
        kTa, kTab, vA, vAb = att["kTa"], att["kTab"], att["vA"], att["vAb"]
        saveM = dict(M)
        M["t"], M["b"] = att["kvmod"]
        M["g"] = None
        make_amod(vecg, vecgb, 0)
        norm_mod(False)
        M.update(saveM)
        st = contextlib.ExitStack()
        ropeA, ropeAb = sb("ropeA", [128, 9, 2, 8], F32, st)
        kf, kfb = sb("kf", [128, 512], F32, st)
        kr, krb = sb("kr", [128, 2, 256], BF16, st)
        rt = [sb(f"rt{i}", [128, 2, 8, 8], F32, st) for i in range(2)]
        fw.dma(SP, ropeA[:], d_ropeA[:, :, :, :], W=[ropeAb])
        fw.dma(POOL, kTa[:, :, 128 + TP:128 + TP + 512], d_ckT[:, :, :], W=[kTab])
        for s in range(4):
            fw.dma(POOL, vA[:, 10 + s, :], d_cv[:, s, :], W=[vAb])
        fw.tag = "kv"
        wk_ = WNEXT(("c", "w_kv", None, 0))
        wv_ = WNEXT(("c", "w_kv", None, 256))

        def finish_tile(tile_idx, keycol0, cn):
            cp(ACT, vA[0:cn, tile_idx, :], kf[0:cn, 256:512], [kfb], [vAb])
            for g in range(4):
                for dup in range(2):
                    o0 = (g % 2) * 128 + dup * 64
                    cp(DVE, kr[0:cn, g // 2, o0:o0 + 64], kf[0:cn, g * 64:(g + 1) * 64], [kfb], [krb])
            pt, pb = PB()
            for g in range(4):
                tr(pt[:, g * 128:g * 128 + cn], kr[0:cn, g // 2, (g % 2) * 128:(g % 2) * 128 + 128], [krb], [pb], sig=(g == 3))
            cp(ACT, kTa[:, :, keycol0:keycol0 + cn], pt[:, 0:512].rearrange("p (g c) -> p g c", c=128)[:, :, 0:cn], [pb], [kTab])

        bt, gt = x_kv
        for t, (c0, cn) in enumerate(TTILES):
            for half, (wt, wb_) in enumerate((wk_, wv_)):
                wv2 = wt[:, :].rearrange("p (k c) -> p k c", c=256)
                pt, pb = PF()
                for k in range(KC):
                    mm(pt[0:cn, 0:256], hT[:, k, c0:c0 + cn], wv2[:, k, :], k == 0, k == KC - 1, [wb_, hb[k]], [pb])
                cp(ACT, kf[0:cn, half * 256:(half + 1) * 256], pt[0:cn, 0:256], [pb], [kfb])
            rope_apply(kf, kfb, 4, t, cn, ropeA, ropeAb, rt)
            if t < 8:
                finish_tile(1 + t, 128 + c0, 128)
            else:
                finish_tile(9, 128 + TP + 512, 32)
            if t == 7:
                fw.dma(SP, o_wkp[:, :], kf[:, 0:256], R=[kfb])
                fw.dma(SP, o_wvp[:, :], kf[:, 256:512], R=[kfb])
                fw.dma(SP, bt.ap(), kf[:, :], R=[kfb], W=[dbuf(bt)])
                fw.allgather(bt.ap().opt(), gt.ap().opt(), R=[dbuf(bt)], W=[dbuf(gt)])
            if t == 8:
                for s in range(4):
                    fw.dma(SP, o_wks[s, 120:128, :], kf[8 * s:8 * s + 8, 0:256], R=[kfb])
                    fw.dma(SP, o_wvs[s, 120:128, :], kf[8 * s:8 * s + 8, 256:512], R=[kfb])
                    fw.dma(SP, o_wks[s, 0:120, :], d_cknat[s, 8:128, :])
                    fw.dma(SP, o_wvs[s, 0:120, :], d_cvnat[s, 8:128, :])
        fw.dma(SP, kf[:, :], gt.ap()[0:128, :], R=[dbuf(gt)], W=[kfb])
        finish_tile(0, 0, 128)
        fw.barrier()
        st.close()

    def attention(l, j):
        make_amod(vecl, veclb, 96)
        norm_mod(False)
        st = contextlib.ExitStack()
        kTa, kTab, vA, vAb = att["kTa"], att["kTab"], att["vA"], att["vAb"]
        ropeA, ropeAb = sb("ropeA", [128, 9, 2, 8], F32, st)
        rt = [sb(f"rt{i}", [128, 2, 8, 8], F32, st) for i in range(2)]
        qfs = [sb(f"qf{i}", [128, 512], F32, st) for i in range(2)]
        qbfs = [sb(f"qbf{i}", [128, 512], BF16, st) for i in range(2)]
        qT, qTb = sb("qT", [128, 4, NT], BF16, st)
        oT, oTb = sb("oT", [128, 4, NT], BF16, st)
        amask, amaskb = sb("amask", [128, 2, 256], F32, st)
        smask, smaskb = sb("smask", [32, 544], F32, st)
        amk, amkb = sb("amk", [128, 2, 256], BF16, st)
        smk, smkb = sb("smk", [32, 544], BF16, st)
        sinks, sinksb = sb("sinks", [128, 2, 32], F32, st)
        pes = [sb(f"pe{i}", [128, 4, 256], BF16, st) for i in range(2)]
        pTs = [sb(f"pT{i}", [128, 8, 128], BF16, st) for i in range(2)]
        sms = [sb(f"sm{i}", [128, 32], F32, st) for i in range(2)]
        pss = [sb(f"pss{i}", [32, 544], BF16, st) for i in range(2)]
        fw.dma(SP, ropeA[:], d_ropeA[:, :, :, :], W=[ropeAb])
        fw.dma(SP, amask[:], d_amask[:, :, :], W=[amaskb])
        fw.dma(SP, smask[:], d_smask[:, :], W=[smaskb])
        fw.dma(SP, sinks[:], d_sinks[:, :, :], W=[sinksb])
        extra_slot(st)
        cp(DVE, amk[:], amask[:], [amaskb], [amkb])
        cp(DVE, smk[:], smask[:], [smaskb], [smkb])
        ctr = {"i": 0}

        for g in range(4):
            fw.tag = "att.q"
            ada_tick((g + 0.5) / 3.6)
            wq = [WNEXT(("c", "att_w_q", j, g * 512 + half * 256)) for half in range(2)]
            def q_tr(t, c0, cn):
                qbf, qbfb = qbfs[t % 2]
                pt, pb = PB()
                for cc in range(4):
                    tr(pt[:, cc * 128:cc * 128 + cn], qbf[0:cn, cc * 128:(cc + 1) * 128], [qbfb], [pb], sig=(cc == 3))
                cp(ACT, qT[:, :, c0:c0 + cn], pt[:, 0:512].rearrange("p (c t) -> p c t", t=128)[:, :, 0:cn], [pb], [qTb])

            qpend = None
            for t, (c0, cn) in enumerate(TTILES):
                qf, qfb = qfs[t % 2]
                qbf, qbfb = qbfs[t % 2]
                for half in range(2):
                    wt, wb_ = wq[half]
                    wv2 = wt[:, :].rearrange("p (k c) -> p k c", c=256)
                    pt, pb = PF()
                    for k in range(KC):
                        mm(pt[0:cn, 0:256], hT[:, k, c0:c0 + cn], wv2[:, k, :], k == 0, k == KC - 1, [wb_, hb[k]], [pb])
                    cp(ACT, qf[0:cn, half * 256:(half + 1) * 256], pt[0:cn, 0:256], [pb], [qfb])
                rope_apply(qf, qfb, 8, t, cn, ropeA, ropeAb, rt)
                cp(ACT, qbf[0:cn, :], qf[0:cn, :], [qfb], [qbfb])
                if qpend is not None:
                    q_tr(*qpend)
                qpend = (t, c0, cn)
            q_tr(*qpend)
            def stage_a(t, bh):
                c0 = t * 128
                wm = 0 if t == 0 else 1
                ix = ctr["i"] % 2
                ctr["i"] += 1
                pe_, peb = pes[ix]
                sm, smb = sms[ix]
                banks = [PF(), PF()]
                for i, (ps, psb_) in enumerate(banks):
                    hp = 2 * bh + i
                    for hh in range(2):
                        base = hh * 64
                        mm(ps[:, hh * 256:(hh + 1) * 256], qT[base:base + 64, hp, c0:c0 + 128], kTa[base:base + 64, g, c0:c0 + 256],
                           True, False, [qTb, kTab], [psb_])
                        mm(ps[:, hh * 256:(hh + 1) * 256], identb[:, :], amk[:, wm, :], False, True, [identbb, amkb], [psb_])
                h0 = g * 8 + 4 * bh
                sk4 = sinks[:, j, h0:h0 + 4]
                for i, (ps, psb_) in enumerate(banks):
                    fw.op(DVE, lambda ps=ps, i=i: nc.vector.tensor_reduce(
                        sm[:, 2 * i:2 * i + 2], ps[:, 0:512].rearrange("p (h k) -> p h k", k=256), mybir.AxisListType.X, ALU.max),
                        R=[psb_], W=[smb])
                stt(DVE, sm[:, 4:8], sm[:, 0:4], 0.125, sk4, ALU.mult, ALU.max, [smb, sinksb], [smb])
                ts(DVE, sm[:, 8:12], sm[:, 4:8], -1.0, None, ALU.mult, None, [smb], [smb])
                tt(DVE, sm[:, 16:20], sk4, sm[:, 4:8], ALU.subtract, [sinksb, smb], [smb])
                for i, (ps, psb_) in enumerate(banks):
                    for hh in range(2):
                        hd = 2 * i + hh
                        act(pe_[:, hd, :], ps[:, hh * 256:(hh + 1) * 256], AF.Exp, [psb_, smb], [peb, smb],
                            bias=sm[:, 8 + hd:9 + hd], scale=0.125, accum=sm[:, 12 + hd:13 + hd])
                act(sm[:, 20:24], sm[:, 16:20], AF.Exp, [smb], [smb])
                tt(DVE, sm[:, 24:28], sm[:, 12:16], sm[:, 20:24], ALU.add, [smb], [smb])
                fw.op(DVE, lambda: nc.vector.reciprocal(sm[:, 28:32], sm[:, 24:28]), R=[smb], W=[smb])
                tt(DVE, pe_[:, :, :], pe_[:, :, :], sm[:, 28:32].unsqueeze(2).to_broadcast([128, 4, 256]), ALU.mult, [peb, smb], [peb])
                return (t, bh, ix)

            def stage_bb(t, bh, ix):
                c0 = t * 128
                pe_, peb = pes[ix]
                pT, pTb = pTs[ix]
                ptT, ptTb = PB()
                for hd in range(4):
                    for kb in range(2):
                        o0 = (hd * 2 + kb) * 128
                        tr(ptT[:, o0:o0 + 128], pe_[:, hd, kb * 128:(kb + 1) * 128], [peb], [ptTb], sig=(hd == 3 and kb == 1))
                cp(ACT, pT[:, :, :], ptT[:, 0:1024].rearrange("p (a q) -> p a q", q=128), [ptTb], [pTb])
                for i in range(2):
                    hp = 2 * bh + i
                    po, pob = PF()
                    for hh in range(2):
                        hd = 2 * i + hh
                        for kb in range(2):
                            mm(po[hh * 64:(hh + 1) * 64, 0:128], vA[:, t + kb, g * 64:(g + 1) * 64], pT[:, hd * 2 + kb, :], kb == 0, kb == 1,
                               [vAb, pTb], [pob], tp=(0, hh * 64))
                    cp(ACT, oT[:, hp, c0:c0 + 128], po[:, 0:128], [pob], [oTb])

            fw.tag = "att.blk"
            pend = None
            for t in range(8):
                for bh in range(2):
                    cur = stage_a(t, bh)
                    if pend is not None:
                        stage_bb(*pend)
                    pend = cur
            stage_bb(*pend)
            ada_tick((g + 1.0) / 3.6)
            fw.tag = "att.samp"
            c0 = TP
            K0 = 128 + TP

            def samp_a(hp, hh):
                ix = ctr["i"] % 2
                ctr["i"] += 1
                pn, pnb = pss[ix]
                sm, smb = sms[ix]
                hq = g * 8 + hp * 2 + hh
                base = hh * 64
                ps, psb_ = PF()
                ps2, ps2b = PF()
                mm(ps[0:32, 0:512], qT[base:base + 64, hp, c0:c0 + 32], kTa[base:base + 64, g, K0:K0 + 512], True, False, [qTb, kTab], [psb_])
                mm(ps[0:32, 0:512], identb[0:32, 0:32], smk[0:32, 0:512], False, True, [identbb, smkb], [psb_])
                mm(ps2[0:32, 0:32], qT[base:base + 64, hp, c0:c0 + 32], kTa[base:base + 64, g, K0 + 512:K0 + 544], True, False, [qTb, kTab], [ps2b])
                mm(ps2[0:32, 0:32], identb[0:32, 0:32], smk[0:32, 512:544], False, True, [identbb, smkb], [ps2b])
                fw.op(DVE, lambda: nc.vector.tensor_reduce(sm[0:32, 0:1], ps[0:32, 0:512], mybir.AxisListType.X, ALU.max), R=[psb_], W=[smb])
                fw.op(DVE, lambda: nc.vector.tensor_reduce(sm[0:32, 1:2], ps2[0:32, 0:32], mybir.AxisListType.X, ALU.max), R=[ps2b], W=[smb])
                tt(DVE, sm[0:32, 2:3], sm[0:32, 0:1], sm[0:32, 1:2], ALU.max, [smb], [smb])
                ts(DVE, sm[0:32, 4:5], sm[0:32, 2:3], 0.125, sinks[0:32, j, hq:hq + 1], ALU.mult, ALU.max, [smb, sinksb], [smb])
                ts(DVE, sm[0:32, 8:9], sm[0:32, 4:5], -1.0, None, ALU.mult, None, [smb], [smb])
                act(pn[0:32, 0:512], ps[0:32, 0:512], AF.Exp, [psb_, smb], [pnb, smb], bias=sm[0:32, 8:9], scale=0.125, accum=sm[0:32, 12:13])
                act(pn[0:32, 512:544], ps2[0:32, 0:32], AF.Exp, [ps2b, smb], [pnb, smb], bias=sm[0:32, 8:9], scale=0.125, accum=sm[0:32, 13:14])
                tt(DVE, sm[0:32, 16:17], sinks[0:32, j, hq:hq + 1], sm[0:32, 4:5], ALU.subtract, [sinksb, smb], [smb])
                act(sm[0:32, 20:21], sm[0:32, 16:17], AF.Exp, [smb], [smb])
                tt(DVE, sm[0:32, 24:25], sm[0:32, 12:13], sm[0:32, 13:14], ALU.add, [smb], [smb])
                tt(DVE, sm[0:32, 25:26], sm[0:32, 24:25], sm[0:32, 20:21], ALU.add, [smb], [smb])
                fw.op(DVE, lambda: nc.vector.reciprocal(sm[0:32, 28:29], sm[0:32, 25:26]), R=[smb], W=[smb])
                ts(DVE, pn[0:32, :], pn[0:32, :], sm[0:32, 28:29], None, ALU.mult, None, [pnb, smb], [pnb])
                return (hp, hh, ix)

            spo = {}

            def samp_b(hp, hh, ix):
                pn, pnb = pss[ix]
                pT, pTb = pTs[ix]
                if hh == 0:
                    spo[hp] = PF()
                po, pob = spo[hp]
                ptT, ptTb = PB()
                for kb in range(5):
                    kn = 128 if kb < 4 else 32
                    tr(ptT[0:kn, kb * 32:(kb + 1) * 32], pn[0:32, kb * 128:kb * 128 + kn], [pnb], [ptTb], sig=(kb == 4))
                cp(ACT, pT[:, 0:4, 0:32], ptT[:, 0:128].rearrange("p (k q) -> p k q", q=32), [ptTb], [pTb])
                cp(ACT, pT[0:32, 4, 0:32], ptT[0:32, 128:160], [ptTb], [pTb])
                for kb in range(5):
                    kn = 128 if kb < 4 else 32
                    vt = vA[0:kn, (10 + kb) if kb < 4 else 9, g * 64:(g + 1) * 64]
                    mm(po[hh * 64:(hh + 1) * 64, 0:32], vt, pT[0:kn, kb, 0:32], kb == 0, kb == 4, [vAb, pTb], [pob], tp=(0, hh * 64))
                if hh == 1:
                    cp(ACT, oT[:, hp, c0:c0 + 32], po[:, 0:32], [pob], [oTb])

            pend = None
            for hp in range(4):
                for hh in range(2):
                    cur = samp_a(hp, hh)
                    if pend is not None:
                        samp_b(*pend)
                    pend = cur
            samp_b(*pend)
            fw.tag = "att.o"
            if g == 0:
                ada_finish(att["gatejob"])
            pieces = [WNEXT(("r", "att_w_o", j, g * 512 + r * 256)) for r in range(2)]
            proj_rows(pieces, 4, oT, oTb)
        ada_flush()
        fw.barrier()
        wcur["l"] = list(wsl)
        st.close()

    fw.dma(SP, bada[:], d_bada[:, :, :], W=[badab])
    kvstack = contextlib.ExitStack()
    kvmodstack = contextlib.ExitStack()
    ada_queue(ada_job_sa(0, 0))
    ada_flush()
    k = 0
    for l in range(DEPTH):
        fw.dma(SP, vecl[:], d_vecl[l, :, :], W=[veclb])
        M.update(t=modall, b=modallb, g=32 + 16 * (k % 2))
        att["gatejob"] = ada_job_gate(l, 0, k % 2)
        ada_queue(att["gatejob"])
        ada_queue(ada_job_std(l, 1, (k + 1) % 2))
        if l < N_A:
            retention(l)
        else:
            if l == N_A:
                kv_stage(kvstack)
                kvmodstack.close()
            attention(l, l - N_A)
        k += 1
        M.update(t=modall, b=modallb, g=32 + 16 * (k % 2))
        if l + 1 < DEPTH:
            if l + 1 == N_A:
                kTa, kTab = sb("kTa", [128, 4, 128 + TP + 512 + 32], BF16, kvstack)
                vA, vAb = sb("vA", [128, 14, 256], BF16, kvstack)
                att.update(dict(kTa=kTa, kTab=kTab, vA=vA, vAb=vAb))
                kvm, _ = sb("kvmod", [128, 32, 33], F32, kvmodstack)
                kvmb = [Buf(f"kvmod{j}") for j in range(32)]
                att["kvmod"] = (kvm, kvmb)
                ada_queue(AdaJob("kv_w_ada", None, 0, 32, kvm, kvmb, lambda j: (vecg[:, 16 + j:17 + j], vecgb)))
            ada_queue(ada_job_sa(l + 1, 0))
        ffn(l)
        k += 1
    kvstack.close()
    stk = contextlib.ExitStack()
    rstd, rstdb = rms_stats(stk, NT, False)
    for kc in range(KC):
        stt(DVE, x[:, kc, :], x[:, kc, :], vecg[:, 48 + kc:49 + kc], rstd[:, 0:NT], ALU.mult, ALU.mult, xb[kc] + [vecgb, rstdb], xb[kc])
        fw.dma(SP, o_yT[:, kc, :], x[:, kc, :], R=xb[kc])
    fw.finish()
    stk.close()
    es.close()
    assert wstate["n"] == NP, (wstate["n"], NP)
    ok, stuck, val = fw.simulate()
    assert ok, ("semaphore deadlock in emitted program", stuck)
    build_program.stats = {q.name: (q.cnt, len(q.log)) for q in fw.queues}
    build_program.maxsem = max(val.values())
    build_program.specs = specs
    build_program.tags = {q.name: q.tags for q in fw.queues}
    return nc


def _piece_cols(Wm, cols):
    sub = Wm[:, cols]
    return np.ascontiguousarray(sub.reshape(KC, 128, 256).transpose(1, 0, 2)).reshape(128, WCOLS)


def _piece_rows(Wm, r0):
    sub = Wm[r0:r0 + 256, :]
    return np.ascontiguousarray(sub.reshape(2, 128, 2048).transpose(1, 0, 2)).reshape(128, WCOLS)


def build_weights(inp, specs):
    out = np.empty((len(specs), 128, WCOLS), np.float32)
    for i, (kind, name, idx, a0) in enumerate(specs):
        Wm = inp[name] if idx is None else inp[name][idx]
        if kind == "c":
            out[i] = _piece_cols(Wm, np.arange(a0, a0 + 256))
        elif kind == "c2":
            out[i] = _piece_cols(Wm, np.concatenate([np.arange(a0 * 128, (a0 + 1) * 128), np.arange(4096 + a0 * 128, 4096 + (a0 + 1) * 128)]))
        else:
            out[i] = _piece_rows(Wm, a0)
    return out


def fm(vec):
    return np.ascontiguousarray(np.asarray(vec, np.float32).reshape(-1, 128).T)


def build_inputs(inp, specs):
    f32 = np.float32
    wts = build_weights(inp, specs)
    vecl = np.zeros((DEPTH, 128, 416), f32)
    for l in range(DEPTH):
        vecl[l, :, 0:96] = fm(inp["b_ada"][l])
        vecl[l, :, 96:112] = fm(inp["norm_mix"][l])
        vecl[l, :, 112:128] = fm(inp["norm_ffn"][l])
        for tap in range(3):
            vecl[l, :, 128 + tap * 64:128 + (tap + 1) * 64] = fm(inp["ffn_conv_w"][l, tap])
        vecl[l, :, 320:384] = fm(inp["ffn_conv_b"][l])
        if l < N_A:
            vecl[l, :, 384:416] = fm(inp["ret_gn"][l])
    bada = np.ascontiguousarray(np.stack([fm(inp["b_ada"][l]) for l in range(DEPTH)], 1))
    vecg = np.zeros((128, 64), f32)
    vecg[:, 0:16] = fm(inp["kv_norm"])
    vecg[:, 16:48] = fm(inp["kv_b_ada"])
    vecg[:, 48:64] = fm(inp["norm_f"])
    gam = np.array(GAM, np.float64)
    m = np.arange(128, dtype=np.float64)
    rsc = np.zeros((128, 64), np.float64)
    for h in range(RH):
        rsc[:, h] = gam[h] ** (127 - m) / 16.0
        rsc[:, 8 + h] = gam[h] ** (-(m + 1)) / 16.0
        ms = (np.arange(32) % 8).astype(np.float64)
        rsc[:32, 16 + h] = gam[h] ** (7 - ms) / 16.0
        rsc[:32, 24 + h] = gam[h] ** (-(ms + 1)) / 16.0
    for s in range(4):
        rsc[8 * s:8 * s + 8, 32 + s] = 1.0
    rsc = rsc.astype(f32)
    cmask = np.zeros((128, 160), f32)
    cmask[:, 0:128] = (m[:, None] <= m[None, :]).astype(f32)
    i32 = np.arange(32)
    cmask[:32, 128:160] = ((i32[:, None] // 8 == i32[None, :] // 8) & (i32[:, None] <= i32[None, :])).astype(f32)
    inv_r = (10000.0 ** (-np.arange(128, dtype=f32) * 2.0 / 256.0)).astype(f32)
    inv_a = (500000.0 ** (-np.arange(8, dtype=f32) * 2.0 / 16.0)).astype(f32)
    qi = np.arange(128)[:, None]
    kj = np.arange(256)[None, :]
    band = np.where(((kj < 128) & (kj >= qi)) | ((kj >= 128) & (kj - 128 <= qi)), 0.0, NEG).astype(f32)
    first = np.where((kj >= 128) & (kj - 128 <= qi), 0.0, NEG).astype(f32)
    sm_ = np.full((32, 544), NEG, f32)
    for s in range(4):
        for i in range(8):
            r = 8 * s + i
            for jx in range(136):
                if i <= jx <= 128 + i:
                    if jx < 128:
                        sm_[r, s * 128 + jx] = 0.0
                    else:
                        sm_[r, 512 + 8 * s + (jx - 128)] = 0.0
    sinks = np.ascontiguousarray(np.broadcast_to(np.asarray(inp["att_sinks"], f32).reshape(1, 2, 32), (128, 2, 32)))

    maps = []
    for c in range(NCORES):
        s_, half = c // 2, c % 2
        t0 = half * TP
        ss = slice(4 * c, 4 * c + 4)
        xcat = np.concatenate([inp["x_prompt"][s_, t0:t0 + TP, :], inp["x_sample"][ss].reshape(TS, D)], 0)
        xT = np.ascontiguousarray(xcat.reshape(NT, KC, 128).transpose(2, 1, 0))
        ccat = np.concatenate([inp["c_prompt"][s_:s_ + 1], inp["c_sample"][ss]], 0)
        cT = np.ascontiguousarray(ccat.reshape(5, KC, 128).transpose(2, 1, 0))
        pos = np.concatenate([np.arange(t0, t0 + TP), np.tile(PAST + np.arange(8), 4)]).astype(f32)
        ang = pos[None, :] * inv_r[:, None]
        ropeR = np.stack([np.cos(ang), np.sin(ang)], 1).astype(f32)
        lpos = np.concatenate([np.arange(128), np.tile(np.arange(8), 4)]).astype(np.float64)
        qdec = np.ascontiguousarray(np.broadcast_to((gam[:, None] ** (lpos[None, :] + 1.0))[:, None, :], (RH, 128, 160))).astype(f32)
        angA = pos[:, None] * inv_a[None, :]
        ropeA = np.zeros((128, 9, 2, 8), f32)
        for t, (c0, cn) in enumerate(TTILES):
            ropeA[:cn, t, 0, :] = np.cos(angA[c0:c0 + cn]).astype(f32)
            ropeA[:cn, t, 1, :] = np.sin(angA[c0:c0 + cn]).astype(f32)
        amask = np.stack([first if half == 0 else band, band], 1).astype(f32)
        ck = np.asarray(inp["cache_win_k"][ss], f32)
        cvv = np.asarray(inp["cache_win_v"][ss], f32)
        ckT = np.zeros((128, 4, 512), f32)
        for g in range(4):
            kt = ck[:, :, g, :].transpose(2, 0, 1).reshape(64, 512)
            ckT[0:64, g, :] = kt
            ckT[64:128, g, :] = kt
        cv = np.ascontiguousarray(cvv.reshape(4, 128, 256).transpose(1, 0, 2))
        convs = np.ascontiguousarray(
            np.asarray(inp["state_conv"][:, ss], f32).reshape(DEPTH, 4, 2, 64, 128).transpose(0, 4, 3, 1, 2)).reshape(DEPTH, 128, 64, 8)
        maps.append(dict(
            xT=xT, cT=cT, flag=np.full((128, 1), float(half), f32), wts=wts, vecl=vecl, vecg=vecg, ropeR=ropeR, qdec=qdec,
            rsc=rsc, cmask=cmask, sret=np.ascontiguousarray(inp["state_ret"][:, ss]), convs=convs, ropeA=ropeA,
            amask=amask, smask=sm_, ckT=ckT, cv=cv, cknat=np.ascontiguousarray(ck.reshape(4, 128, 256)),
            cvnat=np.ascontiguousarray(cvv.reshape(4, 128, 256)), sinks=sinks, ident=np.eye(128, dtype=f32), bada=bada))
    return maps


def assemble(res):
    f32 = np.float32
    y_p = np.zeros((4, 2048, D), f32)
    y_s = np.zeros((32, 8, D), f32)
    ret_p = np.zeros((N_A, 4, RH, 256, 512), f32)
    ret_s = np.zeros((N_A, 32, RH, 256, 512), f32)
    wk_p = np.zeros((4, 128, 4, 64), f32)
    wv_p = np.zeros((4, 128, 4, 64), f32)
    wk_s = np.zeros((32, 128, 4, 64), f32)
    wv_s = np.zeros((32, 128, 4, 64), f32)
    cv_p = np.zeros((DEPTH, 4, 2, 8192), f32)
    cv_s = np.zeros((DEPTH, 32, 2, 8192), f32)
    for c in range(NCORES):
        r = res[c]
        s_, half = c // 2, c % 2
        t0 = half * TP
        yt = r["yT"].transpose(2, 1, 0).reshape(NT, D)
        y_p[s_, t0:t0 + TP] = yt[:TP]
        y_s[4 * c:4 * c + 4] = yt[TP:].reshape(4, 8, D)
        ret_s[:, 4 * c:4 * c + 4] = r["rets"]
        wk_s[4 * c:4 * c + 4] = r["wks"].reshape(4, 128, 4, 64)
        wv_s[4 * c:4 * c + 4] = r["wvs"].reshape(4, 128, 4, 64)
        co = r["convo"].reshape(DEPTH, 128, 64, 5, 2).transpose(0, 3, 4, 2, 1).reshape(DEPTH, 5, 2, 8192)
        cv_s[:, 4 * c:4 * c + 4] = co[:, 1:5]
        if half == 1:
            ret_p[:, s_] = r["retp"]
            wk_p[s_] = r["wkp"].reshape(128, 4, 64)
            wv_p[s_] = r["wvp"].reshape(128, 4, 64)
            cv_p[:, s_] = co[:, 0]
    return (y_p, y_s, ret_p, ret_s, wk_p, wv_p, wk_s, wv_s, cv_p, cv_s)


_CACHE = {}


def kernel(**inputs):
    inp = {k: np.asarray(v) for k, v in inputs.items()}
    if "nc" not in _CACHE:
        _CACHE["nc"] = build_program()
    nc = _CACHE["nc"]
    maps = build_inputs(inp, build_program.specs)
    res = run_bass_kernel_spmd(nc, maps, core_ids=list(range(NCORES)))
    return assemble(res.results)
```

```python
import contextlib
import numpy as np
import concourse.bass as bass
import concourse.mybir as mybir
from concourse.bass_utils import run_bass_kernel_spmd

F32 = mybir.dt.float32
BF16 = mybir.dt.bfloat16
ALU = mybir.AluOpType
AF = mybir.ActivationFunctionType

D = 2048
KC = 16
TP = 1024
TS = 32
NT = TP + TS
NTX = NT + 2
DEPTH = 4
N_A = 2
RH = 8
PAST = 16384
EPS = 1e-6
NEG = -1e30
UE = 1074
NTILES = [(0, 512), (512, 512), (1024, 32)]
NTILES_X = [(0, 512), (512, 512), (1024, 34)]
TTILES = [(i * 128, 128) for i in range(8)] + [(1024, 32)]
GAM = [1.0 - 2.0 ** (-5.0 - h) for h in range(RH)]
NCORES = 8
WCOLS = 4096


class Buf:
    __slots__ = ("name", "w", "r")

    def __init__(self, name):
        self.name = name
        self.w = None
        self.r = {}


class Q:
    def __init__(self, fw, name, eng, is_pe=False):
        self.fw = fw
        self.name = name
        self.eng = eng
        self.is_pe = is_pe
        self.sid = fw.new_sem(name)
        self.cnt = 0
        self.seen = {}
        self.dsems = []
        self.dcnt = []
        self.drr = 0
        self.log = []
        self.tags = []

    def wait(self, ev, same_ok=False):
        if ev is None:
            return
        s, v = ev
        if s == self.sid and (same_ok or self.is_pe):
            return
        if self.seen.get(s, 0) >= v:
            return
        self.eng.wait_ge(self.fw.sems[s], v)
        self.seen[s] = v
        self.log.append(("w", s, v))


class FW:
    def __init__(self, nc):
        self.nc = nc
        self.sems = []
        self.tag = ""
        self.kind = ""
        self.pe = Q(self, "pe", nc.tensor, is_pe=True)
        self.act = Q(self, "act", nc.scalar)
        self.dve = Q(self, "dve", nc.vector)
        self.pool = Q(self, "pool", nc.gpsimd)
        self.sp = Q(self, "sp", nc.sync)
        for q, n in ((self.pool, 6), (self.sp, 10)):
            for i in range(n):
                q.dsems.append(self.new_sem(f"{q.name}_d{i}"))
                q.dcnt.append(0)
        self.cc_sid = self.new_sem("cc")
        self.cc_cnt = 0
        self.queues = [self.pe, self.act, self.dve, self.pool, self.sp]

    def new_sem(self, name):
        self.sems.append(self.nc.alloc_semaphore("s_" + name))
        return len(self.sems) - 1

    def _deps(self, q, R, W):
        for b in R:
            q.wait(b.w)
        for b in W:
            q.wait(b.w, same_ok=True)
            for s, v in b.r.items():
                q.wait((s, v), same_ok=True)

    def _commit(self, ev, R, W):
        s, v = ev
        for b in R:
            if b.r.get(s, 0) < v:
                b.r[s] = v
        for b in W:
            b.w = ev
            b.r = {}

    def op(self, q, emit, R=(), W=(), sig=True):
        self._deps(q, R, W)
        ins = emit()
        q.tags.append((self.tag, self.kind))
        ev = (q.sid, q.cnt + 1)
        if sig:
            ins.then_inc(self.sems[q.sid], 1)
            q.cnt += 1
            q.log.append(("i", q.sid, 1))
        self._commit(ev, R, W)
        return ev

    def dma(self, q, out, in_, R=(), W=()):
        self._deps(q, R, W)
        j = q.drr
        q.drr = (q.drr + 1) % len(q.dsems)
        sid = q.dsems[j]
        q.wait((sid, 16 * q.dcnt[j]))
        ins = q.eng.dma_start(out=out, in_=in_)
        q.dcnt[j] += 1
        ins.then_inc(self.sems[sid], 16)
        q.log.append(("i", sid, 16))
        ev = (sid, 16 * q.dcnt[j])
        self._commit(ev, R, W)
        return ev

    def allgather(self, in_ap, out_ap, R, W):
        q = self.pool
        self._deps(q, R, W)
        q.wait((self.cc_sid, self.cc_cnt))
        ins = self.nc.gpsimd.collective_compute(
            "AllGather", ALU.bypass, replica_groups=[[0, 1], [2, 3], [4, 5], [6, 7]],
            ins=[in_ap], outs=[out_ap])
        self.cc_cnt += 1
        ins.then_inc(self.sems[self.cc_sid], 1)
        q.log.append(("i", self.cc_sid, 1))
        ev = (self.cc_sid, self.cc_cnt)
        self._commit(ev, R, W)
        return ev

    def barrier(self):
        evs = []
        for q in self.queues:
            if q.cnt:
                evs.append((q.sid, q.cnt))
            for sid, c in zip(q.dsems, q.dcnt):
                if c:
                    evs.append((sid, 16 * c))
        if self.cc_cnt:
            evs.append((self.cc_sid, self.cc_cnt))
        for q in self.queues:
            for ev in evs:
                q.wait(ev, same_ok=True)

    def simulate(self):
        val = {}
        pos = {q.name: 0 for q in self.queues}
        progress = True
        while progress:
            progress = False
            for q in self.queues:
                p = pos[q.name]
                while p < len(q.log):
                    k, s_, v = q.log[p]
                    if k == "w":
                        if val.get(s_, 0) < v:
                            break
                    else:
                        val[s_] = val.get(s_, 0) + v
                    p += 1
                    progress = True
                pos[q.name] = p
        stuck = {q.name: (pos[q.name], len(q.log), q.log[pos[q.name]] if pos[q.name] < len(q.log) else None) for q in self.queues}
        ok = all(pos[q.name] == len(q.log) for q in self.queues)
        return ok, stuck, val

    def finish(self):
        q = self.sp
        for qq in self.queues:
            if qq.cnt:
                q.wait((qq.sid, qq.cnt), same_ok=True)
            for sid, c in zip(qq.dsems, qq.dcnt):
                if c:
                    q.wait((sid, 16 * c))
        if self.cc_cnt:
            q.wait((self.cc_sid, self.cc_cnt))


def n_weight_pieces():
    per_ret = 24 + RH * 8 + 24 + 8 * 6
    per_att = 24 + 4 * 4 + 24 + 8 * 6
    return 2 * per_ret + 16 + 2 + 2 * per_att


def build_program():
    nc = bass.Bass("TRN2", target_bir_lowering=False)
    fw = FW(nc)
    PE, ACT, DVE, POOL, SP = fw.pe, fw.act, fw.dve, fw.pool, fw.sp
    NP = n_weight_pieces()

    def din(name, shape, dt=F32):
        return nc.dram_tensor(name, list(shape), dt, kind="ExternalInput")

    def dout(name, shape, dt=F32):
        return nc.dram_tensor(name, list(shape), dt, kind="ExternalOutput")

    d_xT = din("xT", [128, KC, NT])
    d_cT = din("cT", [128, KC, 5])
    d_flag = din("flag", [128, 1])
    d_wts = din("wts", [NP, 128, WCOLS])
    d_vecl = din("vecl", [DEPTH, 128, 416])
    d_vecg = din("vecg", [128, 64])
    d_ropeR = din("ropeR", [128, 2, NT])
    d_qdec = din("qdec", [RH, 128, 160])
    d_rsc = din("rsc", [128, 64])
    d_cmask = din("cmask", [128, 160])
    d_sret = din("sret", [N_A, 4, RH, 256, 512])
    d_convs = din("convs", [DEPTH, 128, 64, 8])
    d_ropeA = din("ropeA", [128, 9, 2, 8])
    d_amask = din("amask", [128, 2, 256])
    d_smask = din("smask", [32, 544])
    d_ckT = din("ckT", [128, 4, 512])
    d_cv = din("cv", [128, 4, 256])
    d_cknat = din("cknat", [4, 128, 256])
    d_cvnat = din("cvnat", [4, 128, 256])
    d_sinks = din("sinks", [128, 2, 32])
    d_ident = din("ident", [128, 128])
    d_bada = din("bada", [128, DEPTH, 96])

    o_yT = dout("yT", [128, KC, NT])
    o_retp = dout("retp", [N_A, RH, 256, 512])
    o_rets = dout("rets", [N_A, 4, RH, 256, 512])
    o_wkp = dout("wkp", [128, 256])
    o_wvp = dout("wvp", [128, 256])
    o_wks = dout("wks", [4, 128, 256])
    o_wvs = dout("wvs", [4, 128, 256])
    o_conv = dout("convo", [DEPTH, 128, 640])

    x_sb = [[(nc.dram_tensor(f"xs_b{l}_{h}", [128, 1024], F32), nc.dram_tensor(f"xs_g{l}_{h}", [256, 1024], F32))
             for h in range(RH)] for l in range(N_A)]
    x_fb = [(nc.dram_tensor(f"xf_b{l}", [128, 32], F32), nc.dram_tensor(f"xf_g{l}", [256, 32], F32)) for l in range(DEPTH)]
    x_kv = (nc.dram_tensor("xkv_b", [128, 512], F32), nc.dram_tensor("xkv_g", [256, 512], F32))
    dbufs = {}

    def dbuf(t):
        if t.name not in dbufs:
            dbufs[t.name] = Buf(t.name)
        return dbufs[t.name]

    es = contextlib.ExitStack()

    uniq = [0]

    def sb(name, shape, dt, stack=None):
        uniq[0] += 1
        t = (stack or es).enter_context(nc.sbuf_tensor(f"s_{name}_{uniq[0]}", list(shape), dt))
        return t, Buf(name)

    x, _ = sb("x", [128, KC, NT], F32)
    xb = [[Buf(f"x{c}_{n}") for n in range(3)] for c in range(KC)]
    hT, _ = sb("hT", [128, KC, NTX], BF16)
    hb = [Buf(f"h{c}") for c in range(KC)]
    modall, _ = sb("modall", [128, 64, 33], F32)
    modallb = [Buf(f"mod_{j}") for j in range(64)]
    M = {"t": modall, "b": modallb, "g": 32}
    bada, badab = sb("bada", [128, DEPTH, 96], F32)
    NWS = 3
    wsl = [sb(f"w{i}", [128, WCOLS], BF16) for i in range(NWS)]
    vecl, veclb = sb("vecl", [128, 416], F32)
    vecg, vecgb = sb("vecg", [128, 64], F32)
    identb, identbb = sb("identb", [128, 128], BF16)
    identf, identfb = sb("identf", [128, 128], F32)
    onesb, onesbb = sb("onesb", [128, 128], BF16)
    csT, csTb = sb("csT", [128, KC, 33], BF16)
    cTs, cTsb = sb("cTs", [128, KC, 5], F32)
    flag, flagb = sb("flag", [128, 1], F32)
    xnb, xnbb = sb("xnb", [128, KC, 2], F32)
    small, smallb = sb("small", [128, 64], F32)

    psf = [(nc.alloc_psum_tensor(f"psf{i}", [128, 512], F32), Buf(f"psf{i}")) for i in range(6)]
    psb = [(nc.alloc_psum_tensor(f"psb{i}", [128, 1024], BF16), Buf(f"psb{i}")) for i in range(2)]
    rr = {"f": 0, "b": 0}

    reserved = set()

    def PF():
        while True:
            i = rr["f"]
            rr["f"] = (i + 1) % 6
            if i not in reserved:
                return psf[i]

    def bank_index(pb):
        return [b for _, b in psf].index(pb)

    def PB():
        i = rr["b"]
        rr["b"] = (i + 1) % 2
        return psb[i]

    wstate = {"n": 0}

    specs = []

    wcur = {"l": list(wsl)}

    def extra_slot(stack):
        if nc.sbuf_bytes_remaining >= 2 * WCOLS + 128:
            wcur["l"] = list(wsl) + [sb("wx", [128, WCOLS], BF16, stack)]

    def WNEXT(spec, cols=WCOLS):
        i = wstate["n"]
        wstate["n"] += 1
        specs.append(spec)
        t, b = wcur["l"][i % len(wcur["l"])]
        fw.dma(POOL, t[:, 0:cols], d_wts[i, :, 0:cols], W=[b])
        return t, b

    def mm(out, lhsT, rhs, start, stop, R, W, tp=None, sig=None):
        sg = stop if sig is None else sig
        fw.kind = 'm' if tp is None else 'p'
        if tp is None:
            fw.op(PE, lambda: nc.tensor.matmul(out, lhsT, rhs, start=start, stop=stop), R=R, W=W, sig=sg)
        else:
            fw.op(PE, lambda: nc.tensor.matmul(out, lhsT, rhs, start=start, stop=stop, tile_position=tp), R=R, W=W, sig=sg)

    def tr(out, in_, R, W, sig=True):
        kp = in_.shape[0]
        fw.kind = 't'
        fw.op(PE, lambda: nc.tensor.transpose(out, in_, identb[0:kp, 0:kp]), R=list(R) + [identbb], W=W, sig=sig)

    def act(out, in_, func, R, W, bias=0.0, scale=1.0, accum=None):
        if accum is None:
            return fw.op(ACT, lambda: nc.scalar.activation(out, in_, func, bias=bias, scale=scale), R=R, W=W)
        return fw.op(ACT, lambda: nc.scalar.activation(out, in_, func, bias=bias, scale=scale, accum_out=accum), R=R, W=W)

    def tt(q, out, a, b, op, R, W):
        return fw.op(q, lambda: q.eng.tensor_tensor(out, a, b, op), R=R, W=W)

    def ts(q, out, a, s1, s2, op0, op1, R, W):
        if s2 is None:
            return fw.op(q, lambda: q.eng.tensor_scalar(out, a, s1, None, op0), R=R, W=W)
        return fw.op(q, lambda: q.eng.tensor_scalar(out, a, s1, s2, op0, op1), R=R, W=W)

    def stt(q, out, a, s, b, op0, op1, R, W):
        return fw.op(q, lambda: q.eng.scalar_tensor_tensor(out, a, s, b, op0, op1), R=R, W=W)

    def cp(q, out, in_, R, W):
        if q is ACT:
            return act(out, in_, AF.Copy, R, W)
        return fw.op(q, lambda: q.eng.tensor_copy(out, in_), R=R, W=W)

    def ntile_of(c0):
        return 0 if c0 < 512 else (1 if c0 < 1024 else 2)

    for c in range(KC):
        fw.dma(SP, x[:, c, :], d_xT[:, c, :], W=xb[c])
    fw.dma(SP, cTs[:], d_cT[:, :, :], W=[cTsb])
    fw.dma(SP, flag[:], d_flag[:, :], W=[flagb])
    fw.dma(SP, vecg[:], d_vecg[:, :], W=[vecgb])
    fw.dma(SP, identf[:], d_ident[:, :], W=[identfb])
    fw.op(DVE, lambda: nc.vector.memset(onesb[:], 1.0), W=[onesbb])
    cp(DVE, identb[:], identf[:], [identfb], [identbb])
    act(csT[:, :, 0:1], cTs[:, :, 0:1], AF.Silu, [cTsb], [csTb])
    for s in range(4):
        for t in range(8):
            act(csT[:, :, 1 + 8 * s + t:2 + 8 * s + t], cTs[:, :, 1 + s:2 + s], AF.Silu, [cTsb], [csTb])

    class AdaJob:
        def __init__(self, wname, l, c0, nj, dst_t, dst_b, bias_fn, jmap=None):
            self.wname, self.l, self.c0, self.nj = wname, l, c0, nj
            self.dst_t, self.dst_b, self.bias_fn = dst_t, dst_b, bias_fn
            self.jmap = jmap or (lambda j: j)
            self.done = 0
            self.total = nj // 2

        def step(self):
            i = self.done
            self.done += 1
            prev_tag = fw.tag
            fw.tag = "adaln"
            wt, wb_ = WNEXT(("c", self.wname, self.l, self.c0 + i * 256))
            wv = wt[:, :].rearrange("p (k c) -> p k c", c=256)
            for mc in range(2):
                j = 2 * i + mc
                pt, pb = PF()
                for k in range(KC):
                    mm(pt[:, 0:33], wv[:, k, mc * 128:(mc + 1) * 128], csT[:, k, :], k == 0, k == KC - 1, [wb_, csTb], [pb])
                bcol, bb = self.bias_fn(j)
                jd = self.jmap(j)
                act(self.dst_t[:, jd, :], pt[:, 0:33], AF.Identity, [pb, bb], [self.dst_b[jd]], bias=bcol)
            fw.tag = prev_tag

    ada = {"jobs": [], "k": 0}

    def ada_queue(job):
        ada["jobs"].append(job)

    def ada_total():
        return sum(jb.total for jb in ada["jobs"]), sum(jb.done for jb in ada["jobs"])

    def ada_tick(frac):
        tot, done = ada_total()
        want = min(tot, int(np.ceil(tot * frac)))
        for jb in ada["jobs"]:
            while done < want and jb.done < jb.total:
                jb.step()
                done += 1

    def ada_flush():
        ada_tick(1.0)
        ada["jobs"] = []

    def ada_step(n):
        for jb in ada["jobs"]:
            while n > 0 and jb.done < jb.total:
                jb.step()
                n -= 1

    def ada_job_std(l, part, slot):
        return AdaJob("w_ada", l, part * 6144, 48, modall, modallb,
                      lambda j: (bada[:, l, part * 48 + j:part * 48 + j + 1], badab),
                      jmap=lambda j: j if j < 32 else 32 + 16 * slot + (j - 32))

    def ada_job_sa(l, part):
        return AdaJob("w_ada", l, part * 6144, 32, modall, modallb,
                      lambda j: (bada[:, l, part * 48 + j:part * 48 + j + 1], badab))

    def ada_job_gate(l, part, slot):
        return AdaJob("w_ada", l, part * 6144 + 4096, 16, modall, modallb,
                      lambda j: (bada[:, l, part * 48 + 32 + j:part * 48 + 33 + j], badab),
                      jmap=lambda j: 32 + 16 * slot + j)

    def ada_finish(job):
        while job.done < job.total:
            job.step()

    def make_amod(gt, gbuf, gcol0):
        mod, modb = M["t"], M["b"]
        for kc in range(KC):
            ts(DVE, mod[:, 16 + kc, :], mod[:, 16 + kc, :], 1.0, gt[:, gcol0 + kc:gcol0 + kc + 1], ALU.add, ALU.mult,
               [modb[16 + kc], gbuf], [modb[16 + kc]])

    def rms_stats(stk, ncols, with_nb):
        fw.tag = "norm"
        rstd, rstdb = sb("rstd", [128, NTX], F32, stk)
        sqs = [sb(f"sq{i}", [128, NTX], BF16, stk) for i in range(2)]
        tiles = NTILES_X if with_nb else NTILES
        pts = [PF() for _ in tiles]
        for kc in range(KC):
            sqt, sqb = sqs[kc % 2]
            if kc % 2 == 0:
                act(sqt[:, 0:NT], x[:, kc, :], AF.Square, xb[kc], [sqb])
            else:
                tt(DVE, sqt[:, 0:NT], x[:, kc, :], x[:, kc, :], ALU.mult, xb[kc], [sqb])
            if with_nb:
                act(sqt[:, NT:NTX], xnb[:, kc, :], AF.Square, [xnbb], [sqb])
            for (n0, nn), (pt, pb) in zip(tiles, pts):
                mm(pt[:, 0:nn], onesb[:, :], sqt[:, n0:n0 + nn], kc == 0, kc == KC - 1, [onesbb, sqb], [pb], sig=True)
        for (n0, nn), (pt, pb) in zip(tiles, pts):
            ts(DVE, rstd[:, n0:n0 + nn], pt[:, 0:nn], 1.0 / D, EPS, ALU.mult, ALU.add, [pb], [rstdb])
        act(rstd[:, 0:ncols], rstd[:, 0:ncols], AF.Sqrt, [rstdb], [rstdb])
        fw.op(DVE, lambda: nc.vector.reciprocal(rstd[:, 0:ncols], rstd[:, 0:ncols]), R=[rstdb], W=[rstdb])
        return rstd, rstdb

    def norm_mod(with_nb):
        stk = contextlib.ExitStack()
        mod, modb = M["t"], M["b"]
        ncols = NTX if with_nb else NT
        rstd, rstdb = rms_stats(stk, ncols, with_nb)
        ntmp = [sb(f"ntmp{i}", [128, NTX], F32, stk) for i in range(2)]
        for kc in range(KC):
            nt_, ntb = ntmp[kc % 2]
            A = mod[:, 16 + kc, :]
            Ab = modb[16 + kc]
            tt(DVE, nt_[:, 0:NT], x[:, kc, :], rstd[:, 0:NT], ALU.mult, xb[kc] + [rstdb], [ntb])
            if with_nb:
                tt(DVE, nt_[:, NT:NTX], xnb[:, kc, :], rstd[:, NT:NTX], ALU.mult, [xnbb, rstdb], [ntb])
            act(hT[:, kc, 0:TP], nt_[:, 0:TP], AF.Identity, [ntb, Ab, modb[kc]], [hb[kc]], bias=mod[:, kc, 0:1], scale=A[:, 0:1])
            if with_nb:
                act(hT[:, kc, NT:NTX], nt_[:, NT:NTX], AF.Identity, [ntb, Ab, modb[kc]], [hb[kc]], bias=mod[:, kc, 0:1], scale=A[:, 0:1])
            tt(DVE, nt_[:, TP:NT], nt_[:, TP:NT], A[:, 1:33], ALU.mult, [ntb, Ab], [ntb])
            tt(DVE, hT[:, kc, TP:NT], nt_[:, TP:NT], mod[:, kc, 1:33], ALU.add, [ntb, modb[kc]], [hb[kc]])
            if kc % 4 == 1 and M["g"] is not None:
                ada_step(1)
        fw.barrier()
        stk.close()

    def resid_add(pt, pb, m, n0, nn, gj):
        mod, modb = M["t"], M["b"]
        nt_i = ntile_of(n0)
        if n0 < TP:
            stt(DVE, x[:, m, n0:n0 + nn], pt[:, 0:nn], mod[:, gj, 0:1], x[:, m, n0:n0 + nn], ALU.mult, ALU.add,
                [pb, modb[gj], xb[m][nt_i]], [xb[m][nt_i]])
        else:
            tt(DVE, small[:, 0:32], pt[:, 0:32], mod[:, gj, 1:33], ALU.mult, [pb, modb[gj]], [smallb])
            tt(DVE, x[:, m, TP:NT], x[:, m, TP:NT], small[:, 0:32], ALU.add, [smallb, xb[m][2]], [xb[m][2]])

    def proj_rows(pieces, kchunks, src, srcb):
        for m in range(KC):
            pts = [PF() for _ in NTILES]
            for kc in range(kchunks):
                wt, wb_ = pieces[kc // 2]
                wv = wt[:, :].rearrange("p (k c) -> p k c", c=2048)
                for (n0, nn), (pt, pb) in zip(NTILES, pts):
                    mm(pt[:, 0:nn], wv[:, kc % 2, m * 128:(m + 1) * 128], src[:, kc, n0:n0 + nn], kc == 0, kc == kchunks - 1,
                       [wb_, srcb], [pb])
            for (n0, nn), (pt, pb) in zip(NTILES, pts):
                resid_add(pt, pb, m, n0, nn, M["g"] + m)

    def ffn(l):
        bt, gt = x_fb[l]
        fw.dma(SP, bt.ap().rearrange("p (k t) -> p k t", t=2), x[:, :, TP - 2:TP], R=[xb[c][1] for c in range(KC)], W=[dbuf(bt)])
        fw.allgather(bt.ap().opt(), gt.ap().opt(), R=[dbuf(bt)], W=[dbuf(gt)])
        fw.dma(SP, xnb[:], gt.ap()[0:128, :].rearrange("p (k t) -> p k t", t=2), R=[dbuf(gt)], W=[xnbb])
        make_amod(vecl, veclb, 112)
        norm_mod(True)
        st = contextlib.ExitStack()
        uext = [sb(f"uext{i}", [128, UE], F32, st) for i in range(2)]
        acc = [sb(f"acc{i}", [128, 1064], F32, st) for i in range(2)]
        sa, sab = sb("sa", [128, 1064], F32, st)
        zg = [sb(f"zg{i}", [128, 4, NT], BF16, st) for i in range(2)]
        cst, cstb = sb("cst", [128, 64, 8], F32, st)
        convo, convob = sb("convo", [128, 64, 10], F32, st)
        extra_slot(st)
        fw.dma(SP, cst[:], d_convs[l, :, :, :], W=[cstb])
        for i in range(2):
            fw.op(DVE, lambda i=i: nc.vector.memset(uext[i][0][:], 0.0), W=[uext[i][1]])
        def down(gi):
            zt_, zb_ = zg[gi % 2]
            fw.tag = "ffn.down"
            pieces = [WNEXT(("r", "ffn_w_down", l, gi * 512 + r * 256)) for r in range(2)]
            proj_rows(pieces, 4, zt_, zb_)

        for grp in range(8):
            zt, zb = zg[grp % 2]
            for jj in range(4):
                j = grp * 4 + jj
                if jj == 1 and grp > 0:
                    down(grp - 1)
                fw.tag = "ffn.up"
                ada_tick((j + 1) / 30.0)
                wt, wb_ = WNEXT(("c2", "ffn_w_up", l, j))
                wv = wt[:, :].rearrange("p (k c) -> p k c", c=256)
                for half in range(2):
                    ut, ub = uext[half]
                    at, ab = acc[half]
                    ch = half * 32 + j
                    w0 = vecl[:, 128 + ch:129 + ch]
                    w1 = vecl[:, 192 + ch:193 + ch]
                    w2 = vecl[:, 256 + ch:257 + ch]
                    cb = vecl[:, 320 + ch:321 + ch]
                    cp(ACT, ut[:, 1026:1066].rearrange("p (s t) -> p s t", t=10)[:, :, 0:2],
                       cst[:, ch, :].rearrange("p (s t) -> p s t", t=2), [cstb], [ub])
                    for (n0, nn) in NTILES_X:
                        pt, pb = PF()
                        for k in range(KC):
                            mm(pt[:, 0:nn], wv[:, k, half * 128:(half + 1) * 128], hT[:, k, n0:n0 + nn], k == 0, k == KC - 1,
                               [wb_, hb[k]], [pb])
                        if n0 < TP:
                            cp(ACT, ut[:, 2 + n0:2 + n0 + nn], pt[:, 0:nn], [pb], [ub])
                        else:
                            cp(ACT, ut[:, 1026:1066].rearrange("p (s t) -> p s t", t=10)[:, :, 2:10],
                               pt[:, 0:32].rearrange("p (s t) -> p s t", t=8), [pb], [ub])
                            ts(DVE, ut[:, 0:2], pt[:, 32:34], flag[:, 0:1], None, ALU.mult, None, [pb, flagb], [ub])
                    ts(DVE, at[:, :], ut[:, 2:1066], w2, cb, ALU.mult, ALU.add, [ub, veclb], [ab])
                    stt(DVE, at[:, :], ut[:, 1:1065], w1, at[:, :], ALU.mult, ALU.add, [ub, veclb, ab], [ab])
                    stt(DVE, at[:, :], ut[:, 0:1064], w0, at[:, :], ALU.mult, ALU.add, [ub, veclb, ab], [ab])
                    cp(ACT, convo[:, ch, :].rearrange("p (s t) -> p s t", t=2),
                       ut[:, 1024:1074].rearrange("p (s t) -> p s t", t=10)[:, :, 0:2], [ub], [convob])
                a_t, a_b = acc[0]
                b_t, b_b = acc[1]
                act(sa[:, :], a_t[:, :], AF.Silu, [a_b], [sab])
                tt(DVE, zt[:, jj, 0:TP], sa[:, 0:TP], b_t[:, 0:TP], ALU.mult, [sab, b_b], [zb])
                tt(DVE, zt[:, jj, TP:NT].rearrange("p (s t) -> p s t", t=8),
                   sa[:, 1024:1064].rearrange("p (s t) -> p s t", t=10)[:, :, 2:10],
                   b_t[:, 1024:1064].rearrange("p (s t) -> p s t", t=10)[:, :, 2:10], ALU.mult, [sab, b_b], [zb])
        down(7)
        ada_flush()
        fw.dma(SP, o_conv[l, :, :], convo[:, :, :].rearrange("p c t -> p (c t)"), R=[convob])
        fw.barrier()
        wcur["l"] = list(wsl)
        st.close()

    def retention(l):
        make_amod(vecl, veclb, 96)
        norm_mod(False)
        st = contextlib.ExitStack()
        rope, ropeb = sb("rope", [128, 2, NT], F32, st)
        cmask, cmaskb = sb("cmask", [128, 160], F32, st)
        rsc, rscb = sb("rsc", [128, 64], F32, st)
        qd, qdb = sb("qd", [128, 160], F32, st)
        q2T, q2b = sb("q2T", [128, 2, NT], BF16, st)
        kT, kTb = sb("kT", [128, 2, NT], BF16, st)
        ktok, ktokb = sb("ktok", [128, 9, 256], BF16, st)
        ktf = [sb(f"ktf{i}", [128, 256], BF16, st) for i in range(2)]
        v, _ = sb("v", [128, 9, 512], BF16, st)
        vbs = [Buf(f"v{t}") for t in range(9)]
        gnsg, gnsgb = sb("gnsg", [128, 9, 512], BF16, st)
        S, _ = sb("S", [128, 2, 512], F32, st)
        Sb, _ = sb("Sb", [128, 2, 512], BF16, st)
        S2, _ = sb("S2", [128, 2, 512], F32, st)
        Sbuf = [Buf("S_0"), Buf("S_1")]
        Sbb = [Buf("Sb_0"), Buf("Sb_1")]
        S2buf = [Buf("S2_0"), Buf("S2_1")]
        gT, gTb = sb("gT", [128, 4, NT], BF16, st)
        ra = [sb(f"ra{i}", [128, 512], F32, st) for i in range(2)]
        scr = [dict(PT=sb(f"PT{i}", [128, 128], BF16, st), gated=sb(f"gated{i}", [128, 512], BF16, st),
                    st6=sb(f"st6{i}", [128, 8], F32, st), tn=ra[i]) for i in range(2)]
        q2m, q2mb = sb("q2m", [128, 2, 32], BF16, st)
        ktm, ktmb = sb("ktm", [32, 256], BF16, st)
        fw.dma(SP, rope[:], d_ropeR[:, :, :], W=[ropeb])
        fw.dma(SP, cmask[:], d_cmask[:, :], W=[cmaskb])
        fw.dma(SP, rsc[:], d_rsc[:, :], W=[rscb])
        cosT = rope[:, 0, :]
        sinT = rope[:, 1, :]

        def rope_proj(wt, wb_, out_t, out_b, scale_q):
            wv = wt[:, :].rearrange("p (k c) -> p k c", c=256)
            for (n0, nn) in NTILES:
                p1, p1b = PF()
                p2, p2b = PF()
                for mc, (pt, pb) in enumerate(((p1, p1b), (p2, p2b))):
                    for k in range(KC):
                        mm(pt[:, 0:nn], wv[:, k, mc * 128:(mc + 1) * 128], hT[:, k, n0:n0 + nn], k == 0, k == KC - 1,
                           [wb_, hb[k]], [pb])
                a_, ab_ = ra[0]
                b_, bb_ = ra[1]
                cs = cosT[:, n0:n0 + nn]
                sn = sinT[:, n0:n0 + nn]
                if nn == 512:
                    qdv = qd[:, 0:128].unsqueeze(1).to_broadcast([128, 4, 128])
                    av = a_[:, 0:512].rearrange("p (c t) -> p c t", t=128)
                else:
                    qdv = qd[:, 128:160]
                    av = a_[:, 0:32]
                for half in range(2):
                    pa, pab = (p1, p1b) if half == 0 else (p2, p2b)
                    pc, pcb = (p2, p2b) if half == 0 else (p1, p1b)
                    tt(DVE, a_[:, 0:nn], pa[:, 0:nn], cs, ALU.mult, [pab, ropeb], [ab_])
                    tt(DVE, b_[:, 0:nn], pc[:, 0:nn], sn, ALU.mult, [pcb, ropeb], [bb_])
                    op = ALU.subtract if half == 0 else ALU.add
                    if scale_q:
                        tt(DVE, a_[:, 0:nn], a_[:, 0:nn], b_[:, 0:nn], op, [ab_, bb_], [ab_])
                        ov = out_t[:, half, n0:n0 + nn]
                        if nn == 512:
                            ov = ov.rearrange("p (c t) -> p c t", t=128)
                        tt(DVE, ov, av, qdv, ALU.mult, [ab_, qdb], [out_b])
                    else:
                        tt(DVE, out_t[:, half, n0:n0 + nn], a_[:, 0:nn], b_[:, 0:nn], op, [ab_, bb_], [out_b])

        def tok_proj(evac, c0w):
            for half in range(2):
                wt, wb_ = WNEXT(("c", "ret_w_in", l, c0w + half * 256))
                wv = wt[:, :].rearrange("p (k c) -> p k c", c=256)
                for t, (c0, cn) in enumerate(TTILES):
                    pt, pb = PF()
                    for k in range(KC):
                        mm(pt[0:cn, 0:256], hT[:, k, c0:c0 + cn], wv[:, k, :], k == 0, k == KC - 1, [wb_, hb[k]], [pb])
                    evac(t, cn, half, pt, pb)

        for h in range(RH):
            gam = GAM[h]
            kdec_c = rsc[:, h:h + 1]
            ksc_c = rsc[:, 8 + h:9 + h]
            kdec_s = rsc[0:32, 16 + h:17 + h]
            ksc_s = rsc[0:32, 24 + h:25 + h]
            fw.dma(SP, qd[:], d_qdec[h, :, :], W=[qdb])
            sbufs = [(S2, S2buf), (S, Sbuf), (S2, S2buf), (S, Sbuf)]
            fw.dma(SP, S2[:], d_sret[l, 0, h, :, :].rearrange("(c p) e -> p c e", p=128), W=S2buf)
            fw.tag = "ret.K"
            ada_tick((h + 0.3) / 7.5)
            wt, wb_ = WNEXT(("c", "ret_w_in", l, 2048 + h * 256))
            rope_proj(wt, wb_, kT, kTb, False)
            fw.tag = "ret.ktr"
            for t, (c0, cn) in enumerate(TTILES):
                pt, pb = PB()
                for dc in range(2):
                    tr(pt[0:cn, dc * 128:(dc + 1) * 128], kT[:, dc, c0:c0 + cn], [kTb], [pb], sig=(dc == 1))
                act(ktok[0:cn, t, :], pt[0:cn, 0:256], AF.Identity, [pb, rscb], [ktokb], scale=(kdec_c if cn == 128 else kdec_s))
            fw.tag = "ret.V"
            pS = [PF(), PF()]
            for _, pbb in pS:
                reserved.add(bank_index(pbb))

            def evac_v(t, cn, half, pt, pb):
                cp(ACT, v[0:cn, t, half * 256:(half + 1) * 256], pt[0:cn, 0:256], [pb], [vbs[t]])
                if half == 1 and t >= 1:
                    tl = t - 1
                    ptag = fw.tag
                    fw.tag = "ret.Sloc"
                    kf, kfb = ktf[tl % 2]
                    ts(DVE, kf[:, :], ktok[:, tl, :], float(gam ** (128 * (7 - tl))), None, ALU.mult, None, [ktokb], [kfb])
                    for dc in range(2):
                        mm(pS[dc][0][:, :], kf[:, dc * 128:(dc + 1) * 128], v[:, tl, :], tl == 0, tl == 7, [kfb, vbs[tl]], [pS[dc][1]], sig=True)
                    fw.tag = ptag

            tok_proj(evac_v, 4096 + h * 512)
            ada_tick((h + 0.6) / 7.5)
            if h == 0:
                ada_step(2)
            wq_pre = WNEXT(("c", "ret_w_in", l, h * 256))
            fw.tag = "ret.Sloc"
            for dc in range(2):
                cp(ACT, S[:, dc, :], pS[dc][0][:, :], [pS[dc][1]], [Sbuf[dc]])
            for _, pbb in pS:
                reserved.discard(bank_index(pbb))
            bt, gt = x_sb[l][h]
            fw.dma(SP, bt.ap().rearrange("p (c e) -> p c e", c=2), S[:], R=Sbuf, W=[dbuf(bt)])
            fw.allgather(bt.ap().opt(), gt.ap().opt(), R=[dbuf(bt)], W=[dbuf(gt)])
            fw.tag = "ret.Q"
            wt, wb_ = wq_pre
            rope_proj(wt, wb_, q2T, q2b, True)
            fw.tag = "ret.G"
            tok_proj(lambda t, cn, half, pt, pb: act(gnsg[0:cn, t, half * 256:(half + 1) * 256], pt[0:cn, 0:256], AF.Silu, [pb], [gnsgb]),
                     8192 + h * 512)
            ada_tick((h + 1.0) / 7.5)
            if h == 0:
                ada_step(2)
            fw.tag = "ret.Sloc"
            fw.dma(SP, S[:], gt.ap()[0:128, :].rearrange("p (c e) -> p c e", c=2), R=[dbuf(gt)], W=Sbuf)
            for dc in range(2):
                ts(DVE, S[:, dc, :], S[:, dc, :], flag[:, 0:1], None, ALU.mult, None, [Sbuf[dc], flagb], [Sbuf[dc]])
                cp(ACT, Sb[:, dc, :], S[:, dc, :], [Sbuf[dc]], [Sbb[dc]])

            def gn_chain(t, cn, po, pob, sc_):
                (st6_, st6b_), (tn_, tnb_), (gated_, gatedb_) = sc_["st6"], sc_["tn"], sc_["gated"]
                fw.op(DVE, lambda: nc.vector.bn_stats(st6_[0:cn, 0:6], po[0:cn, :]), R=[pob], W=[st6b_])
                fw.op(DVE, lambda: nc.vector.bn_aggr(st6_[0:cn, 6:8], st6_[0:cn, 0:6]), R=[st6b_], W=[st6b_])
                ts(DVE, st6_[0:cn, 7:8], st6_[0:cn, 7:8], EPS, None, ALU.add, None, [st6b_], [st6b_])
                act(st6_[0:cn, 7:8], st6_[0:cn, 7:8], AF.Sqrt, [st6b_], [st6b_])
                fw.op(DVE, lambda: nc.vector.reciprocal(st6_[0:cn, 7:8], st6_[0:cn, 7:8]), R=[st6b_], W=[st6b_])
                ts(DVE, tn_[0:cn, :], po[0:cn, :], st6_[0:cn, 6:7], st6_[0:cn, 7:8], ALU.subtract, ALU.mult, [pob, st6b_], [tnb_])
                tt(DVE, gated_[0:cn, :], tn_[0:cn, :], gnsg[0:cn, t, :], ALU.mult, [tnb_, gnsgb], [gatedb_])

            def stage_b(t, c0, cn, sc_):
                gated_, gatedb_ = sc_["gated"]
                pt, pb = PB()
                for ec in range(4):
                    tr(pt[:, ec * 128:ec * 128 + cn], gated_[0:cn, ec * 128:(ec + 1) * 128], [gatedb_], [pb], sig=(ec == 3))
                for ec in range(4):
                    gcol = vecl[:, 384 + h * 4 + ec:385 + h * 4 + ec]
                    act(gT[:, ec, c0:c0 + cn], pt[:, ec * 128:ec * 128 + cn], AF.Identity, [pb, veclb], [gTb], scale=gcol)

            fw.tag = "ret.scan"
            pend = None
            for t in range(8):
                c0 = t * 128
                sc_ = scr[t % 2]
                PT_, PTb_ = sc_["PT"]
                psc, pscb = PF()
                pS = [PF(), PF()]
                po, pob = PF()
                for dc in range(2):
                    mm(psc[:, 0:128], kT[:, dc, c0:c0 + 128], q2T[:, dc, c0:c0 + 128], dc == 0, dc == 1, [kTb, q2b], [pscb])
                stt(DVE, PT_[:, :], psc[:, 0:128], ksc_c, cmask[:, 0:128], ALU.mult, ALU.mult, [pscb, rscb, cmaskb], [PTb_])
                for dc in range(2):
                    mm(pS[dc][0][:, :], ktok[:, t, dc * 128:(dc + 1) * 128], v[:, t, :], True, True, [ktokb, vbs[t]], [pS[dc][1]])
                mm(po[:, :], PT_[:, :], v[:, t, :], True, False, [PTb_, vbs[t]], [pob])
                for dc in range(2):
                    mm(po[:, :], q2T[:, dc, c0:c0 + 128], Sb[:, dc, :], False, dc == 1, [q2b, Sbb[dc]], [pob])
                for dc in range(2):
                    stt(DVE, S[:, dc, :], S[:, dc, :], float(gam ** 128), pS[dc][0][:, :], ALU.mult, ALU.add, [Sbuf[dc], pS[dc][1]], [Sbuf[dc]])
                    cp(ACT, Sb[:, dc, :], S[:, dc, :], [Sbuf[dc]], [Sbb[dc]])
                gn_chain(t, 128, po, pob, sc_)
                if pend is not None:
                    stage_b(*pend)
                pend = (t, c0, 128, sc_)
            fw.dma(SP, o_retp[l, h, :, :].rearrange("(c p) e -> p c e", p=128), S[:], R=Sbuf)

            fw.tag = "ret.samp"
            c0 = TP
            sc_ = scr[0]
            PT_, PTb_ = sc_["PT"]
            psc, pscb = PF()
            po, pob = PF()
            pS = [PF(), PF()]
            for dc in range(2):
                mm(psc[0:32, 0:32], kT[:, dc, c0:c0 + 32], q2T[:, dc, c0:c0 + 32], dc == 0, dc == 1, [kTb, q2b], [pscb])
            stt(DVE, PT_[0:32, 0:32], psc[0:32, 0:32], ksc_s, cmask[0:32, 128:160], ALU.mult, ALU.mult, [pscb, rscb, cmaskb], [PTb_])
            mm(po[0:32, :], PT_[0:32, 0:32], v[0:32, 8, :], True, False, [PTb_, vbs[8]], [pob])
            stage_b(*pend)
            for s in range(4):
                Sx, Sxb = sbufs[s]
                if s + 1 < 4:
                    Sn, Snb = sbufs[s + 1]
                    fw.dma(SP, Sn[:], d_sret[l, s + 1, h, :, :].rearrange("(c p) e -> p c e", p=128), W=Snb)
                for dc in range(2):
                    cp(ACT, Sb[:, dc, :], Sx[:, dc, :], [Sxb[dc]], [Sbb[dc]])
                fw.op(DVE, lambda: nc.vector.memset(q2m[:], 0.0), W=[q2mb])
                cp(DVE, q2m[:, :, 8 * s:8 * s + 8], q2T[:, :, c0 + 8 * s:c0 + 8 * s + 8], [q2b], [q2mb])
                for dc in range(2):
                    mm(po[0:32, :], q2m[:, dc, :], Sb[:, dc, :], False, (s == 3 and dc == 1), [q2mb, Sbb[dc]], [pob])
                ts(DVE, ktm[:, :], ktok[0:32, 8, :], rsc[0:32, 32 + s:33 + s], None, ALU.mult, None, [ktokb, rscb], [ktmb])
                for dc in range(2):
                    mm(pS[dc][0][:, :], ktm[:, dc * 128:(dc + 1) * 128], v[0:32, 8, :], True, True, [ktmb, vbs[8]], [pS[dc][1]])
                for dc in range(2):
                    stt(DVE, Sx[:, dc, :], Sx[:, dc, :], float(gam ** 8), pS[dc][0][:, :], ALU.mult, ALU.add, [Sxb[dc], pS[dc][1]], [Sxb[dc]])
                fw.dma(SP, o_rets[l, s, h, :, :].rearrange("(c p) e -> p c e", p=128), Sx[:], R=Sxb)
            gn_chain(8, 32, po, pob, sc_)
            stage_b(8, c0, 32, sc_)
            fw.tag = "ret.out"
            if h == 0:
                ada_finish(att["gatejob"])
            pieces = [WNEXT(("r", "ret_w_out", l, h * 512 + r * 256)) for r in range(2)]
            proj_rows(pieces, 4, gT, gTb)
        ada_flush()
        fw.barrier()
        st.close()

    att = {}

    def rope_apply(buf_t, buf_b, nh, t, cn, ropeA, ropeAb, rt):
        sv = buf_t[0:cn, 0:nh * 64].rearrange("p (h d) -> p h d", d=64)
        x1 = sv[:, :, 0:8]
        x2 = sv[:, :, 8:16]
        cs = ropeA[0:cn, t, 0, :].unsqueeze(1).to_broadcast([cn, nh, 8])
        sn = ropeA[0:cn, t, 1, :].unsqueeze(1).to_broadcast([cn, nh, 8])
        (r0, r0b), (r1, r1b) = rt
        a = r0[0:cn, 0, 0:nh, :]
        b = r0[0:cn, 1, 0:nh, :]
        c = r1[0:cn, 0, 0:nh, :]
        d = r1[0:cn, 1, 0:nh, :]
        tt(DVE, a, x1, cs, ALU.mult, [buf_b, ropeAb], [r0b])
        tt(DVE, b, x2, sn, ALU.mult, [buf_b, ropeAb], [r0b])
        tt(DVE, c, x2, cs, ALU.mult, [buf_b, ropeAb], [r1b])
        tt(DVE, d, x1, sn, ALU.mult, [buf_b, ropeAb], [r1b])
        tt(DVE, x1, a, b, ALU.subtract, [r0b], [buf_b])
        tt(DVE, x2, c, d, ALU.add, [r1b], [buf_b])

    def kv_stage(stk):
        kTa, kTab, vA, vAb = att["kTa"], att["kTab"], att["vA"], att["vAb"]
        saveM = dict(M)
        M["t"], M["b"] = att["kvmod"]
        M["g"] = None
        make_amod(vecg, vecgb, 0)
        norm_mod(False)
        M.update(saveM)
        st = contextlib.ExitStack()
        ropeA, ropeAb = sb("ropeA", [128, 9, 2, 8], F32, st)
        kf, kfb = sb("kf", [128, 512], F32, st)
        kr, krb = sb("kr", [128, 2, 256], BF16, st)
        rt = [sb(f"rt{i}", [128, 2, 8, 8], F32, st) for i in range(2)]
        fw.dma(SP, ropeA[:], d_ropeA[:, :, :, :], W=[ropeAb])
        fw.dma(POOL, kTa[:, :, 128 + TP:128 + TP + 512], d_ckT[:, :, :], W=[kTab])
        for s in range(4):
            fw.dma(POOL, vA[:, 10 + s, :], d_cv[:, s, :], W=[vAb])
        fw.tag = "kv"
        wk_ = WNEXT(("c", "w_kv", None, 0))
        wv_ = WNEXT(("c", "w_kv", None, 256))

        def finish_tile(tile_idx, keycol0, cn):
            cp(ACT, vA[0:cn, tile_idx, :], kf[0:cn, 256:512], [kfb], [vAb])
            for g in range(4):
                for dup in range(2):
                    o0 = (g % 2) * 128 + dup * 64
                    cp(DVE, kr[0:cn, g // 2, o0:o0 + 64], kf[0:cn, g * 64:(g + 1) * 64], [kfb], [krb])
            pt, pb = PB()
            for g in range(4):
                tr(pt[:, g * 128:g * 128 + cn], kr[0:cn, g // 2, (g % 2) * 128:(g % 2) * 128 + 128], [krb], [pb], sig=(g == 3))
            cp(ACT, kTa[:, :, keycol0:keycol0 + cn], pt[:, 0:512].rearrange("p (g c) -> p g c", c=128)[:, :, 0:cn], [pb], [kTab])

        bt, gt = x_kv
        for t, (c0, cn) in enumerate(TTILES):
            for half, (wt, wb_) in enumerate((wk_, wv_)):
                wv2 = wt[:, :].rearrange("p (k c) -> p k c", c=256)
                pt, pb = PF()
                for k in range(KC):
                    mm(pt[0:cn, 0:256], hT[:, k, c0:c0 + cn], wv2[:, k, :], k == 0, k == KC - 1, [wb_, hb[k]], [pb])
                cp(ACT, kf[0:cn, half * 256:(half + 1) * 256], pt[0:cn, 0:256], [pb], [kfb])
            rope_apply(kf, kfb, 4, t, cn, ropeA, ropeAb, rt)
            if t < 8:
                finish_tile(1 + t, 128 + c0, 128)
            else:
                finish_tile(9, 128 + TP + 512, 32)
            if t == 7:
                fw.dma(SP, o_wkp[:, :], kf[:, 0:256], R=[kfb])
                fw.dma(SP, o_wvp[:, :], kf[:, 256:512], R=[kfb])
                fw.dma(SP, bt.ap(), kf[:, :], R=[kfb], W=[dbuf(bt)])
                fw.allgather(bt.ap().opt(), gt.ap().opt(), R=[dbuf(bt)], W=[dbuf(gt)])
            if t == 8:
                for s in range(4):
                    fw.dma(SP, o_wks[s, 120:128, :], kf[8 * s:8 * s + 8, 0:256], R=[kfb])
                    fw.dma(SP, o_wvs[s, 120:128, :], kf[8 * s:8 * s + 8, 256:512], R=[kfb])
                    fw.dma(SP, o_wks[s, 0:120, :], d_cknat[s, 8:128, :])
                    fw.dma(SP, o_wvs[s, 0:120, :], d_cvnat[s, 8:128, :])
        fw.dma(SP, kf[:, :], gt.ap()[0:128, :], R=[dbuf(gt)], W=[kfb])
        finish_tile(0, 0, 128)
        fw.barrier()
        st.close()

    def attention(l, j):
        make_amod(vecl, veclb, 96)
        norm_mod(False)
        st = contextlib.ExitStack()
        kTa, kTab, vA, vAb = att["kTa"], att["kTab"], att["vA"], att["vAb"]
        ropeA, ropeAb = sb("ropeA", [128, 9, 2, 8], F32, st)
        rt = [sb(f"rt{i}", [128, 2, 8, 8], F32, st) for i in range(2)]
        qfs = [sb(f"qf{i}", [128, 512], F32, st) for i in range(2)]
        qbfs = [sb(f"qbf{i}", [128, 512], BF16, st) for i in range(2)]
        qT, qTb = sb("qT", [128, 4, NT], BF16, st)
        oT, oTb = sb("oT", [128, 4, NT], BF16, st)
        amask, amaskb = sb("amask", [128, 2, 256], F32, st)
        smask, smaskb = sb("smask", [32, 544], F32, st)
        amk, amkb = sb("amk", [128, 2, 256], BF16, st)
        smk, smkb = sb("smk", [32, 544], BF16, st)
        sinks, sinksb = sb("sinks", [128, 2, 32], F32, st)
        pes = [sb(f"pe{i}", [128, 4, 256], BF16, st) for i in range(2)]
        pTs = [sb(f"pT{i}", [128, 8, 128], BF16, st) for i in range(2)]
        sms = [sb(f"sm{i}", [128, 32], F32, st) for i in range(2)]
        pss = [sb(f"pss{i}", [32, 544], BF16, st) for i in range(2)]
        fw.dma(SP, ropeA[:], d_ropeA[:, :, :, :], W=[ropeAb])
        fw.dma(SP, amask[:], d_amask[:, :, :], W=[amaskb])
        fw.dma(SP, smask[:], d_smask[:, :], W=[smaskb])
        fw.dma(SP, sinks[:], d_sinks[:, :, :], W=[sinksb])
        extra_slot(st)
        cp(DVE, amk[:], amask[:], [amaskb], [amkb])
        cp(DVE, smk[:], smask[:], [smaskb], [smkb])
        ctr = {"i": 0}

        for g in range(4):
            fw.tag = "att.q"
            ada_tick((g + 0.5) / 3.6)
            wq = [WNEXT(("c", "att_w_q", j, g * 512 + half * 256)) for half in range(2)]
            def q_tr(t, c0, cn):
                qbf, qbfb = qbfs[t % 2]
                pt, pb = PB()
                for cc in range(4):
                    tr(pt[:, cc * 128:cc * 128 + cn], qbf[0:cn, cc * 128:(cc + 1) * 128], [qbfb], [pb], sig=(cc == 3))
                cp(ACT, qT[:, :, c0:c0 + cn], pt[:, 0:512].rearrange("p (c t) -> p c t", t=128)[:, :, 0:cn], [pb], [qTb])

            qpend = None
            for t, (c0, cn) in enumerate(TTILES):
                qf, qfb = qfs[t % 2]
                qbf, qbfb = qbfs[t % 2]
                for half in range(2):
                    wt, wb_ = wq[half]
                    wv2 = wt[:, :].rearrange("p (k c) -> p k c", c=256)
                    pt, pb = PF()
                    for k in range(KC):
                        mm(pt[0:cn, 0:256], hT[:, k, c0:c0 + cn], wv2[:, k, :], k == 0, k == KC - 1, [wb_, hb[k]], [pb])
                    cp(ACT, qf[0:cn, half * 256:(half + 1) * 256], pt[0:cn, 0:256], [pb], [qfb])
                rope_apply(qf, qfb, 8, t, cn, ropeA, ropeAb, rt)
                cp(ACT, qbf[0:cn, :], qf[0:cn, :], [qfb], [qbfb])
                if qpend is not None:
                    q_tr(*qpend)
                qpend = (t, c0, cn)
            q_tr(*qpend)
            def stage_a(t, bh):
                c0 = t * 128
                wm = 0 if t == 0 else 1
                ix = ctr["i"] % 2
                ctr["i"] += 1
                pe_, peb = pes[ix]
                sm, smb = sms[ix]
                banks = [PF(), PF()]
                for i, (ps, psb_) in enumerate(banks):
                    hp = 2 * bh + i
                    for hh in range(2):
                        base = hh * 64
                        mm(ps[:, hh * 256:(hh + 1) * 256], qT[base:base + 64, hp, c0:c0 + 128], kTa[base:base + 64, g, c0:c0 + 256],
                           True, False, [qTb, kTab], [psb_])
                        mm(ps[:, hh * 256:(hh + 1) * 256], identb[:, :], amk[:, wm, :], False, True, [identbb, amkb], [psb_])
                h0 = g * 8 + 4 * bh
                sk4 = sinks[:, j, h0:h0 + 4]
                for i, (ps, psb_) in enumerate(banks):
                    fw.op(DVE, lambda ps=ps, i=i: nc.vector.tensor_reduce(
                        sm[:, 2 * i:2 * i + 2], ps[:, 0:512].rearrange("p (h k) -> p h k", k=256), mybir.AxisListType.X, ALU.max),
                        R=[psb_], W=[smb])
                stt(DVE, sm[:, 4:8], sm[:, 0:4], 0.125, sk4, ALU.mult, ALU.max, [smb, sinksb], [smb])
                ts(DVE, sm[:, 8:12], sm[:, 4:8], -1.0, None, ALU.mult, None, [smb], [smb])
                tt(DVE, sm[:, 16:20], sk4, sm[:, 4:8], ALU.subtract, [sinksb, smb], [smb])
                for i, (ps, psb_) in enumerate(banks):
                    for hh in range(2):
                        hd = 2 * i + hh
                        act(pe_[:, hd, :], ps[:, hh * 256:(hh + 1) * 256], AF.Exp, [psb_, smb], [peb, smb],
                            bias=sm[:, 8 + hd:9 + hd], scale=0.125, accum=sm[:, 12 + hd:13 + hd])
                act(sm[:, 20:24], sm[:, 16:20], AF.Exp, [smb], [smb])
                tt(DVE, sm[:, 24:28], sm[:, 12:16], sm[:, 20:24], ALU.add, [smb], [smb])
                fw.op(DVE, lambda: nc.vector.reciprocal(sm[:, 28:32], sm[:, 24:28]), R=[smb], W=[smb])
                tt(DVE, pe_[:, :, :], pe_[:, :, :], sm[:, 28:32].unsqueeze(2).to_broadcast([128, 4, 256]), ALU.mult, [peb, smb], [peb])
                return (t, bh, ix)

            def stage_bb(t, bh, ix):
                c0 = t * 128
                pe_, peb = pes[ix]
                pT, pTb = pTs[ix]
                ptT, ptTb = PB()
                for hd in range(4):
                    for kb in range(2):
                        o0 = (hd * 2 + kb) * 128
                        tr(ptT[:, o0:o0 + 128], pe_[:, hd, kb * 128:(kb + 1) * 128], [peb], [ptTb], sig=(hd == 3 and kb == 1))
                cp(ACT, pT[:, :, :], ptT[:, 0:1024].rearrange("p (a q) -> p a q", q=128), [ptTb], [pTb])
                for i in range(2):
                    hp = 2 * bh + i
                    po, pob = PF()
                    for hh in range(2):
                        hd = 2 * i + hh
                        for kb in range(2):
                            mm(po[hh * 64:(hh + 1) * 64, 0:128], vA[:, t + kb, g * 64:(g + 1) * 64], pT[:, hd * 2 + kb, :], kb == 0, kb == 1,
                               [vAb, pTb], [pob], tp=(0, hh * 64))
                    cp(ACT, oT[:, hp, c0:c0 + 128], po[:, 0:128], [pob], [oTb])

            fw.tag = "att.blk"
            pend = None
            for t in range(8):
                for bh in range(2):
                    cur = stage_a(t, bh)
                    if pend is not None:
                        stage_bb(*pend)
                    pend = cur
            stage_bb(*pend)
            ada_tick((g + 1.0) / 3.6)
            fw.tag = "att.samp"
            c0 = TP
            K0 = 128 + TP

            def samp_a(hp, hh):
                ix = ctr["i"] % 2
                ctr["i"] += 1
                pn, pnb = pss[ix]
                sm, smb = sms[ix]
                hq = g * 8 + hp * 2 + hh
                base = hh * 64
                ps, psb_ = PF()
                ps2, ps2b = PF()
                mm(ps[0:32, 0:512], qT[base:base + 64, hp, c0:c0 + 32], kTa[base:base + 64, g, K0:K0 + 512], True, False, [qTb, kTab], [psb_])
                mm(ps[0:32, 0:512], identb[0:32, 0:32], smk[0:32, 0:512], False, True, [identbb, smkb], [psb_])
                mm(ps2[0:32, 0:32], qT[base:base + 64, hp, c0:c0 + 32], kTa[base:base + 64, g, K0 + 512:K0 + 544], True, False, [qTb, kTab], [ps2b])
                mm(ps2[0:32, 0:32], identb[0:32, 0:32], smk[0:32, 512:544], False, True, [identbb, smkb], [ps2b])
                fw.op(DVE, lambda: nc.vector.tensor_reduce(sm[0:32, 0:1], ps[0:32, 0:512], mybir.AxisListType.X, ALU.max), R=[psb_], W=[smb])
                fw.op(DVE, lambda: nc.vector.tensor_reduce(sm[0:32, 1:2], ps2[0:32, 0:32], mybir.AxisListType.X, ALU.max), R=[ps2b], W=[smb])
                tt(DVE, sm[0:32, 2:3], sm[0:32, 0:1], sm[0:32, 1:2], ALU.max, [smb], [smb])
                ts(DVE, sm[0:32, 4:5], sm[0:32, 2:3], 0.125, sinks[0:32, j, hq:hq + 1], ALU.mult, ALU.max, [smb, sinksb], [smb])
                ts(DVE, sm[0:32, 8:9], sm[0:32, 4:5], -1.0, None, ALU.mult, None, [smb], [smb])
                act(pn[0:32, 0:512], ps[0:32, 0:512], AF.Exp, [psb_, smb], [pnb, smb], bias=sm[0:32, 8:9], scale=0.125, accum=sm[0:32, 12:13])
                act(pn[0:32, 512:544], ps2[0:32, 0:32], AF.Exp, [ps2b, smb], [pnb, smb], bias=sm[0:32, 8:9], scale=0.125, accum=sm[0:32, 13:14])
                tt(DVE, sm[0:32, 16:17], sinks[0:32, j, hq:hq + 1], sm[0:32, 4:5], ALU.subtract, [sinksb, smb], [smb])
                act(sm[0:32, 20:21], sm[0:32, 16:17], AF.Exp, [smb], [smb])
                tt(DVE, sm[0:32, 24:25], sm[0:32, 12:13], sm[0:32, 13:14], ALU.add, [smb], [smb])
                tt(DVE, sm[0:32, 25:26], sm[0:32, 24:25], sm[0:32, 20:21], ALU.add, [smb], [smb])
                fw.op(DVE, lambda: nc.vector.reciprocal(sm[0:32, 28:29], sm[0:32, 25:26]), R=[smb], W=[smb])
                ts(DVE, pn[0:32, :], pn[0:32, :], sm[0:32, 28:29], None, ALU.mult, None, [pnb, smb], [pnb])
                return (hp, hh, ix)

            spo = {}

            def samp_b(hp, hh, ix):
                pn, pnb = pss[ix]
                pT, pTb = pTs[ix]
                if hh == 0:
                    spo[hp] = PF()
                po, pob = spo[hp]
                ptT, ptTb = PB()
                for kb in range(5):
                    kn = 128 if kb < 4 else 32
                    tr(ptT[0:kn, kb * 32:(kb + 1) * 32], pn[0:32, kb * 128:kb * 128 + kn], [pnb], [ptTb], sig=(kb == 4))
                cp(ACT, pT[:, 0:4, 0:32], ptT[:, 0:128].rearrange("p (k q) -> p k q", q=32), [ptTb], [pTb])
                cp(ACT, pT[0:32, 4, 0:32], ptT[0:32, 128:160], [ptTb], [pTb])
                for kb in range(5):
                    kn = 128 if kb < 4 else 32
                    vt = vA[0:kn, (10 + kb) if kb < 4 else 9, g * 64:(g + 1) * 64]
                    mm(po[hh * 64:(hh + 1) * 64, 0:32], vt, pT[0:kn, kb, 0:32], kb == 0, kb == 4, [vAb, pTb], [pob], tp=(0, hh * 64))
                if hh == 1:
                    cp(ACT, oT[:, hp, c0:c0 + 32], po[:, 0:32], [pob], [oTb])

            pend = None
            for hp in range(4):
                for hh in range(2):
                    cur = samp_a(hp, hh)
                    if pend is not None:
                        samp_b(*pend)
                    pend = cur
            samp_b(*pend)
            fw.tag = "att.o"
            if g == 0:
                ada_finish(att["gatejob"])
            pieces = [WNEXT(("r", "att_w_o", j, g * 512 + r * 256)) for r in range(2)]
            proj_rows(pieces, 4, oT, oTb)
        ada_flush()
        fw.barrier()
        wcur["l"] = list(wsl)
        st.close()

    fw.dma(SP, bada[:], d_bada[:, :, :], W=[badab])
    kvstack = contextlib.ExitStack()
    kvmodstack = contextlib.ExitStack()
    ada_queue(ada_job_sa(0, 0))
    ada_flush()
    k = 0
    for l in range(DEPTH):
        fw.dma(SP, vecl[:], d_vecl[l, :, :], W=[veclb])
        M.update(t=modall, b=modallb, g=32 + 16 * (k % 2))
        att["gatejob"] = ada_job_gate(l, 0, k % 2)
        ada_queue(att["gatejob"])
        ada_queue(ada_job_std(l, 1, (k + 1) % 2))
        if l < N_A:
            retention(l)
        else:
            if l == N_A:
                kv_stage(kvstack)
                kvmodstack.close()
            attention(l, l - N_A)
        k += 1
        M.update(t=modall, b=modallb, g=32 + 16 * (k % 2))
        if l + 1 < DEPTH:
            if l + 1 == N_A:
                kTa, kTab = sb("kTa", [128, 4, 128 + TP + 512 + 32], BF16, kvstack)
                vA, vAb = sb("vA", [128, 14, 256], BF16, kvstack)
                att.update(dict(kTa=kTa, kTab=kTab, vA=vA, vAb=vAb))
                kvm, _ = sb("kvmod", [128, 32, 33], F32, kvmodstack)
                kvmb = [Buf(f"kvmod{j}") for j in range(32)]
                att["kvmod"] = (kvm, kvmb)
                ada_queue(AdaJob("kv_w_ada", None, 0, 32, kvm, kvmb, lambda j: (vecg[:, 16 + j:17 + j], vecgb)))
            ada_queue(ada_job_sa(l + 1, 0))
        ffn(l)
        k += 1
    kvstack.close()
    stk = contextlib.ExitStack()
    rstd, rstdb = rms_stats(stk, NT, False)
    for kc in range(KC):
        stt(DVE, x[:, kc, :], x[:, kc, :], vecg[:, 48 + kc:49 + kc], rstd[:, 0:NT], ALU.mult, ALU.mult, xb[kc] + [vecgb, rstdb], xb[kc])
        fw.dma(SP, o_yT[:, kc, :], x[:, kc, :], R=xb[kc])
    fw.finish()
    stk.close()
    es.close()
    assert wstate["n"] == NP, (wstate["n"], NP)
    ok, stuck, val = fw.simulate()
    assert ok, ("semaphore deadlock in emitted program", stuck)
    build_program.stats = {q.name: (q.cnt, len(q.log)) for q in fw.queues}
    build_program.maxsem = max(val.values())
    build_program.specs = specs
    build_program.tags = {q.name: q.tags for q in fw.queues}
    return nc


def _piece_cols(Wm, cols):
    sub = Wm[:, cols]
    return np.ascontiguousarray(sub.reshape(KC, 128, 256).transpose(1, 0, 2)).reshape(128, WCOLS)


def _piece_rows(Wm, r0):
    sub = Wm[r0:r0 + 256, :]
    return np.ascontiguousarray(sub.reshape(2, 128, 2048).transpose(1, 0, 2)).reshape(128, WCOLS)


def build_weights(inp, specs):
    out = np.empty((len(specs), 128, WCOLS), np.float32)
    for i, (kind, name, idx, a0) in enumerate(specs):
        Wm = inp[name] if idx is None else inp[name][idx]
        if kind == "c":
            out[i] = _piece_cols(Wm, np.arange(a0, a0 + 256))
        elif kind == "c2":
            out[i] = _piece_cols(Wm, np.concatenate([np.arange(a0 * 128, (a0 + 1) * 128), np.arange(4096 + a0 * 128, 4096 + (a0 + 1) * 128)]))
        else:
            out[i] = _piece_rows(Wm, a0)
    return out


def fm(vec):
    return np.ascontiguousarray(np.asarray(vec, np.float32).reshape(-1, 128).T)


def build_inputs(inp, specs):
    f32 = np.float32
    wts = build_weights(inp, specs)
    vecl = np.zeros((DEPTH, 128, 416), f32)
    for l in range(DEPTH):
        vecl[l, :, 0:96] = fm(inp["b_ada"][l])
        vecl[l, :, 96:112] = fm(inp["norm_mix"][l])
        vecl[l, :, 112:128] = fm(inp["norm_ffn"][l])
        for tap in range(3):
            vecl[l, :, 128 + tap * 64:128 + (tap + 1) * 64] = fm(inp["ffn_conv_w"][l, tap])
        vecl[l, :, 320:384] = fm(inp["ffn_conv_b"][l])
        if l < N_A:
            vecl[l, :, 384:416] = fm(inp["ret_gn"][l])
    bada = np.ascontiguousarray(np.stack([fm(inp["b_ada"][l]) for l in range(DEPTH)], 1))
    vecg = np.zeros((128, 64), f32)
    vecg[:, 0:16] = fm(inp["kv_norm"])
    vecg[:, 16:48] = fm(inp["kv_b_ada"])
    vecg[:, 48:64] = fm(inp["norm_f"])
    gam = np.array(GAM, np.float64)
    m = np.arange(128, dtype=np.float64)
    rsc = np.zeros((128, 64), np.float64)
    for h in range(RH):
        rsc[:, h] = gam[h] ** (127 - m) / 16.0
        rsc[:, 8 + h] = gam[h] ** (-(m + 1)) / 16.0
        ms = (np.arange(32) % 8).astype(np.float64)
        rsc[:32, 16 + h] = gam[h] ** (7 - ms) / 16.0
        rsc[:32, 24 + h] = gam[h] ** (-(ms + 1)) / 16.0
    for s in range(4):
        rsc[8 * s:8 * s + 8, 32 + s] = 1.0
    rsc = rsc.astype(f32)
    cmask = np.zeros((128, 160), f32)
    cmask[:, 0:128] = (m[:, None] <= m[None, :]).astype(f32)
    i32 = np.arange(32)
    cmask[:32, 128:160] = ((i32[:, None] // 8 == i32[None, :] // 8) & (i32[:, None] <= i32[None, :])).astype(f32)
    inv_r = (10000.0 ** (-np.arange(128, dtype=f32) * 2.0 / 256.0)).astype(f32)
    inv_a = (500000.0 ** (-np.arange(8, dtype=f32) * 2.0 / 16.0)).astype(f32)
    qi = np.arange(128)[:, None]
    kj = np.arange(256)[None, :]
    band = np.where(((kj < 128) & (kj >= qi)) | ((kj >= 128) & (kj - 128 <= qi)), 0.0, NEG).astype(f32)
    first = np.where((kj >= 128) & (kj - 128 <= qi), 0.0, NEG).astype(f32)
    sm_ = np.full((32, 544), NEG, f32)
    for s in range(4):
        for i in range(8):
            r = 8 * s + i
            for jx in range(136):
                if i <= jx <= 128 + i:
                    if jx < 128:
                        sm_[r, s * 128 + jx] = 0.0
                    else:
                        sm_[r, 512 + 8 * s + (jx - 128)] = 0.0
    sinks = np.ascontiguousarray(np.broadcast_to(np.asarray(inp["att_sinks"], f32).reshape(1, 2, 32), (128, 2, 32)))

    maps = []
    for c in range(NCORES):
        s_, half = c // 2, c % 2
        t0 = half * TP
        ss = slice(4 * c, 4 * c + 4)
        xcat = np.concatenate([inp["x_prompt"][s_, t0:t0 + TP, :], inp["x_sample"][ss].reshape(TS, D)], 0)
        xT = np.ascontiguousarray(xcat.reshape(NT, KC, 128).transpose(2, 1, 0))
        ccat = np.concatenate([inp["c_prompt"][s_:s_ + 1], inp["c_sample"][ss]], 0)
        cT = np.ascontiguousarray(ccat.reshape(5, KC, 128).transpose(2, 1, 0))
        pos = np.concatenate([np.arange(t0, t0 + TP), np.tile(PAST + np.arange(8), 4)]).astype(f32)
        ang = pos[None, :] * inv_r[:, None]
        ropeR = np.stack([np.cos(ang), np.sin(ang)], 1).astype(f32)
        lpos = np.concatenate([np.arange(128), np.tile(np.arange(8), 4)]).astype(np.float64)
        qdec = np.ascontiguousarray(np.broadcast_to((gam[:, None] ** (lpos[None, :] + 1.0))[:, None, :], (RH, 128, 160))).astype(f32)
        angA = pos[:, None] * inv_a[None, :]
        ropeA = np.zeros((128, 9, 2, 8), f32)
        for t, (c0, cn) in enumerate(TTILES):
            ropeA[:cn, t, 0, :] = np.cos(angA[c0:c0 + cn]).astype(f32)
            ropeA[:cn, t, 1, :] = np.sin(angA[c0:c0 + cn]).astype(f32)
        amask = np.stack([first if half == 0 else band, band], 1).astype(f32)
        ck = np.asarray(inp["cache_win_k"][ss], f32)
        cvv = np.asarray(inp["cache_win_v"][ss], f32)
        ckT = np.zeros((128, 4, 512), f32)
        for g in range(4):
            kt = ck[:, :, g, :].transpose(2, 0, 1).reshape(64, 512)
            ckT[0:64, g, :] = kt
            ckT[64:128, g, :] = kt
        cv = np.ascontiguousarray(cvv.reshape(4, 128, 256).transpose(1, 0, 2))
        convs = np.ascontiguousarray(
            np.asarray(inp["state_conv"][:, ss], f32).reshape(DEPTH, 4, 2, 64, 128).transpose(0, 4, 3, 1, 2)).reshape(DEPTH, 128, 64, 8)
        maps.append(dict(
            xT=xT, cT=cT, flag=np.full((128, 1), float(half), f32), wts=wts, vecl=vecl, vecg=vecg, ropeR=ropeR, qdec=qdec,
            rsc=rsc, cmask=cmask, sret=np.ascontiguousarray(inp["state_ret"][:, ss]), convs=convs, ropeA=ropeA,
            amask=amask, smask=sm_, ckT=ckT, cv=cv, cknat=np.ascontiguousarray(ck.reshape(4, 128, 256)),
            cvnat=np.ascontiguousarray(cvv.reshape(4, 128, 256)), sinks=sinks, ident=np.eye(128, dtype=f32), bada=bada))
    return maps


def assemble(res):
    f32 = np.float32
    y_p = np.zeros((4, 2048, D), f32)
    y_s = np.zeros((32, 8, D), f32)
    ret_p = np.zeros((N_A, 4, RH, 256, 512), f32)
    ret_s = np.zeros((N_A, 32, RH, 256, 512), f32)
    wk_p = np.zeros((4, 128, 4, 64), f32)
    wv_p = np.zeros((4, 128, 4, 64), f32)
    wk_s = np.zeros((32, 128, 4, 64), f32)
    wv_s = np.zeros((32, 128, 4, 64), f32)
    cv_p = np.zeros((DEPTH, 4, 2, 8192), f32)
    cv_s = np.zeros((DEPTH, 32, 2, 8192), f32)
    for c in range(NCORES):
        r = res[c]
        s_, half = c // 2, c % 2
        t0 = half * TP
        yt = r["yT"].transpose(2, 1, 0).reshape(NT, D)
        y_p[s_, t0:t0 + TP] = yt[:TP]
        y_s[4 * c:4 * c + 4] = yt[TP:].reshape(4, 8, D)
        ret_s[:, 4 * c:4 * c + 4] = r["rets"]
        wk_s[4 * c:4 * c + 4] = r["wks"].reshape(4, 128, 4, 64)
        wv_s[4 * c:4 * c + 4] = r["wvs"].reshape(4, 128, 4, 64)
        co = r["convo"].reshape(DEPTH, 128, 64, 5, 2).transpose(0, 3, 4, 2, 1).reshape(DEPTH, 5, 2, 8192)
        cv_s[:, 4 * c:4 * c + 4] = co[:, 1:5]
        if half == 1:
            ret_p[:, s_] = r["retp"]
            wk_p[s_] = r["wkp"].reshape(128, 4, 64)
            wv_p[s_] = r["wvp"].reshape(128, 4, 64)
            cv_p[:, s_] = co[:, 0]
    return (y_p, y_s, ret_p, ret_s, wk_p, wv_p, wk_s, wv_s, cv_p, cv_s)


_CACHE = {}


def kernel(**inputs):
    inp = {k: np.asarray(v) for k, v in inputs.items()}
    if "nc" not in _CACHE:
        _CACHE["nc"] = build_program()
    nc = _CACHE["nc"]
    maps = build_inputs(inp, build_program.specs)
    res = run_bass_kernel_spmd(nc, maps, core_ids=list(range(NCORES)))
    return assemble(res.results)
```
